# Optimizing a Trainium2 kernel written in Bass

```python
import jax, jax.numpy as jnp
from jax import lax
import numpy as np

D_MODEL = 1024
BATCH = 32
SEQ = 2048
DEPTH = 2

HEAD_DIM = 64
N_Q_HEADS = 16
N_KV_HEADS = 4
GQA_GROUP = N_Q_HEADS // N_KV_HEADS
ATTN_WIDTH = N_Q_HEADS * HEAD_DIM
KV_WIDTH = N_KV_HEADS * HEAD_DIM
WINDOW = 128
BLOCK = WINDOW
CONV_WIDTH = D_MODEL
CONV_K = 3
D_FF = 4 * D_MODEL
RMS_EPS = 1e-6
NEG_INF = -1e30
IN_SPLIT_SIZES = (ATTN_WIDTH, KV_WIDTH, KV_WIDTH, CONV_WIDTH, CONV_WIDTH, CONV_WIDTH, D_MODEL, D_MODEL)
IN_COLS = sum(IN_SPLIT_SIZES)

kernel_name = "hybrid_swa_sink_alibi_shortconv_gated_block"


def rmsnorm(x, g):
    xf = x.astype(jnp.float32)
    y = xf * lax.rsqrt(jnp.mean(xf * xf, axis=-1, keepdims=True) + RMS_EPS)
    return (y * g.astype(jnp.float32)).astype(x.dtype)


def alibi_slopes():
    h = np.arange(1, N_Q_HEADS + 1, dtype=np.float32)
    return jnp.asarray(np.power(np.float32(2.0), -8.0 * h / N_Q_HEADS), dtype=jnp.float32)


def sliding_window_sink_attention(q, k, v, sinks):
    B, S = q.shape[0], q.shape[1]
    nblk = S // BLOCK
    qb = q.reshape(B, nblk, BLOCK, N_KV_HEADS, GQA_GROUP, HEAD_DIM)
    kb = k.reshape(B, nblk, BLOCK, N_KV_HEADS, HEAD_DIM)
    vb = v.reshape(B, nblk, BLOCK, N_KV_HEADS, HEAD_DIM)
    pad = ((0, 0), (1, 0), (0, 0), (0, 0), (0, 0))
    k_band = jnp.concatenate([jnp.pad(kb[:, :-1], pad), kb], axis=2)
    v_band = jnp.concatenate([jnp.pad(vb[:, :-1], pad), vb], axis=2)
    qb = jnp.moveaxis(qb, 1, 0)
    k_band = jnp.moveaxis(k_band, 1, 0)
    v_band = jnp.moveaxis(v_band, 1, 0)

    r = jnp.arange(BLOCK)[:, None]
    j = jnp.arange(2 * BLOCK)[None, :]
    dist = BLOCK + r - j
    in_window = (dist >= 0) & (dist < WINDOW)
    slopes = alibi_slopes().reshape(N_KV_HEADS, GQA_GROUP)
    bias = -slopes[:, :, None, None] * dist.astype(jnp.float32)[None, None]
    sink = sinks.astype(jnp.float32).reshape(1, N_KV_HEADS, GQA_GROUP, 1, 1)
    scale = HEAD_DIM ** -0.5

    def one_block(args):
        qi, ki, vi, i = args
        s = jnp.einsum('bqhgd,bkhd->bhgqk', qi, ki, preferred_element_type=jnp.float32) * scale + bias
        valid = in_window & ((i - 1) * BLOCK + j >= 0)
        s = jnp.where(valid, s, NEG_INF)
        m = jnp.maximum(jnp.max(s, axis=-1, keepdims=True), sink)
        p = jnp.exp(s - m)
        denom = jnp.sum(p, axis=-1, keepdims=True) + jnp.exp(sink - m)
        return jnp.einsum('bhgqk,bkhd->bqhgd', (p / denom).astype(vi.dtype), vi)

    out = lax.map(one_block, (qb, k_band, v_band, jnp.arange(nblk)))
    return jnp.moveaxis(out, 0, 1).reshape(B, S, ATTN_WIDTH)


def gated_short_conv(b_gate, c_gate, u, conv_w, conv_b):
    y = c_gate * u
    z = lax.conv_general_dilated(
        y, conv_w[:, None, :], window_strides=(1,), padding=[(CONV_K - 1, 0)],
        dimension_numbers=('NWC', 'WIO', 'NWC'), feature_group_count=CONV_WIDTH)
    return b_gate * (z + conv_b)


def setup_inputs(seed: int = 0) -> dict:
    key = jax.random.key(seed)
    ks = jax.random.split(key, 18)
    f32 = jnp.float32
    nrm = lambda k, shape, s: jax.random.normal(k, shape, f32) * s
    return {
        "x": nrm(ks[0], (BATCH, SEQ, D_MODEL), 1.0),
        "g_mix": 1.0 + nrm(ks[1], (DEPTH, D_MODEL), 0.02),
        "w_in": nrm(ks[2], (DEPTH, D_MODEL, IN_COLS), D_MODEL ** -0.5),
        "b_gates": nrm(ks[3], (DEPTH, 2 * D_MODEL), 0.02),
        "sinks": nrm(ks[4], (DEPTH, N_Q_HEADS), 1.0),
        "w_attn_out": nrm(ks[5], (DEPTH, ATTN_WIDTH, D_MODEL), ATTN_WIDTH ** -0.5),
        "conv_w": nrm(ks[6], (DEPTH, CONV_K, CONV_WIDTH), CONV_K ** -0.5),
        "conv_b": nrm(ks[7], (DEPTH, CONV_WIDTH), 0.02),
        "w_conv_out": nrm(ks[8], (DEPTH, CONV_WIDTH, D_MODEL), CONV_WIDTH ** -0.5),
        "w_o": nrm(ks[9], (DEPTH, D_MODEL, D_MODEL), D_MODEL ** -0.5),
        "g_mlp": 1.0 + nrm(ks[10], (DEPTH, D_MODEL), 0.02),
        "w_up": nrm(ks[11], (DEPTH, D_MODEL, D_FF), D_MODEL ** -0.5),
        "w_down": nrm(ks[12], (DEPTH, D_FF, D_MODEL), D_FF ** -0.5),
        "g_final": 1.0 + nrm(ks[13], (D_MODEL,), 0.02),
    }


def reference(x, g_mix, w_in, b_gates, sinks, w_attn_out, conv_w, conv_b, w_conv_out, w_o,
              g_mlp, w_up, w_down, g_final):
    B, S, _ = x.shape
    split_idx = list(np.cumsum(IN_SPLIT_SIZES)[:-1])
    for l in range(DEPTH):
        h = rmsnorm(x, g_mix[l])
        proj = jnp.einsum('bsd,dc->bsc', h, w_in[l])
        q, k, v, cb, cc, cu, ga, gc = jnp.split(proj, split_idx, axis=-1)
        q = q.reshape(B, S, N_Q_HEADS, HEAD_DIM)
        k = k.reshape(B, S, N_KV_HEADS, HEAD_DIM)
        v = v.reshape(B, S, N_KV_HEADS, HEAD_DIM)
        y_attn = jnp.einsum('bse,ed->bsd', sliding_window_sink_attention(q, k, v, sinks[l]), w_attn_out[l])
        y_conv = jnp.einsum('bse,ed->bsd', gated_short_conv(cb, cc, cu, conv_w[l], conv_b[l]), w_conv_out[l])
        gate_a = jax.nn.sigmoid(ga + b_gates[l, :D_MODEL])
        gate_c = jax.nn.sigmoid(gc + b_gates[l, D_MODEL:])
        merged = gate_a * y_attn + gate_c * y_conv
        x = x + jnp.einsum('bsd,de->bse', merged, w_o[l])
        h2 = rmsnorm(x, g_mlp[l])
        u = jnp.square(jax.nn.relu(jnp.einsum('bsd,df->bsf', h2, w_up[l])))
        x = x + jnp.einsum('bsf,fd->bsd', u, w_down[l])
    return rmsnorm(x, g_final)
```

```python
import numpy as np
import ml_dtypes
import concourse.bass as bass
import concourse.mybir as mybir
from concourse.bass_utils import run_bass_kernel_spmd

F32 = mybir.dt.float32
BF16 = mybir.dt.bfloat16
AF = mybir.ActivationFunctionType
ALU = mybir.AluOpType

D = 1024
KC = 8
T = 512
NBLK = 4
L_FULL = 2
SEQ = 2048
N_CORES = 8
NQH = 16
HD = 64
EPS = 1e-6
NWSLOT = 5
NPSB = 7
ADEPTH = 2
TRLAG = 2

OQ, OK_, OV, OCB, OCC, OCU, OGA, OGC = 0, 1024, 1280, 1536, 2560, 3584, 4608, 5632
HE = [0, 1, 2, 3, 8, 9, 10, 11]
HO = [4, 5, 6, 7, 12, 13, 14, 15]


def _win_perm():
    cols = []
    for c in range(8):
        cols += list(range(OQ + HE[c] * 64, OQ + HE[c] * 64 + 64))
        cols += list(range(OQ + HO[c] * 64, OQ + HO[c] * 64 + 64))
    cols += list(range(OK_, OK_ + 256))
    cols += list(range(OV, OV + 256))
    for i in range(8):
        cols += list(range(OCC + 128 * i, OCC + 128 * i + 128))
        cols += list(range(OCU + 128 * i, OCU + 128 * i + 128))
        cols += list(range(OCB + 128 * i, OCB + 128 * i + 128))
    cols += list(range(OGC, OGC + 1024))
    cols += list(range(OGA, OGA + 1024))
    return np.asarray(cols, dtype=np.int64)


MQ, MKV, MC, MGC, MGA = 0, 1024, 1536, 4608, 5632


def _tile_catalogue():
    t512 = [("win", 0, MQ, 512), ("win", 0, MQ + 512, 512), ("win", 0, MKV, 512)]
    for jg in range(2):
        t512 += [("win", 0, MGC + 512 * jg, 512), ("wco", 0, 512 * jg, 512)]
    for jg in range(2):
        t512 += [("win", 0, MGA + 512 * jg, 512), ("wao", 0, 512 * jg, 512)]
    t512 += [("wo", 0, 512 * og, 512) for og in range(2)]
    t512 += [("wup", 0, 512 * fg, 512) for fg in range(8)]
    t512 += [("wdn", 1024 * kq, 512 * og, 512) for og in range(2) for kq in range(4)]
    t384 = [("win", 0, MC + 384 * i, 384) for i in range(8)]
    return t512, t384


T512, T384 = _tile_catalogue()
TIDX = {k: (0, i) for i, k in enumerate(T512)}
TIDX.update({k: (1, i) for i, k in enumerate(T384)})


class Prog:
    ENG = ["pe", "act", "dve", "pool", "sp"]

    def __init__(self):
        self.q = {e: [] for e in self.ENG}
        self.cnt = {}
        self.seen = {e: {} for e in self.ENG}
        self.lastw = {}
        self.readers = {}

    def op(self, eng, fn, reads=(), writes=(), inc=True, dma=None):
        deps = {}

        def add(t):
            if t is None:
                return
            s, v = t
            if v > deps.get(s, 0):
                deps[s] = v

        for r in reads:
            add(self.lastw.get(r))
        for w in writes:
            add(self.lastw.get(w))
            rd = self.readers.get(w)
            if rd:
                for s, v in rd.items():
                    add((s, v))
        waits = []
        seen = self.seen[eng]
        for s, v in deps.items():
            if s == "pe" and eng == "pe":
                continue
            if v > seen.get(s, 0):
                waits.append((s, v))
                seen[s] = v
        if dma is not None:
            self.cnt[dma] = self.cnt.get(dma, 0) + 16
            tok = (dma, self.cnt[dma])
            incspec = (dma, 16)
        elif inc:
            self.cnt[eng] = self.cnt.get(eng, 0) + 1
            tok = (eng, self.cnt[eng])
            incspec = (eng, 1)
        else:
            tok = (eng, self.cnt.get(eng, 0) + 1)
            incspec = None
        self.q[eng].append((waits, fn, incspec))
        for r in reads:
            rd = self.readers.setdefault(r, {})
            if tok[1] > rd.get(tok[0], 0):
                rd[tok[0]] = tok[1]
        for w in writes:
            self.lastw[w] = tok
            self.readers[w] = {}
        return tok

    def final_wait(self, eng, toks):
        waits = []
        for s, v in toks:
            if v > self.seen[eng].get(s, 0):
                waits.append((s, v))
                self.seen[eng][s] = v
        self.q[eng].append((waits, None, None))

    def sem_keys(self):
        return list(self.cnt.keys())


def build_program(ntiles, depth, tiles_per_seq=4, dbg=None):
    nc = bass.Bass("TRN2", target_bir_lowering=False)
    L = depth
    ntok = ntiles * T
    dt = nc.dram_tensor
    x_d = dt("x", [ntok, D], F32, kind="ExternalInput").ap()
    out_d = dt("out", [ntok, D], F32, kind="ExternalOutput").ap()
    win_d = dt("w_in", [L, D, 6656], F32, kind="ExternalInput").ap()
    wao_d = dt("w_ao", [L, D, D], F32, kind="ExternalInput").ap()
    wco_d = dt("w_co", [L, D, D], F32, kind="ExternalInput").ap()
    wo_d = dt("w_o", [L, D, D], F32, kind="ExternalInput").ap()
    wup_d = dt("w_up", [L, D, 4 * D], F32, kind="ExternalInput").ap()
    wdn_d = dt("w_dn", [L, 4 * D, D], F32, kind="ExternalInput").ap()
    NPV = L * 64 + 8
    pvec_d = dt("pvec", [128, NPV], F32, kind="ExternalInput").ap()
    sink_d = dt("sinkb", [128, L * 16], F32, kind="ExternalInput").ap()
    etab_d = dt("etab", [128, 2 * 16 * 128], F32, kind="ExternalInput").ap()
    idf_d = dt("identf", [128, 128], F32, kind="ExternalInput").ap()
    idb_d = dt("identb", [128, 128], BF16, kind="ExternalInput").ap()
    scr512 = dt("wscr512", [L, len(T512), 128, KC * 512], BF16, kind="Internal").ap()
    scr384 = dt("wscr384", [L, len(T384), 128, KC * 384], BF16, kind="Internal").ap()

    sb = nc.alloc_sbuf_tensor
    big = sb("big", [128, 32, T], BF16)
    xT = sb("xT", [128, KC, T], F32)
    mbuf = sb("mbuf", [128, KC, T], F32)
    a2 = sb("a2", [128, KC, T], BF16)
    xin = sb("xin", [128, 4, D], F32)
    xout = sb("xout", [128, 2, D], F32)
    xsq = sb("xsq", [128, 3, T], BF16)
    rstd = sb("rstd", [128, T], F32)
    mse = sb("mse", [128, T], F32)
    epsc = sb("epsc", [128, 1], F32)
    kT = sb("kT", [128, L, 2, 5 * 128], BF16)
    Vt = sb("Vt", [128, L, 5, 4 * 65], BF16)
    ccs = sb("ccs", [128, 1, T], F32)
    yb = sb("yb", [128, 2, T + 2], F32)
    zb = sb("zb", [128, 2, T], F32)
    yh = sb("yh", [128, L, 8, 2], F32)
    sexp = sb("sexp", [128, 2, T], F32)
    PT = sb("PT", [128, 2 * (ADEPTH + 1), T], BF16)
    attn_n = sb("attn_n", [128, 2, D], BF16)
    den = sb("den", [128, 2, 4], F32)
    rden = sb("rden", [128, 2, 4], F32)
    gate = sb("gate", [128, 2, T], F32)
    tmpb = sb("tmpb", [128, 2, T], F32)
    relu = sb("relu", [128, 2, T], F32)
    wbuf = sb("wbuf", [128, NWSLOT, KC, 512], BF16)
    etab = sb("etab_s", [128, 2, 16, 128], F32)
    pvec = sb("pvec_s", [128, NPV], F32)
    hbias = sb("hbias", [128, L, 16], F32)
    sexpk = sb("sinkexp", [128, L * 16], F32)
    identf = sb("identf_s", [128, 128], F32)
    identb = sb("identb_s", [128, 128], BF16)
    ones = sb("ones", [128, 128], BF16)

    psf = nc.alloc_psum_tensor("psf", [128, NPSB, 512], F32)
    pst = nc.alloc_psum_tensor("pst", [128, 1024], BF16)

    P = Prog()
    P.sbuf_left = nc.sbuf_bytes_remaining

    def col(name, l, c):
        base = {"g_mix": 0, "g_mlp": 8, "b_ga": 16, "b_gc": 24, "cw0": 32, "cw1": 40, "cw2": 48, "cb": 56}[name]
        o = l * 64 + base + c
        return pvec[:, o:o + 1]

    def gfin(c):
        o = L * 64 + c
        return pvec[:, o:o + 1]

    P.op("sp", lambda e: e.dma_start(out=pvec[:], in_=pvec_d), writes=[("c", 0)], dma="cst")
    P.op("sp", lambda e: e.dma_start(out=sexpk[:], in_=sink_d), writes=[("c", 1)], dma="cst")
    P.op("sp", lambda e: e.dma_start(out=etab[:].rearrange("p a h q -> p (a h q)"), in_=etab_d), writes=[("c", 2)], dma="cst")
    P.op("sp", lambda e: e.dma_start(out=identf[:], in_=idf_d), writes=[("c", 3)], dma="cst")
    P.op("sp", lambda e: e.dma_start(out=identb[:], in_=idb_d), writes=[("c", 4)], dma="cst")
    CST = [("c", i) for i in range(5)]
    P.op("dve", lambda e: e.memset(ones[:], 1.0), reads=CST, writes=[("c", 5)])
    P.op("dve", lambda e: e.memset(epsc[:], EPS), writes=[("c", 6)])
    P.op("dve", lambda e: e.memset(Vt[:], 1.0), writes=[("V", l, s) for l in range(L) for s in range(5)])
    for l in range(L):
        P.op("dve", lambda e, l=l: e.tensor_scalar(hbias[:, l, :], pvec[:, l * 64 + 16:l * 64 + 32], 0.5, None, ALU.mult),
             reads=CST, writes=[("c", 7 + l)])
    P.op("act", lambda e: e.activation(sexpk[:], sexpk[:], AF.Exp), reads=CST, writes=[("c", 1)])
    ALLC = [("c", i) for i in range(7 + L)]
    P.op("pe", lambda e: e.nop(), reads=ALLC, inc=False)
    P.op("act", lambda e: e.nop(), reads=ALLC)
    P.op("dve", lambda e: e.nop(), reads=ALLC)
    P.op("pool", lambda e: e.nop(), reads=ALLC)

    def wsrc(kind, l, r0, c0, n):
        dten = {"win": win_d, "wao": wao_d, "wco": wco_d, "wo": wo_d, "wup": wup_d, "wdn": wdn_d}[kind]
        return dten[l, r0:r0 + 1024, c0:c0 + n].rearrange("(kc p) n -> p kc n", p=128)

    wtiles = []
    cur = {"t": 0}

    def wget(kind, l, r0, c0, ncol):
        i = len(wtiles)
        slot = i % NWSLOT
        res = ("w", slot)
        rd = dict(P.readers.get(res, {}))
        wtiles.append(((kind, l, r0, c0, ncol), slot, rd, cur["t"]))
        fill = i // NWSLOT + 1
        P.lastw[res] = (res, 32 * fill)
        P.readers[res] = {}
        P.cnt[res] = 32 * fill
        return slot

    def emit_weight_dmas():
        seen = P.seen["pool"]
        q = P.q["pool"]
        st_cnt = {}
        pend_st = []

        def need(s, v, waits):
            if v > seen.get(s, 0):
                waits.append((s, v))
                seen[s] = v

        def flush_store(k):
            while len(pend_st) > k:
                slot_, scr_, ncol_, fill_ = pend_st.pop(0)
                waits = []
                need(("w", slot_), 32 * fill_, waits)
                st_cnt[slot_] = st_cnt.get(slot_, 0) + 16
                P.cnt[("ws", slot_)] = st_cnt[slot_]
                q.append((waits, (lambda e, slot_=slot_, scr_=scr_, ncol_=ncol_: e.dma_start(
                    out=scr_, in_=wbuf[:, slot_, :, 0:ncol_])), (("ws", slot_), 16)))

        started_bf16 = False
        for i, ((kind, l, r0, c0, ncol), slot, rd, t_) in enumerate(wtiles):
            arr, idx = TIDX[(kind, r0, c0, ncol)]
            scr = (scr512 if arr == 0 else scr384)[l, idx].rearrange("p (kc n) -> p kc n", kc=KC)
            fill = i // NWSLOT + 1
            waits = []
            for s, v in rd.items():
                need(s, v, waits)
            if t_ == 0:
                if st_cnt.get(slot, 0):
                    need(("ws", slot), st_cnt[slot], waits)
                src_ap = wsrc(kind, l, r0, c0, ncol)
                for h in range(2):
                    fn = (lambda e, slot=slot, src_ap=src_ap, ncol=ncol, h=h: e.dma_start(
                        out=wbuf[:, slot, 4 * h:4 * h + 4, 0:ncol], in_=src_ap[:, 4 * h:4 * h + 4, :]))
                    q.append((waits if h == 0 else [], fn, (("w", slot), 16)))
                pend_st.append((slot, scr, ncol, fill))
                flush_store(2)
            else:
                if not started_bf16:
                    flush_store(0)
                    for s_, v_ in st_cnt.items():
                        need(("ws", s_), v_, waits)
                    started_bf16 = True
                fn = (lambda e, slot=slot, scr=scr, ncol=ncol: e.dma_start(out=wbuf[:, slot, :, 0:ncol], in_=scr))
                q.append((waits, fn, (("w", slot), 32)))
        flush_store(0)

    psstate = {"n": 0}

    def psget():
        b = psstate["n"] % NPSB
        psstate["n"] += 1
        return b

    rr = {}

    def ring(name, n):
        v = rr.get(name, 0)
        rr[name] = v + 1
        return v % n

    def mm_group(b, slot, c0, rhs_fn, rhs_res, ncol=128):
        for kc in range(KC):
            P.op("pe",
                 lambda e, b=b, slot=slot, c0=c0, kc=kc: e.matmul(
                     psf[0:ncol, b, :], wbuf[:, slot, kc, c0:c0 + ncol], rhs_fn(kc), start=(kc == 0), stop=(kc == KC - 1)),
                 reads=[("w", slot), rhs_res(kc)], writes=[("ps", b)], inc=(kc == KC - 1))

    def bigc(i):
        return big[:, i, :]

    H0, Q0, CV0, AT0 = 0, 8, 16, 24

    pending_out = []

    def emit_out_blocks():
        while pending_out:
            t_, blk = pending_out.pop(0)
            s = ring("xout", 2)
            for half in range(2):
                b = psget()
                for j in range(4):
                    cidx = half * 4 + j
                    P.op("pe", lambda e, b=b, j=j, cidx=cidx, blk=blk: e.transpose(
                        psf[:, b, j * 128:(j + 1) * 128], mbuf[:, cidx, blk * 128:(blk + 1) * 128], identf[:]),
                        reads=[("m", cidx)], writes=[("ps", b)], inc=(j == 3))
                if half == 0:
                    P.op("act", lambda e, b=b, s=s, half=half: e.activation(xout[:, s, half * 512:(half + 1) * 512], psf[:, b, :], AF.Copy),
                         reads=[("ps", b)], writes=[("xout", s)])
                else:
                    P.op("dve", lambda e, b=b, s=s, half=half: e.tensor_copy(xout[:, s, half * 512:(half + 1) * 512], psf[:, b, :]),
                         reads=[("ps", b)], writes=[("xout", s)])
            r0 = (t_ * NBLK + blk) * 128
            P.op("sp", lambda e, s=s, r0=r0: e.dma_start(out=out_d[r0:r0 + 128, :], in_=xout[:, s, :]),
                 reads=[("xout", s)], dma=("xo", s))

    def emit_square(kc):
        s = ring("xsq", 3)
        if kc % 2 == 0:
            P.op("act", lambda e, kc=kc, s=s: e.activation(xsq[:, s, :], xT[:, kc, :], AF.Square),
                 reads=[("xT", kc)], writes=[("xsq", s)])
        else:
            P.op("dve", lambda e, kc=kc, s=s: e.tensor_tensor(xsq[:, s, :], xT[:, kc, :], xT[:, kc, :], ALU.mult),
                 reads=[("xT", kc)], writes=[("xsq", s)])
        return s

    def emit_norm(gcol, dst_ap, dst_res):
        b = psget()
        for kc in range(KC):
            s = emit_square(kc)
            P.op("pe", lambda e, kc=kc, s=s, b=b: e.matmul(psf[:, b, :], ones[:], xsq[:, s, :], start=(kc == 0), stop=(kc == KC - 1)),
                 reads=[("xsq", s)], writes=[("ps", b)], inc=True)
        P.op("act", lambda e, b=b: e.activation(mse[:], psf[:, b, :], AF.Ln, bias=epsc[:, 0:1], scale=1.0 / D),
             reads=[("ps", b)], writes=[("mse",)])
        P.op("act", lambda e: e.activation(rstd[:], mse[:], AF.Exp, scale=-0.5),
             reads=[("mse",)], writes=[("rstd",)])
        for kc in range(KC):
            P.op("dve", lambda e, kc=kc: e.scalar_tensor_tensor(dst_ap(kc), xT[:, kc, :], gcol(kc), rstd[:], ALU.mult, ALU.mult),
                 reads=[("xT", kc), ("rstd",)], writes=[dst_res(kc)])
        emit_out_blocks()

    def emit_xload(t, blk):
        s = ring("xin", 4)
        r0 = (t * NBLK + blk) * 128
        P.op("sp", lambda e, s=s, r0=r0: e.dma_start(out=xin[:, s, :], in_=x_d[r0:r0 + 128, :]),
             writes=[("xin", s)], dma=("xin", s))
        return s

    xslots = {}

    def run_interleaved(primary, filler, nprimary, nfiller, holdback=2):
        spread = nfiller - holdback
        done_f = 0
        for i in range(nprimary):
            next(primary, None)
            want = ((i + 1) * spread) // nprimary
            while done_f < want:
                next(filler, None)
                done_f += 1
        for _ in primary:
            pass
        for _ in filler:
            pass

    for t in range(ntiles):
        cur["t"] = t
        first = (t % tiles_per_seq == 0)
        last_of_seq = (t % tiles_per_seq == tiles_per_seq - 1)
        for blk in range(NBLK):
            if (t, blk) not in xslots:
                xslots[(t, blk)] = emit_xload(t, blk)
            s = xslots[(t, blk)]
            for half in range(2):
                b = psget()
                for j in range(4):
                    cidx = half * 4 + j
                    P.op("pe", lambda e, b=b, j=j, s=s, cidx=cidx: e.transpose(
                        psf[:, b, j * 128:(j + 1) * 128], xin[:, s, cidx * 128:(cidx + 1) * 128], identf[:]),
                        reads=[("xin", s)], writes=[("ps", b)], inc=(j == 3))
                eng = "act" if half == 0 else "dve"
                if eng == "act":
                    fn = lambda e, b=b, half=half, blk=blk: e.activation(
                        xT[:, half * 4:half * 4 + 4, blk * 128:(blk + 1) * 128],
                        psf[:, b, :].rearrange("p (a q) -> p a q", a=4), AF.Copy)
                else:
                    fn = lambda e, b=b, half=half, blk=blk: e.tensor_copy(
                        xT[:, half * 4:half * 4 + 4, blk * 128:(blk + 1) * 128],
                        psf[:, b, :].rearrange("p (a q) -> p a q", a=4))
                P.op(eng, fn, reads=[("ps", b)], writes=[("xT", half * 4 + j) for j in range(4)])

        hrhs = lambda kc: bigc(H0 + kc)
        hres = lambda kc: ("big", H0 + kc)

        def gen_conv(l, first, last_of_seq):
            def emit_cb(prev):
                slot_, i_, s_ = prev
                bcb = psget()
                mm_group(bcb, slot_, 256, hrhs, hres)
                P.op("dve", lambda e, s_=s_, bcb=bcb, i_=i_: e.tensor_tensor(bigc(CV0 + i_), psf[:, bcb, :], zb[:, s_, :], ALU.mult),
                     reads=[("ps", bcb), ("z", s_)], writes=[("big", CV0 + i_)])

            prev = None
            for i in range(8):
                slot = wget("win", l, 0, MC + 384 * i, 384)
                if prev is not None:
                    emit_cb(prev)
                bcc, bcu = psget(), psget()
                mm_group(bcc, slot, 0, hrhs, hres)
                mm_group(bcu, slot, 128, hrhs, hres)
                s = ring("conv", 2)
                P.op("act", lambda e, s=s, bcc=bcc: e.activation(ccs[:, 0, :], psf[:, bcc, :], AF.Copy),
                     reads=[("ps", bcc)], writes=[("ccs", 0)])
                if first:
                    P.op("dve", lambda e, s=s: e.memset(yb[:, s, 0:2], 0.0), writes=[("ybh", s)])
                else:
                    P.op("dve", lambda e, s=s, l=l, i=i: e.tensor_copy(yb[:, s, 0:2], yh[:, l, i, :]),
                         reads=[("yh", l, i)], writes=[("ybh", s)])
                P.op("dve", lambda e, s=s, bcu=bcu: e.tensor_tensor(yb[:, s, 2:T + 2], psf[:, bcu, :], ccs[:, 0, :], ALU.mult),
                     reads=[("ps", bcu), ("ccs", 0)], writes=[("yb", s)])
                P.op("act", lambda e, s=s, l=l, i=i: e.activation(zb[:, s, :], yb[:, s, 2:T + 2], AF.Identity,
                                                                  bias=col("cb", l, i), scale=col("cw2", l, i)),
                     reads=[("yb", s)], writes=[("z", s)])
                P.op("dve", lambda e, s=s, l=l, i=i: e.scalar_tensor_tensor(zb[:, s, :], yb[:, s, 1:T + 1], col("cw1", l, i), zb[:, s, :], ALU.mult, ALU.add),
                     reads=[("yb", s), ("ybh", s), ("z", s)], writes=[("z", s)])
                P.op("dve", lambda e, s=s, l=l, i=i: e.scalar_tensor_tensor(zb[:, s, :], yb[:, s, 0:T], col("cw0", l, i), zb[:, s, :], ALU.mult, ALU.add),
                     reads=[("yb", s), ("ybh", s), ("z", s)], writes=[("z", s)])
                if not last_of_seq:
                    P.op("dve", lambda e, s=s, l=l, i=i: e.tensor_copy(yh[:, l, i, :], yb[:, s, T:T + 2]),
                         reads=[("yb", s)], writes=[("yh", l, i)])
                prev = (slot, i, s)
                yield
            emit_cb(prev)
            yield

        def gen_yconv(l):
            for jg in range(2):
                sg = wget("win", l, 0, MGC + 512 * jg, 512)
                sw = wget("wco", l, 0, 512 * jg, 512)
                for jj in range(4):
                    j = 4 * jg + jj
                    bg = psget()
                    mm_group(bg, sg, jj * 128, hrhs, hres)
                    gs = ring("gate", 2)
                    P.op("act", lambda e, bg=bg, gs=gs, l=l, j=j: e.activation(gate[:, gs, :], psf[:, bg, :], AF.Tanh,
                                                                               bias=hbias[:, l, 8 + j:9 + j], scale=0.5),
                         reads=[("ps", bg)], writes=[("gate", gs)])
                    by = psget()
                    mm_group(by, sw, jj * 128, lambda kc: bigc(CV0 + kc), lambda kc: ("big", CV0 + kc))
                    P.op("dve", lambda e, by=by, gs=gs, j=j: e.scalar_tensor_tensor(mbuf[:, j, :], gate[:, gs, :], 1.0, psf[:, by, :], ALU.add, ALU.mult),
                         reads=[("ps", by), ("gate", gs)], writes=[("m", j)])
                    yield

        def gen_attn(l, first, last_of_seq):
            units = [(blk, g) for blk in range(NBLK) for g in range(4)]

            def emit_scores(blk, g):
                p = g % 2
                cb0 = 4 * (g // 2)
                kc2 = g // 2
                kbs = [1] if (first and blk == 0) else [0, 1]
                pts = {}
                for kb in kbs:
                    slot_k = blk + kb
                    b = psget()
                    P.op("pe", lambda e, b=b, p=p, cb0=cb0, kc2=kc2, slot_k=slot_k, blk=blk, l=l: e.matmul(
                        psf[:, b, :].rearrange("p (a q) -> p a q", a=4),
                        kT[p * 64:(p + 1) * 64, l, kc2, slot_k * 128:(slot_k + 1) * 128],
                        big[p * 64:(p + 1) * 64, Q0 + cb0:Q0 + cb0 + 4, blk * 128:(blk + 1) * 128],
                        start=True, stop=True),
                        reads=[("kT", l, slot_k)] + [("big", Q0 + cb0 + a) for a in range(4)], writes=[("ps", b)], inc=True)
                    se = ring("sexp", 2)
                    P.op("act", lambda e, b=b, se=se: e.activation(sexp[:, se, :], psf[:, b, :], AF.Exp),
                         reads=[("ps", b)], writes=[("sexp", se)])
                    pt = ring("PT", 2 * (ADEPTH + 1))
                    pts[kb] = pt
                    P.op("dve", lambda e, se=se, pt=pt, kb=kb, g=g: e.tensor_tensor(
                        PT[:, pt, :].rearrange("p (a q) -> p a q", a=4),
                        sexp[:, se, :].rearrange("p (a q) -> p a q", a=4),
                        etab[:, kb, 4 * g:4 * g + 4, :], ALU.mult),
                        reads=[("sexp", se)], writes=[("PT", pt)])
                return kbs, pts

            def emit_pv(blk, g, kbs, pts, an):
                bo = psget()
                for a in range(4):
                    for ki, kb in enumerate(kbs):
                        slot_k = blk + kb
                        P.op("pe", lambda e, bo=bo, a=a, kb=kb, pt=pts[kb], slot_k=slot_k, g=g, ki=ki, nk=len(kbs), l=l: e.matmul(
                            psf[:, bo, a * 65:(a + 1) * 65], PT[:, pt, a * 128:(a + 1) * 128],
                            Vt[:, l, slot_k, g * 65:(g + 1) * 65], start=(ki == 0), stop=(ki == nk - 1)),
                            reads=[("PT", pts[kb]), ("V", l, slot_k)], writes=[("ps", bo)],
                            inc=(a == 3 and ki == len(kbs) - 1))
                ds = ring("den", 2)
                P.op("dve", lambda e, bo=bo, ds=ds, g=g, l=l: e.tensor_tensor(
                    den[:, ds, :], psf[:, bo, 0:260].rearrange("p (a d) -> p a d", a=4)[:, :, 64],
                    sexpk[:, l * 16 + 4 * g:l * 16 + 4 * g + 4], ALU.add),
                    reads=[("ps", bo)], writes=[("den", ds)])
                P.op("dve", lambda e, ds=ds: e.reciprocal(rden[:, ds, :], den[:, ds, :]),
                     reads=[("den", ds)], writes=[("rden", ds)])
                P.op("dve", lambda e, bo=bo, g=g, ds=ds, an=an: e.tensor_tensor(
                    attn_n[:, an, g * 256:(g + 1) * 256].rearrange("p (a d) -> p a d", a=4),
                    psf[:, bo, 0:260].rearrange("p (a d) -> p a d", a=4)[:, :, 0:64],
                    rden[:, ds, :].unsqueeze(2).broadcast_to([128, 4, 64]), ALU.mult),
                    reads=[("ps", bo), ("rden", ds)], writes=[("attn_n", an, g)])

            def emit_tr(blk, an):
                for j in range(8):
                    P.op("pe", lambda e, j=j, an=an: e.transpose(pst[:, j * 128:(j + 1) * 128], attn_n[:, an, j * 128:(j + 1) * 128], identb[:]),
                         reads=[("attn_n", an, g_) for g_ in range(4)], writes=[("pst",)], inc=(j == 7))
                P.op("dve", lambda e, blk=blk: e.tensor_copy(
                    big[:, AT0:AT0 + 8, blk * 128:(blk + 1) * 128], pst[:].rearrange("p (a q) -> p a q", a=8)),
                    reads=[("pst",)], writes=[("big", AT0 + j) for j in range(8)])

            ans = {}
            pend = []
            trq = []

            def do_pv(item):
                pb, pg, pk, pp = item
                emit_pv(pb, pg, pk, pp, ans[pb])
                if pg == 3:
                    trq.append([pb, TRLAG])

            def tick_tr(force=False):
                for it in list(trq):
                    it[1] -= 1
                    if it[1] <= 0 or force:
                        emit_tr(it[0], ans[it[0]])
                        trq.remove(it)

            for (blk, g) in units:
                if g == 0:
                    ans[blk] = ring("attn_n", 2)
                sc = emit_scores(blk, g)
                pend.append((blk, g, sc[0], sc[1]))
                if len(pend) > ADEPTH:
                    do_pv(pend.pop(0))
                tick_tr()
                yield
            while pend:
                do_pv(pend.pop(0))
                tick_tr()
                yield
            tick_tr(force=True)
            if not last_of_seq:
                P.op("dve", lambda e, l=l: e.tensor_copy(kT[:, l, :, 0:128], kT[:, l, :, 512:640]),
                     reads=[("kT", l, 4)], writes=[("kT", l, 0)])
                P.op("dve", lambda e, l=l: e.tensor_copy(Vt[:, l, 0, :], Vt[:, l, 4, :]),
                     reads=[("V", l, 4)], writes=[("V", l, 0)])
            yield

        def chain(*gens):
            for g_ in gens:
                for _ in g_:
                    yield

        def emit_layer(l, first, last_of_seq):
            emit_norm(lambda kc, l=l: col("g_mix", l, kc), lambda kc: bigc(H0 + kc), lambda kc: ("big", H0 + kc))
            for qi in range(2):
                slot = wget("win", l, 0, MQ + 512 * qi, 512)
                if qi == 0:
                    bs = [psget() for _ in range(4)]
                    for kc in range(KC):
                        for c in range(4):
                            P.op("pe", lambda e, b=bs[c], slot=slot, kc=kc, c=c: e.matmul(
                                psf[:, b, :], wbuf[:, slot, kc, c * 128:(c + 1) * 128], bigc(H0 + kc),
                                start=(kc == 0), stop=(kc == KC - 1)),
                                reads=[("w", slot), ("big", H0 + kc)], writes=[("ps", bs[c])], inc=(kc == KC - 1))
                    for c in range(4):
                        P.op("act", lambda e, b=bs[c], c=c: e.activation(bigc(Q0 + c), psf[:, b, :], AF.Copy, scale=0.125),
                             reads=[("ps", bs[c])], writes=[("big", Q0 + c)])
                    continue
                for c in range(4):
                    b = psget()
                    mm_group(b, slot, c * 128, hrhs, hres)
                    P.op("act", lambda e, b=b, qi=qi, c=c: e.activation(bigc(Q0 + 4 * qi + c), psf[:, b, :], AF.Copy, scale=0.125),
                         reads=[("ps", b)], writes=[("big", Q0 + 4 * qi + c)])
            slot = wget("win", l, 0, MKV, 512)
            for cc_ in range(2):
                b = psget()
                mm_group(b, slot, cc_ * 128, hrhs, hres)
                P.op("dve", lambda e, b=b, cc_=cc_, l=l: e.tensor_copy(kT[:, l, cc_, 128:640], psf[:, b, :]),
                     reads=[("ps", b)], writes=[("kT", l, 1 + s_) for s_ in range(4)])
            for blk in range(NBLK):
                b = psget()
                for kc in range(KC):
                    P.op("pe", lambda e, b=b, slot=slot, kc=kc, blk=blk: e.matmul(
                        psf[:, b, 0:256], bigc(H0 + kc)[:, blk * 128:(blk + 1) * 128], wbuf[:, slot, kc, 256:512],
                        start=(kc == 0), stop=(kc == KC - 1)),
                        reads=[("w", slot), ("big", H0 + kc)], writes=[("ps", b)], inc=(kc == KC - 1))
                P.op("dve", lambda e, b=b, blk=blk, l=l: e.tensor_copy(
                    Vt[:, l, 1 + blk, :].rearrange("p (g d) -> p g d", g=4)[:, :, 0:64],
                    psf[:, b, 0:256].rearrange("p (g d) -> p g d", g=4)),
                    reads=[("ps", b)], writes=[("V", l, 1 + blk)])
            run_interleaved(gen_attn(l, first, last_of_seq), chain(gen_conv(l, first, last_of_seq), gen_yconv(l)),
                            NBLK * 4 + ADEPTH + 1, 17, holdback=2)
            for jg in range(2):
                sg = wget("win", l, 0, MGA + 512 * jg, 512)
                sw = wget("wao", l, 0, 512 * jg, 512)
                for jj in range(4):
                    j = 4 * jg + jj
                    bg = psget()
                    mm_group(bg, sg, jj * 128, hrhs, hres)
                    gs = ring("gate", 2)
                    P.op("act", lambda e, bg=bg, gs=gs, l=l, j=j: e.activation(gate[:, gs, :], psf[:, bg, :], AF.Tanh,
                                                                               bias=hbias[:, l, j:j + 1], scale=0.5),
                         reads=[("ps", bg)], writes=[("gate", gs)])
                    by = psget()
                    mm_group(by, sw, jj * 128, lambda kc: bigc(AT0 + kc), lambda kc: ("big", AT0 + kc))
                    ts = ring("tmpb", 2)
                    P.op("dve", lambda e, by=by, gs=gs, ts=ts: e.scalar_tensor_tensor(tmpb[:, ts, :], gate[:, gs, :], 1.0, psf[:, by, :], ALU.add, ALU.mult),
                         reads=[("ps", by), ("gate", gs)], writes=[("tmpb", ts)])
                    P.op("dve", lambda e, ts=ts, j=j: e.tensor_tensor(bigc(CV0 + j), tmpb[:, ts, :], mbuf[:, j, :], ALU.add),
                         reads=[("tmpb", ts), ("m", j)], writes=[("big", CV0 + j)])
            for og in range(2):
                slot = wget("wo", l, 0, 512 * og, 512)
                for jj in range(4):
                    j = 4 * og + jj
                    b = psget()
                    mm_group(b, slot, jj * 128, lambda kc: bigc(CV0 + kc), lambda kc: ("big", CV0 + kc))
                    P.op("dve", lambda e, b=b, j=j: e.scalar_tensor_tensor(xT[:, j, :], psf[:, b, :], 0.5, xT[:, j, :], ALU.mult, ALU.add),
                         reads=[("ps", b), ("xT", j)], writes=[("xT", j)])
                    P.op("act", lambda e, j=j, l=l: e.activation(a2[:, j, :], xT[:, j, :], AF.Copy, scale=col("g_mlp", l, j)),
                         reads=[("xT", j)], writes=[("a2", j)])
            for fg in range(8):
                slot = wget("wup", l, 0, 512 * fg, 512)
                for ff in range(4):
                    f = 4 * fg + ff
                    b = psget()
                    mm_group(b, slot, ff * 128, lambda kc: a2[:, kc, :], lambda kc: ("a2", kc))
                    rs = ring("relu", 2)
                    P.op("act", lambda e, b=b, rs=rs: e.activation(relu[:, rs, :], psf[:, b, :], AF.Relu),
                         reads=[("ps", b)], writes=[("relu", rs)])
                    P.op("dve", lambda e, rs=rs, f=f: e.tensor_tensor(bigc(f), relu[:, rs, :], relu[:, rs, :], ALU.mult),
                         reads=[("relu", rs)], writes=[("big", f)])
                if fg == 0:
                    b = psget()
                    for kc in range(KC):
                        s = emit_square(kc)
                        P.op("pe", lambda e, kc=kc, s=s, b=b: e.matmul(psf[:, b, :], ones[:], xsq[:, s, :], start=(kc == 0), stop=(kc == KC - 1)),
                             reads=[("xsq", s)], writes=[("ps", b)], inc=True)
                    P.op("act", lambda e, b=b: e.activation(mse[:], psf[:, b, :], AF.Ln, bias=epsc[:, 0:1], scale=1.0 / D),
                         reads=[("ps", b)], writes=[("mse",)])
                    P.op("act", lambda e: e.activation(rstd[:], mse[:], AF.Exp, scale=-1.0),
                         reads=[("mse",)], writes=[("rstd",)])
            for og in range(2):
                bs = [psget() for _ in range(4)]
                for kq in range(4):
                    slot = wget("wdn", l, 1024 * kq, 512 * og, 512)
                    for jj in range(4):
                        for kc in range(KC):
                            f = kq * 8 + kc
                            P.op("pe", lambda e, b=bs[jj], slot=slot, jj=jj, kc=kc, f=f, kq=kq: e.matmul(
                                psf[:, b, :], wbuf[:, slot, kc, jj * 128:(jj + 1) * 128], bigc(f),
                                start=(kq == 0 and kc == 0), stop=(kq == 3 and kc == KC - 1)),
                                reads=[("w", slot), ("big", f)], writes=[("ps", bs[jj])], inc=(kc == KC - 1))
                for jj in range(4):
                    j = 4 * og + jj
                    ts = ring("tmpb", 2)
                    P.op("dve", lambda e, b=bs[jj], ts=ts: e.tensor_tensor(tmpb[:, ts, :], psf[:, b, :], rstd[:], ALU.mult),
                         reads=[("ps", bs[jj]), ("rstd",)], writes=[("tmpb", ts)])
                    P.op("dve", lambda e, ts=ts, j=j: e.tensor_tensor(xT[:, j, :], tmpb[:, ts, :], xT[:, j, :], ALU.add),
                         reads=[("tmpb", ts), ("xT", j)], writes=[("xT", j)])

        for l in range(L):
            emit_layer(l, first, last_of_seq)
        dsrc = None
        if t + 1 < ntiles:
            for blk in range(4):
                xslots[(t + 1, blk)] = emit_xload(t + 1, blk)
        b = psget()
        for kc in range(KC):
            s = emit_square(kc)
            P.op("pe", lambda e, kc=kc, s=s, b=b: e.matmul(psf[:, b, :], ones[:], xsq[:, s, :], start=(kc == 0), stop=(kc == KC - 1)),
                 reads=[("xsq", s)], writes=[("ps", b)], inc=True)
        P.op("act", lambda e, b=b: e.activation(mse[:], psf[:, b, :], AF.Ln, bias=epsc[:, 0:1], scale=1.0 / D),
             reads=[("ps", b)], writes=[("mse",)])
        P.op("act", lambda e: e.activation(rstd[:], mse[:], AF.Exp, scale=-0.5),
             reads=[("mse",)], writes=[("rstd",)])
        for kc in range(KC):
            P.op("dve", lambda e, kc=kc: e.scalar_tensor_tensor(mbuf[:, kc, :], xT[:, kc, :], gfin(kc), rstd[:], ALU.mult, ALU.mult),
                 reads=[("xT", kc), ("rstd",)], writes=[("m", kc)])
        for blk in range(NBLK):
            pending_out.append((t, blk))

    emit_out_blocks()
    emit_weight_dmas()
    P.final_wait("sp", [(k, v) for k, v in P.cnt.items() if isinstance(k, tuple) and k[0] == "xo"])

    sems = {}
    for k in P.sem_keys():
        nm = "s_" + ("_".join(str(z) for z in k) if isinstance(k, tuple) else str(k))
        sems[k] = nc.alloc_semaphore(nm)

    def runner(name):
        def body(e):
            for waits, fn, inc in P.q[name]:
                for s, v in waits:
                    e.wait_ge(sems[s], v)
                if fn is None:
                    continue
                ins = fn(e)
                if inc is not None:
                    ins.then_inc(sems[inc[0]], inc[1])
        return body

    with nc.Block() as block:
        block.tensor(runner("pe"))
        block.scalar(runner("act"))
        block.vector(runner("dve"))
        block.gpsimd(runner("pool"))
        block.sync(runner("sp"))
    return nc, P


def host_consts(depth):
    slopes = np.power(np.float32(2.0), -8.0 * np.arange(1, NQH + 1, dtype=np.float32) / NQH).astype(np.float32)
    j = np.arange(128)[:, None]
    r = np.arange(128)[None, :]
    et = np.zeros((128, 2, NQH, 128), dtype=np.float32)
    for h in range(NQH):
        dprev = (128 + r - j).astype(np.float32)
        dcur = (r - j).astype(np.float32)
        et[:, 0, h, :] = np.where(j > r, np.exp(-slopes[h] * np.maximum(dprev, 0.0)), 0.0)
        et[:, 1, h, :] = np.where(j <= r, np.exp(-slopes[h] * np.maximum(dcur, 0.0)), 0.0)
    identf = np.eye(128, dtype=np.float32)
    identb = np.eye(128, dtype=np.float32).astype(ml_dtypes.bfloat16)
    return et.reshape(128, -1), identf, identb


def prep_weights(inputs, depth):
    L = depth
    f = lambda a: np.ascontiguousarray(np.asarray(a, dtype=np.float32))
    perm = _win_perm()
    w_in = f(np.asarray(inputs["w_in"])[:L][:, :, perm])

    def cols(v):
        return np.asarray(v, dtype=np.float32).reshape(8, 128).T

    pv = np.zeros((128, L * 64 + 8), dtype=np.float32)
    for l in range(L):
        o = l * 64
        pv[:, o + 0:o + 8] = cols(inputs["g_mix"][l])
        pv[:, o + 8:o + 16] = cols(inputs["g_mlp"][l])
        pv[:, o + 16:o + 24] = cols(np.asarray(inputs["b_gates"])[l, :D])
        pv[:, o + 24:o + 32] = cols(np.asarray(inputs["b_gates"])[l, D:])
        pv[:, o + 32:o + 40] = cols(np.asarray(inputs["conv_w"])[l, 0])
        pv[:, o + 40:o + 48] = cols(np.asarray(inputs["conv_w"])[l, 1])
        pv[:, o + 48:o + 56] = cols(np.asarray(inputs["conv_w"])[l, 2])
        pv[:, o + 56:o + 64] = cols(inputs["conv_b"][l])
    pv[:, L * 64:L * 64 + 8] = cols(inputs["g_final"])
    sinkb = np.ascontiguousarray(np.broadcast_to(np.asarray(inputs["sinks"], dtype=np.float32)[:L].reshape(1, L * 16), (128, L * 16)))
    et, identf, identb = host_consts(L)
    shared = {
        "w_in": w_in,
        "w_ao": f(np.asarray(inputs["w_attn_out"])[:L]),
        "w_co": f(np.asarray(inputs["w_conv_out"])[:L]),
        "w_o": f(np.asarray(inputs["w_o"])[:L]),
        "w_up": f(np.asarray(inputs["w_up"])[:L]),
        "w_dn": f(np.asarray(inputs["w_down"])[:L]),
        "pvec": pv, "sinkb": sinkb, "etab": et, "identf": identf, "identb": identb,
    }
    return shared


_CACHE = {}


def run(inputs, depth=L_FULL, seq=SEQ, dbg=None):
    x = np.asarray(inputs["x"], dtype=np.float32)
    B, S, _ = x.shape
    assert S == seq and B % N_CORES == 0
    bpc = B // N_CORES
    tiles_per_seq = S // T
    ntiles = bpc * tiles_per_seq
    key = (ntiles, depth, tiles_per_seq, dbg)
    if key not in _CACHE:
        _CACHE[key] = build_program(ntiles, depth, tiles_per_seq, dbg)[0]
    nc = _CACHE[key]
    shared = prep_weights(inputs, depth)
    in_maps = []
    for c in range(N_CORES):
        m = dict(shared)
        m["x"] = np.ascontiguousarray(x[c * bpc:(c + 1) * bpc].reshape(bpc * S, D))
        in_maps.append(m)
    res = run_bass_kernel_spmd(nc, in_maps, core_ids=list(range(N_CORES)))
    outs = [np.asarray(r["out"], dtype=np.float32).reshape(bpc, S, D) for r in res.results]
    return np.concatenate(outs, axis=0)


def kernel(**inputs):
    return run(inputs)
```

```python
import numpy as np
import ml_dtypes
import concourse.bass as bass
import concourse.mybir as mybir
from concourse.bass_utils import run_bass_kernel_spmd

F32 = mybir.dt.float32
BF16 = mybir.dt.bfloat16
AF = mybir.ActivationFunctionType
ALU = mybir.AluOpType

D = 1024
KC = 8
T = 512
NBLK = 4
L_FULL = 2
SEQ = 2048
N_CORES = 8
NQH = 16
HD = 64
EPS = 1e-6
NWSLOT = 5
NPSB = 7
ADEPTH = 2
TRLAG = 2

OQ, OK_, OV, OCB, OCC, OCU, OGA, OGC = 0, 1024, 1280, 1536, 2560, 3584, 4608, 5632
HE = [0, 1, 2, 3, 8, 9, 10, 11]
HO = [4, 5, 6, 7, 12, 13, 14, 15]


def _win_perm():
    cols = []
    for c in range(8):
        cols += list(range(OQ + HE[c] * 64, OQ + HE[c] * 64 + 64))
        cols += list(range(OQ + HO[c] * 64, OQ + HO[c] * 64 + 64))
    cols += list(range(OK_, OK_ + 256))
    cols += list(range(OV, OV + 256))
    for i in range(8):
        cols += list(range(OCC + 128 * i, OCC + 128 * i + 128))
        cols += list(range(OCU + 128 * i, OCU + 128 * i + 128))
        cols += list(range(OCB + 128 * i, OCB + 128 * i + 128))
    cols += list(range(OGC, OGC + 1024))
    cols += list(range(OGA, OGA + 1024))
    return np.asarray(cols, dtype=np.int64)


MQ, MKV, MC, MGC, MGA = 0, 1024, 1536, 4608, 5632


def _tile_catalogue():
    t512 = [("win", 0, MQ, 512), ("win", 0, MQ + 512, 512), ("win", 0, MKV, 512)]
    for jg in range(2):
        t512 += [("win", 0, MGC + 512 * jg, 512), ("wco", 0, 512 * jg, 512)]
    for jg in range(2):
        t512 += [("win", 0, MGA + 512 * jg, 512), ("wao", 0, 512 * jg, 512)]
    t512 += [("wo", 0, 512 * og, 512) for og in range(2)]
    t512 += [("wup", 0, 512 * fg, 512) for fg in range(8)]
    t512 += [("wdn", 1024 * kq, 512 * og, 512) for og in range(2) for kq in range(4)]
    t384 = [("win", 0, MC + 384 * i, 384) for i in range(8)]
    return t512, t384


T512, T384 = _tile_catalogue()
TIDX = {k: (0, i) for i, k in enumerate(T512)}
TIDX.update({k: (1, i) for i, k in enumerate(T384)})


class Prog:
    ENG = ["pe", "act", "dve", "pool", "sp"]

    def __init__(self):
        self.q = {e: [] for e in self.ENG}
        self.cnt = {}
        self.seen = {e: {} for e in self.ENG}
        self.lastw = {}
        self.readers = {}

    def op(self, eng, fn, reads=(), writes=(), inc=True, dma=None):
        deps = {}

        def add(t):
            if t is None:
                return
            s, v = t
            if v > deps.get(s, 0):
                deps[s] = v

        for r in reads:
            add(self.lastw.get(r))
        for w in writes:
            add(self.lastw.get(w))
            rd = self.readers.get(w)
            if rd:
                for s, v in rd.items():
                    add((s, v))
        waits = []
        seen = self.seen[eng]
        for s, v in deps.items():
            if s == "pe" and eng == "pe":
                continue
            if v > seen.get(s, 0):
                waits.append((s, v))
                seen[s] = v
        if dma is not None:
            self.cnt[dma] = self.cnt.get(dma, 0) + 16
            tok = (dma, self.cnt[dma])
            incspec = (dma, 16)
        elif inc:
            self.cnt[eng] = self.cnt.get(eng, 0) + 1
            tok = (eng, self.cnt[eng])
            incspec = (eng, 1)
        else:
            tok = (eng, self.cnt.get(eng, 0) + 1)
            incspec = None
        self.q[eng].append((waits, fn, incspec))
        for r in reads:
            rd = self.readers.setdefault(r, {})
            if tok[1] > rd.get(tok[0], 0):
                rd[tok[0]] = tok[1]
        for w in writes:
            self.lastw[w] = tok
            self.readers[w] = {}
        return tok

    def final_wait(self, eng, toks):
        waits = []
        for s, v in toks:
            if v > self.seen[eng].get(s, 0):
                waits.append((s, v))
                self.seen[eng][s] = v
        self.q[eng].append((waits, None, None))

    def sem_keys(self):
        return list(self.cnt.keys())


def build_program(ntiles, depth, tiles_per_seq=4, dbg=None):
    nc = bass.Bass("TRN2", target_bir_lowering=False)
    L = depth
    ntok = ntiles * T
    dt = nc.dram_tensor
    x_d = dt("x", [ntok, D], F32, kind="ExternalInput").ap()
    out_d = dt("out", [ntok, D], F32, kind="ExternalOutput").ap()
    win_d = dt("w_in", [L, D, 6656], F32, kind="ExternalInput").ap()
    wao_d = dt("w_ao", [L, D, D], F32, kind="ExternalInput").ap()
    wco_d = dt("w_co", [L, D, D], F32, kind="ExternalInput").ap()
    wo_d = dt("w_o", [L, D, D], F32, kind="ExternalInput").ap()
    wup_d = dt("w_up", [L, D, 4 * D], F32, kind="ExternalInput").ap()
    wdn_d = dt("w_dn", [L, 4 * D, D], F32, kind="ExternalInput").ap()
    NPV = L * 64 + 8
    pvec_d = dt("pvec", [128, NPV], F32, kind="ExternalInput").ap()
    sink_d = dt("sinkb", [128, L * 16], F32, kind="ExternalInput").ap()
    etab_d = dt("etab", [128, 2 * 16 * 128], F32, kind="ExternalInput").ap()
    idf_d = dt("identf", [128, 128], F32, kind="ExternalInput").ap()
    idb_d = dt("identb", [128, 128], BF16, kind="ExternalInput").ap()
    scr512 = dt("wscr512", [L, len(T512), 128, KC * 512], BF16, kind="Internal").ap()
    scr384 = dt("wscr384", [L, len(T384), 128, KC * 384], BF16, kind="Internal").ap()

    sb = nc.alloc_sbuf_tensor
    big = sb("big", [128, 32, T], BF16)
    xT = sb("xT", [128, KC, T], F32)
    mbuf = sb("mbuf", [128, KC, T], F32)
    a2 = sb("a2", [128, KC, T], BF16)
    xin = sb("xin", [128, 4, D], F32)
    xout = sb("xout", [128, 2, D], F32)
    xsq = sb("xsq", [128, 3, T], BF16)
    rstd = sb("rstd", [128, T], F32)
    mse = sb("mse", [128, T], F32)
    epsc = sb("epsc", [128, 1], F32)
    kT = sb("kT", [128, L, 2, 5 * 128], BF16)
    Vt = sb("Vt", [128, L, 5, 4 * 65], BF16)
    ccs = sb("ccs", [128, 1, T], F32)
    yb = sb("yb", [128, 2, T + 2], F32)
    zb = sb("zb", [128, 2, T], F32)
    yh = sb("yh", [128, L, 8, 2], F32)
    sexp = sb("sexp", [128, 2, T], F32)
    PT = sb("PT", [128, 2 * (ADEPTH + 1), T], BF16)
    attn_n = sb("attn_n", [128, 2, D], BF16)
    den = sb("den", [128, 2, 4], F32)
    rden = sb("rden", [128, 2, 4], F32)
    gate = sb("gate", [128, 2, T], F32)
    tmpb = sb("tmpb", [128, 2, T], F32)
    relu = sb("relu", [128, 2, T], F32)
    wbuf = sb("wbuf", [128, NWSLOT, KC, 512], BF16)
    etab = sb("etab_s", [128, 2, 16, 128], F32)
    pvec = sb("pvec_s", [128, NPV], F32)
    hbias = sb("hbias", [128, L, 16], F32)
    sexpk = sb("sinkexp", [128, L * 16], F32)
    identf = sb("identf_s", [128, 128], F32)
    identb = sb("identb_s", [128, 128], BF16)
    ones = sb("ones", [128, 128], BF16)

    psf = nc.alloc_psum_tensor("psf", [128, NPSB, 512], F32)
    pst = nc.alloc_psum_tensor("pst", [128, 1024], BF16)

    P = Prog()
    P.sbuf_left = nc.sbuf_bytes_remaining

    def col(name, l, c):
        base = {"g_mix": 0, "g_mlp": 8, "b_ga": 16, "b_gc": 24, "cw0": 32, "cw1": 40, "cw2": 48, "cb": 56}[name]
        o = l * 64 + base + c
        return pvec[:, o:o + 1]

    def gfin(c):
        o = L * 64 + c
        return pvec[:, o:o + 1]

    P.op("sp", lambda e: e.dma_start(out=pvec[:], in_=pvec_d), writes=[("c", 0)], dma="cst")
    P.op("sp", lambda e: e.dma_start(out=sexpk[:], in_=sink_d), writes=[("c", 1)], dma="cst")
    P.op("sp", lambda e: e.dma_start(out=etab[:].rearrange("p a h q -> p (a h q)"), in_=etab_d), writes=[("c", 2)], dma="cst")
    P.op("sp", lambda e: e.dma_start(out=identf[:], in_=idf_d), writes=[("c", 3)], dma="cst")
    P.op("sp", lambda e: e.dma_start(out=identb[:], in_=idb_d), writes=[("c", 4)], dma="cst")
    CST = [("c", i) for i in range(5)]
    P.op("dve", lambda e: e.memset(ones[:], 1.0), reads=CST, writes=[("c", 5)])
    P.op("dve", lambda e: e.memset(epsc[:], EPS), writes=[("c", 6)])
    P.op("dve", lambda e: e.memset(Vt[:], 1.0), writes=[("V", l, s) for l in range(L) for s in range(5)])
    for l in range(L):
        P.op("dve", lambda e, l=l: e.tensor_scalar(hbias[:, l, :], pvec[:, l * 64 + 16:l * 64 + 32], 0.5, None, ALU.mult),
             reads=CST, writes=[("c", 7 + l)])
    P.op("act", lambda e: e.activation(sexpk[:], sexpk[:], AF.Exp), reads=CST, writes=[("c", 1)])
    ALLC = [("c", i) for i in range(7 + L)]
    P.op("pe", lambda e: e.nop(), reads=ALLC, inc=False)
    P.op("act", lambda e: e.nop(), reads=ALLC)
    P.op("dve", lambda e: e.nop(), reads=ALLC)
    P.op("pool", lambda e: e.nop(), reads=ALLC)

    def wsrc(kind, l, r0, c0, n):
        dten = {"win": win_d, "wao": wao_d, "wco": wco_d, "wo": wo_d, "wup": wup_d, "wdn": wdn_d}[kind]
        return dten[l, r0:r0 + 1024, c0:c0 + n].rearrange("(kc p) n -> p kc n", p=128)

    wtiles = []
    cur = {"t": 0}

    def wget(kind, l, r0, c0, ncol):
        i = len(wtiles)
        slot = i % NWSLOT
        res = ("w", slot)
        rd = dict(P.readers.get(res, {}))
        wtiles.append(((kind, l, r0, c0, ncol), slot, rd, cur["t"]))
        fill = i // NWSLOT + 1
        P.lastw[res] = (res, 32 * fill)
        P.readers[res] = {}
        P.cnt[res] = 32 * fill
        return slot

    def emit_weight_dmas():
        seen = P.seen["pool"]
        q = P.q["pool"]
        st_cnt = {}
        pend_st = []

        def need(s, v, waits):
            if v > seen.get(s, 0):
                waits.append((s, v))
                seen[s] = v

        def flush_store(k):
            while len(pend_st) > k:
                slot_, scr_, ncol_, fill_ = pend_st.pop(0)
                waits = []
                need(("w", slot_), 32 * fill_, waits)
                st_cnt[slot_] = st_cnt.get(slot_, 0) + 16
                P.cnt[("ws", slot_)] = st_cnt[slot_]
                q.append((waits, (lambda e, slot_=slot_, scr_=scr_, ncol_=ncol_: e.dma_start(
                    out=scr_, in_=wbuf[:, slot_, :, 0:ncol_])), (("ws", slot_), 16)))

        started_bf16 = False
        for i, ((kind, l, r0, c0, ncol), slot, rd, t_) in enumerate(wtiles):
            arr, idx = TIDX[(kind, r0, c0, ncol)]
            scr = (scr512 if arr == 0 else scr384)[l, idx].rearrange("p (kc n) -> p kc n", kc=KC)
            fill = i // NWSLOT + 1
            waits = []
            for s, v in rd.items():
                need(s, v, waits)
            if t_ == 0:
                if st_cnt.get(slot, 0):
                    need(("ws", slot), st_cnt[slot], waits)
                src_ap = wsrc(kind, l, r0, c0, ncol)
                for h in range(2):
                    fn = (lambda e, slot=slot, src_ap=src_ap, ncol=ncol, h=h: e.dma_start(
                        out=wbuf[:, slot, 4 * h:4 * h + 4, 0:ncol], in_=src_ap[:, 4 * h:4 * h + 4, :]))
                    q.append((waits if h == 0 else [], fn, (("w", slot), 16)))
                pend_st.append((slot, scr, ncol, fill))
                flush_store(2)
            else:
                if not started_bf16:
                    flush_store(0)
                    for s_, v_ in st_cnt.items():
                        need(("ws", s_), v_, waits)
                    started_bf16 = True
                fn = (lambda e, slot=slot, scr=scr, ncol=ncol: e.dma_start(out=wbuf[:, slot, :, 0:ncol], in_=scr))
                q.append((waits, fn, (("w", slot), 32)))
        flush_store(0)

    psstate = {"n": 0}

    def psget():
        b = psstate["n"] % NPSB
        psstate["n"] += 1
        return b

    rr = {}

    def ring(name, n):
        v = rr.get(name, 0)
        rr[name] = v + 1
        return v % n

    def mm_group(b, slot, c0, rhs_fn, rhs_res, ncol=128):
        for kc in range(KC):
            P.op("pe",
                 lambda e, b=b, slot=slot, c0=c0, kc=kc: e.matmul(
                     psf[0:ncol, b, :], wbuf[:, slot, kc, c0:c0 + ncol], rhs_fn(kc), start=(kc == 0), stop=(kc == KC - 1)),
                 reads=[("w", slot), rhs_res(kc)], writes=[("ps", b)], inc=(kc == KC - 1))

    def bigc(i):
        return big[:, i, :]

    H0, Q0, CV0, AT0 = 0, 8, 16, 24

    pending_out = []

    def emit_out_blocks():
        while pending_out:
            t_, blk = pending_out.pop(0)
            s = ring("xout", 2)
            for half in range(2):
                b = psget()
                for j in range(4):
                    cidx = half * 4 + j
                    P.op("pe", lambda e, b=b, j=j, cidx=cidx, blk=blk: e.transpose(
                        psf[:, b, j * 128:(j + 1) * 128], mbuf[:, cidx, blk * 128:(blk + 1) * 128], identf[:]),
                        reads=[("m", cidx)], writes=[("ps", b)], inc=(j == 3))
                if half == 0:
                    P.op("act", lambda e, b=b, s=s, half=half: e.activation(xout[:, s, half * 512:(half + 1) * 512], psf[:, b, :], AF.Copy),
                         reads=[("ps", b)], writes=[("xout", s)])
                else:
                    P.op("dve", lambda e, b=b, s=s, half=half: e.tensor_copy(xout[:, s, half * 512:(half + 1) * 512], psf[:, b, :]),
                         reads=[("ps", b)], writes=[("xout", s)])
            r0 = (t_ * NBLK + blk) * 128
            P.op("sp", lambda e, s=s, r0=r0: e.dma_start(out=out_d[r0:r0 + 128, :], in_=xout[:, s, :]),
                 reads=[("xout", s)], dma=("xo", s))

    def emit_square(kc):
        s = ring("xsq", 3)
        if kc % 2 == 0:
            P.op("act", lambda e, kc=kc, s=s: e.activation(xsq[:, s, :], xT[:, kc, :], AF.Square),
                 reads=[("xT", kc)], writes=[("xsq", s)])
        else:
            P.op("dve", lambda e, kc=kc, s=s: e.tensor_tensor(xsq[:, s, :], xT[:, kc, :], xT[:, kc, :], ALU.mult),
                 reads=[("xT", kc)], writes=[("xsq", s)])
        return s

    def emit_norm(gcol, dst_ap, dst_res):
        b = psget()
        for kc in range(KC):
            s = emit_square(kc)
            P.op("pe", lambda e, kc=kc, s=s, b=b: e.matmul(psf[:, b, :], ones[:], xsq[:, s, :], start=(kc == 0), stop=(kc == KC - 1)),
                 reads=[("xsq", s)], writes=[("ps", b)], inc=True)
        P.op("act", lambda e, b=b: e.activation(mse[:], psf[:, b, :], AF.Ln, bias=epsc[:, 0:1], scale=1.0 / D),
             reads=[("ps", b)], writes=[("mse",)])
        P.op("act", lambda e: e.activation(rstd[:], mse[:], AF.Exp, scale=-0.5),
             reads=[("mse",)], writes=[("rstd",)])
        for kc in range(KC):
            P.op("dve", lambda e, kc=kc: e.scalar_tensor_tensor(dst_ap(kc), xT[:, kc, :], gcol(kc), rstd[:], ALU.mult, ALU.mult),
                 reads=[("xT", kc), ("rstd",)], writes=[dst_res(kc)])
        emit_out_blocks()

    def emit_xload(t, blk):
        s = ring("xin", 4)
        r0 = (t * NBLK + blk) * 128
        P.op("sp", lambda e, s=s, r0=r0: e.dma_start(out=xin[:, s, :], in_=x_d[r0:r0 + 128, :]),
             writes=[("xin", s)], dma=("xin", s))
        return s

    xslots = {}

    def run_interleaved(primary, filler, nprimary, nfiller):
        done_f = 0
        for i in range(nprimary):
            next(primary, None)
            want = min(nfiller, -(-((i + 1) * nfiller) // (nprimary - 1)))
            while done_f < want:
                next(filler, None)
                done_f += 1
        for _ in primary:
            pass
        for _ in filler:
            pass

    for t in range(ntiles):
        cur["t"] = t
        first = (t % tiles_per_seq == 0)
        last_of_seq = (t % tiles_per_seq == tiles_per_seq - 1)
        for blk in range(NBLK):
            if (t, blk) not in xslots:
                xslots[(t, blk)] = emit_xload(t, blk)
            s = xslots[(t, blk)]
            for half in range(2):
                b = psget()
                for j in range(4):
                    cidx = half * 4 + j
                    P.op("pe", lambda e, b=b, j=j, s=s, cidx=cidx: e.transpose(
                        psf[:, b, j * 128:(j + 1) * 128], xin[:, s, cidx * 128:(cidx + 1) * 128], identf[:]),
                        reads=[("xin", s)], writes=[("ps", b)], inc=(j == 3))
                eng = "act" if half == 0 else "dve"
                if eng == "act":
                    fn = lambda e, b=b, half=half, blk=blk: e.activation(
                        xT[:, half * 4:half * 4 + 4, blk * 128:(blk + 1) * 128],
                        psf[:, b, :].rearrange("p (a q) -> p a q", a=4), AF.Copy)
                else:
                    fn = lambda e, b=b, half=half, blk=blk: e.tensor_copy(
                        xT[:, half * 4:half * 4 + 4, blk * 128:(blk + 1) * 128],
                        psf[:, b, :].rearrange("p (a q) -> p a q", a=4))
                P.op(eng, fn, reads=[("ps", b)], writes=[("xT", half * 4 + j) for j in range(4)])

        hrhs = lambda kc: bigc(H0 + kc)
        hres = lambda kc: ("big", H0 + kc)

        def gen_conv(l, first, last_of_seq):
            def emit_cb(prev):
                slot_, i_, s_ = prev
                bcb = psget()
                mm_group(bcb, slot_, 256, hrhs, hres)
                P.op("dve", lambda e, s_=s_, bcb=bcb, i_=i_: e.tensor_tensor(bigc(CV0 + i_), psf[:, bcb, :], zb[:, s_, :], ALU.mult),
                     reads=[("ps", bcb), ("z", s_)], writes=[("big", CV0 + i_)])

            prev = None
            for i in range(8):
                slot = wget("win", l, 0, MC + 384 * i, 384)
                if prev is not None:
                    emit_cb(prev)
                bcc, bcu = psget(), psget()
                mm_group(bcc, slot, 0, hrhs, hres)
                mm_group(bcu, slot, 128, hrhs, hres)
                s = ring("conv", 2)
                P.op("act", lambda e, s=s, bcc=bcc: e.activation(ccs[:, 0, :], psf[:, bcc, :], AF.Copy),
                     reads=[("ps", bcc)], writes=[("ccs", 0)])
                if first:
                    P.op("dve", lambda e, s=s: e.memset(yb[:, s, 0:2], 0.0), writes=[("ybh", s)])
                else:
                    P.op("dve", lambda e, s=s, l=l, i=i: e.tensor_copy(yb[:, s, 0:2], yh[:, l, i, :]),
                         reads=[("yh", l, i)], writes=[("ybh", s)])
                P.op("dve", lambda e, s=s, bcu=bcu: e.tensor_tensor(yb[:, s, 2:T + 2], psf[:, bcu, :], ccs[:, 0, :], ALU.mult),
                     reads=[("ps", bcu), ("ccs", 0)], writes=[("yb", s)])
                P.op("act", lambda e, s=s, l=l, i=i: e.activation(zb[:, s, :], yb[:, s, 2:T + 2], AF.Identity,
                                                                  bias=col("cb", l, i), scale=col("cw2", l, i)),
                     reads=[("yb", s)], writes=[("z", s)])
                P.op("dve", lambda e, s=s, l=l, i=i: e.scalar_tensor_tensor(zb[:, s, :], yb[:, s, 1:T + 1], col("cw1", l, i), zb[:, s, :], ALU.mult, ALU.add),
                     reads=[("yb", s), ("ybh", s), ("z", s)], writes=[("z", s)])
                P.op("dve", lambda e, s=s, l=l, i=i: e.scalar_tensor_tensor(zb[:, s, :], yb[:, s, 0:T], col("cw0", l, i), zb[:, s, :], ALU.mult, ALU.add),
                     reads=[("yb", s), ("ybh", s), ("z", s)], writes=[("z", s)])
                if not last_of_seq:
                    P.op("dve", lambda e, s=s, l=l, i=i: e.tensor_copy(yh[:, l, i, :], yb[:, s, T:T + 2]),
                         reads=[("yb", s)], writes=[("yh", l, i)])
                prev = (slot, i, s)
                yield
            emit_cb(prev)
            yield

        def gen_yconv(l):
            for jg in range(2):
                sg = wget("win", l, 0, MGC + 512 * jg, 512)
                sw = wget("wco", l, 0, 512 * jg, 512)
                for jj in range(4):
                    j = 4 * jg + jj
                    bg = psget()
                    mm_group(bg, sg, jj * 128, hrhs, hres)
                    gs = ring("gate", 2)
                    P.op("act", lambda e, bg=bg, gs=gs, l=l, j=j: e.activation(gate[:, gs, :], psf[:, bg, :], AF.Tanh,
                                                                               bias=hbias[:, l, 8 + j:9 + j], scale=0.5),
                         reads=[("ps", bg)], writes=[("gate", gs)])
                    by = psget()
                    mm_group(by, sw, jj * 128, lambda kc: bigc(CV0 + kc), lambda kc: ("big", CV0 + kc))
                    P.op("dve", lambda e, by=by, gs=gs, j=j: e.scalar_tensor_tensor(mbuf[:, j, :], gate[:, gs, :], 1.0, psf[:, by, :], ALU.add, ALU.mult),
                         reads=[("ps", by), ("gate", gs)], writes=[("m", j)])
                    yield

        def gen_attn(l, first, last_of_seq):
            units = [(blk, g) for blk in range(NBLK) for g in range(4)]

            def emit_scores(blk, g):
                p = g % 2
                cb0 = 4 * (g // 2)
                kc2 = g // 2
                kbs = [1] if (first and blk == 0) else [0, 1]
                pts = {}
                for kb in kbs:
                    slot_k = blk + kb
                    b = psget()
                    P.op("pe", lambda e, b=b, p=p, cb0=cb0, kc2=kc2, slot_k=slot_k, blk=blk, l=l: e.matmul(
                        psf[:, b, :].rearrange("p (a q) -> p a q", a=4),
                        kT[p * 64:(p + 1) * 64, l, kc2, slot_k * 128:(slot_k + 1) * 128],
                        big[p * 64:(p + 1) * 64, Q0 + cb0:Q0 + cb0 + 4, blk * 128:(blk + 1) * 128],
                        start=True, stop=True),
                        reads=[("kT", l, slot_k)] + [("big", Q0 + cb0 + a) for a in range(4)], writes=[("ps", b)], inc=True)
                    se = ring("sexp", 2)
                    P.op("act", lambda e, b=b, se=se: e.activation(sexp[:, se, :], psf[:, b, :], AF.Exp),
                         reads=[("ps", b)], writes=[("sexp", se)])
                    pt = ring("PT", 2 * (ADEPTH + 1))
                    pts[kb] = pt
                    P.op("dve", lambda e, se=se, pt=pt, kb=kb, g=g: e.tensor_tensor(
                        PT[:, pt, :].rearrange("p (a q) -> p a q", a=4),
                        sexp[:, se, :].rearrange("p (a q) -> p a q", a=4),
                        etab[:, kb, 4 * g:4 * g + 4, :], ALU.mult),
                        reads=[("sexp", se)], writes=[("PT", pt)])
                return kbs, pts

            def emit_pv(blk, g, kbs, pts, an):
                bo = psget()
                for a in range(4):
                    for ki, kb in enumerate(kbs):
                        slot_k = blk + kb
                        P.op("pe", lambda e, bo=bo, a=a, kb=kb, pt=pts[kb], slot_k=slot_k, g=g, ki=ki, nk=len(kbs), l=l: e.matmul(
                            psf[:, bo, a * 65:(a + 1) * 65], PT[:, pt, a * 128:(a + 1) * 128],
                            Vt[:, l, slot_k, g * 65:(g + 1) * 65], start=(ki == 0), stop=(ki == nk - 1)),
                            reads=[("PT", pts[kb]), ("V", l, slot_k)], writes=[("ps", bo)],
                            inc=(a == 3 and ki == len(kbs) - 1))
                ds = ring("den", 2)
                P.op("dve", lambda e, bo=bo, ds=ds, g=g, l=l: e.tensor_tensor(
                    den[:, ds, :], psf[:, bo, 0:260].rearrange("p (a d) -> p a d", a=4)[:, :, 64],
                    sexpk[:, l * 16 + 4 * g:l * 16 + 4 * g + 4], ALU.add),
                    reads=[("ps", bo)], writes=[("den", ds)])
                P.op("dve", lambda e, ds=ds: e.reciprocal(rden[:, ds, :], den[:, ds, :]),
                     reads=[("den", ds)], writes=[("rden", ds)])
                P.op("dve", lambda e, bo=bo, g=g, ds=ds, an=an: e.tensor_tensor(
                    attn_n[:, an, g * 256:(g + 1) * 256].rearrange("p (a d) -> p a d", a=4),
                    psf[:, bo, 0:260].rearrange("p (a d) -> p a d", a=4)[:, :, 0:64],
                    rden[:, ds, :].unsqueeze(2).broadcast_to([128, 4, 64]), ALU.mult),
                    reads=[("ps", bo), ("rden", ds)], writes=[("attn_n", an, g)])

            def emit_tr(blk, an):
                for j in range(8):
                    P.op("pe", lambda e, j=j, an=an: e.transpose(pst[:, j * 128:(j + 1) * 128], attn_n[:, an, j * 128:(j + 1) * 128], identb[:]),
                         reads=[("attn_n", an, g_) for g_ in range(4)], writes=[("pst",)], inc=(j == 7))
                P.op("dve", lambda e, blk=blk: e.tensor_copy(
                    big[:, AT0:AT0 + 8, blk * 128:(blk + 1) * 128], pst[:].rearrange("p (a q) -> p a q", a=8)),
                    reads=[("pst",)], writes=[("big", AT0 + j) for j in range(8)])

            ans = {}
            pend = []
            trq = []

            def do_pv(item):
                pb, pg, pk, pp = item
                emit_pv(pb, pg, pk, pp, ans[pb])
                if pg == 3:
                    trq.append([pb, TRLAG])

            def tick_tr(force=False):
                for it in list(trq):
                    it[1] -= 1
                    if it[1] <= 0 or force:
                        emit_tr(it[0], ans[it[0]])
                        trq.remove(it)

            for (blk, g) in units:
                if g == 0:
                    ans[blk] = ring("attn_n", 2)
                sc = emit_scores(blk, g)
                pend.append((blk, g, sc[0], sc[1]))
                if len(pend) > ADEPTH:
                    do_pv(pend.pop(0))
                tick_tr()
                yield
            while pend:
                do_pv(pend.pop(0))
                tick_tr()
                yield
            yield
            yield
            tick_tr(force=True)
            if not last_of_seq:
                P.op("dve", lambda e, l=l: e.tensor_copy(kT[:, l, :, 0:128], kT[:, l, :, 512:640]),
                     reads=[("kT", l, 4)], writes=[("kT", l, 0)])
                P.op("dve", lambda e, l=l: e.tensor_copy(Vt[:, l, 0, :], Vt[:, l, 4, :]),
                     reads=[("V", l, 4)], writes=[("V", l, 0)])
            yield

        def chain(*gens):
            for g_ in gens:
                for _ in g_:
                    yield

        def emit_layer(l, first, last_of_seq):
            emit_norm(lambda kc, l=l: col("g_mix", l, kc), lambda kc: bigc(H0 + kc), lambda kc: ("big", H0 + kc))
            for qi in range(2):
                slot = wget("win", l, 0, MQ + 512 * qi, 512)
                if qi == 0:
                    bs = [psget() for _ in range(4)]
                    for kc in range(KC):
                        for c in range(4):
                            P.op("pe", lambda e, b=bs[c], slot=slot, kc=kc, c=c: e.matmul(
                                psf[:, b, :], wbuf[:, slot, kc, c * 128:(c + 1) * 128], bigc(H0 + kc),
                                start=(kc == 0), stop=(kc == KC - 1)),
                                reads=[("w", slot), ("big", H0 + kc)], writes=[("ps", bs[c])], inc=(kc == KC - 1))
                    for c in range(4):
                        P.op("act", lambda e, b=bs[c], c=c: e.activation(bigc(Q0 + c), psf[:, b, :], AF.Copy, scale=0.125),
                             reads=[("ps", bs[c])], writes=[("big", Q0 + c)])
                    continue
                for c in range(4):
                    b = psget()
                    mm_group(b, slot, c * 128, hrhs, hres)
                    P.op("act", lambda e, b=b, qi=qi, c=c: e.activation(bigc(Q0 + 4 * qi + c), psf[:, b, :], AF.Copy, scale=0.125),
                         reads=[("ps", b)], writes=[("big", Q0 + 4 * qi + c)])
            slot = wget("win", l, 0, MKV, 512)
            for cc_ in range(2):
                b = psget()
                mm_group(b, slot, cc_ * 128, hrhs, hres)
                P.op("dve", lambda e, b=b, cc_=cc_, l=l: e.tensor_copy(kT[:, l, cc_, 128:640], psf[:, b, :]),
                     reads=[("ps", b)], writes=[("kT", l, 1 + s_) for s_ in range(4)])
            for blk in range(NBLK):
                b = psget()
                for kc in range(KC):
                    P.op("pe", lambda e, b=b, slot=slot, kc=kc, blk=blk: e.matmul(
                        psf[:, b, 0:256], bigc(H0 + kc)[:, blk * 128:(blk + 1) * 128], wbuf[:, slot, kc, 256:512],
                        start=(kc == 0), stop=(kc == KC - 1)),
                        reads=[("w", slot), ("big", H0 + kc)], writes=[("ps", b)], inc=(kc == KC - 1))
                P.op("dve", lambda e, b=b, blk=blk, l=l: e.tensor_copy(
                    Vt[:, l, 1 + blk, :].rearrange("p (g d) -> p g d", g=4)[:, :, 0:64],
                    psf[:, b, 0:256].rearrange("p (g d) -> p g d", g=4)),
                    reads=[("ps", b)], writes=[("V", l, 1 + blk)])
            run_interleaved(gen_attn(l, first, last_of_seq), chain(gen_conv(l, first, last_of_seq), gen_yconv(l)),
                            NBLK * 4 + ADEPTH + 3, 17)
            for jg in range(2):
                sg = wget("win", l, 0, MGA + 512 * jg, 512)
                sw = wget("wao", l, 0, 512 * jg, 512)
                for jj in range(4):
                    j = 4 * jg + jj
                    bg = psget()
                    mm_group(bg, sg, jj * 128, hrhs, hres)
                    gs = ring("gate", 2)
                    P.op("act", lambda e, bg=bg, gs=gs, l=l, j=j: e.activation(gate[:, gs, :], psf[:, bg, :], AF.Tanh,
                                                                               bias=hbias[:, l, j:j + 1], scale=0.5),
                         reads=[("ps", bg)], writes=[("gate", gs)])
                    by = psget()
                    mm_group(by, sw, jj * 128, lambda kc: bigc(AT0 + kc), lambda kc: ("big", AT0 + kc))
                    ts = ring("tmpb", 2)
                    P.op("dve", lambda e, by=by, gs=gs, ts=ts: e.scalar_tensor_tensor(tmpb[:, ts, :], gate[:, gs, :], 1.0, psf[:, by, :], ALU.add, ALU.mult),
                         reads=[("ps", by), ("gate", gs)], writes=[("tmpb", ts)])
                    P.op("dve", lambda e, ts=ts, j=j: e.tensor_tensor(bigc(CV0 + j), tmpb[:, ts, :], mbuf[:, j, :], ALU.add),
                         reads=[("tmpb", ts), ("m", j)], writes=[("big", CV0 + j)])
            for og in range(2):
                slot = wget("wo", l, 0, 512 * og, 512)
                for jj in range(4):
                    j = 4 * og + jj
                    b = psget()
                    mm_group(b, slot, jj * 128, lambda kc: bigc(CV0 + kc), lambda kc: ("big", CV0 + kc))
                    P.op("dve", lambda e, b=b, j=j: e.scalar_tensor_tensor(xT[:, j, :], psf[:, b, :], 0.5, xT[:, j, :], ALU.mult, ALU.add),
                         reads=[("ps", b), ("xT", j)], writes=[("xT", j)])
                    P.op("act", lambda e, j=j, l=l: e.activation(a2[:, j, :], xT[:, j, :], AF.Copy, scale=col("g_mlp", l, j)),
                         reads=[("xT", j)], writes=[("a2", j)])
            for fg in range(8):
                slot = wget("wup", l, 0, 512 * fg, 512)
                for ff in range(4):
                    f = 4 * fg + ff
                    b = psget()
                    mm_group(b, slot, ff * 128, lambda kc: a2[:, kc, :], lambda kc: ("a2", kc))
                    rs = ring("relu", 2)
                    P.op("act", lambda e, b=b, rs=rs: e.activation(relu[:, rs, :], psf[:, b, :], AF.Relu),
                         reads=[("ps", b)], writes=[("relu", rs)])
                    P.op("dve", lambda e, rs=rs, f=f: e.tensor_tensor(bigc(f), relu[:, rs, :], relu[:, rs, :], ALU.mult),
                         reads=[("relu", rs)], writes=[("big", f)])
                if fg == 0:
                    b = psget()
                    for kc in range(KC):
                        s = emit_square(kc)
                        P.op("pe", lambda e, kc=kc, s=s, b=b: e.matmul(psf[:, b, :], ones[:], xsq[:, s, :], start=(kc == 0), stop=(kc == KC - 1)),
                             reads=[("xsq", s)], writes=[("ps", b)], inc=True)
                    P.op("act", lambda e, b=b: e.activation(mse[:], psf[:, b, :], AF.Ln, bias=epsc[:, 0:1], scale=1.0 / D),
                         reads=[("ps", b)], writes=[("mse",)])
                    P.op("act", lambda e: e.activation(rstd[:], mse[:], AF.Exp, scale=-1.0),
                         reads=[("mse",)], writes=[("rstd",)])
            for og in range(2):
                bs = [psget() for _ in range(4)]
                for kq in range(4):
                    slot = wget("wdn", l, 1024 * kq, 512 * og, 512)
                    for jj in range(4):
                        for kc in range(KC):
                            f = kq * 8 + kc
                            P.op("pe", lambda e, b=bs[jj], slot=slot, jj=jj, kc=kc, f=f, kq=kq: e.matmul(
                                psf[:, b, :], wbuf[:, slot, kc, jj * 128:(jj + 1) * 128], bigc(f),
                                start=(kq == 0 and kc == 0), stop=(kq == 3 and kc == KC - 1)),
                                reads=[("w", slot), ("big", f)], writes=[("ps", bs[jj])], inc=(kc == KC - 1))
                for jj in range(4):
                    j = 4 * og + jj
                    ts = ring("tmpb", 2)
                    P.op("dve", lambda e, b=bs[jj], ts=ts: e.tensor_tensor(tmpb[:, ts, :], psf[:, b, :], rstd[:], ALU.mult),
                         reads=[("ps", bs[jj]), ("rstd",)], writes=[("tmpb", ts)])
                    P.op("dve", lambda e, ts=ts, j=j: e.tensor_tensor(xT[:, j, :], tmpb[:, ts, :], xT[:, j, :], ALU.add),
                         reads=[("tmpb", ts), ("xT", j)], writes=[("xT", j)])

        for l in range(L):
            emit_layer(l, first, last_of_seq)
        dsrc = None
        if t + 1 < ntiles:
            for blk in range(4):
                xslots[(t + 1, blk)] = emit_xload(t + 1, blk)
        b = psget()
        for kc in range(KC):
            s = emit_square(kc)
            P.op("pe", lambda e, kc=kc, s=s, b=b: e.matmul(psf[:, b, :], ones[:], xsq[:, s, :], start=(kc == 0), stop=(kc == KC - 1)),
                 reads=[("xsq", s)], writes=[("ps", b)], inc=True)
        P.op("act", lambda e, b=b: e.activation(mse[:], psf[:, b, :], AF.Ln, bias=epsc[:, 0:1], scale=1.0 / D),
             reads=[("ps", b)], writes=[("mse",)])
        P.op("act", lambda e: e.activation(rstd[:], mse[:], AF.Exp, scale=-0.5),
             reads=[("mse",)], writes=[("rstd",)])
        for kc in range(KC):
            P.op("dve", lambda e, kc=kc: e.scalar_tensor_tensor(mbuf[:, kc, :], xT[:, kc, :], gfin(kc), rstd[:], ALU.mult, ALU.mult),
                 reads=[("xT", kc), ("rstd",)], writes=[("m", kc)])
        for blk in range(NBLK):
            pending_out.append((t, blk))

    emit_out_blocks()
    emit_weight_dmas()
    P.final_wait("sp", [(k, v) for k, v in P.cnt.items() if isinstance(k, tuple) and k[0] == "xo"])

    sems = {}
    for k in P.sem_keys():
        nm = "s_" + ("_".join(str(z) for z in k) if isinstance(k, tuple) else str(k))
        sems[k] = nc.alloc_semaphore(nm)

    def runner(name):
        def body(e):
            for waits, fn, inc in P.q[name]:
                for s, v in waits:
                    e.wait_ge(sems[s], v)
                if fn is None:
                    continue
                ins = fn(e)
                if inc is not None:
                    ins.then_inc(sems[inc[0]], inc[1])
        return body

    with nc.Block() as block:
        block.tensor(runner("pe"))
        block.scalar(runner("act"))
        block.vector(runner("dve"))
        block.gpsimd(runner("pool"))
        block.sync(runner("sp"))
    return nc, P


def host_consts(depth):
    slopes = np.power(np.float32(2.0), -8.0 * np.arange(1, NQH + 1, dtype=np.float32) / NQH).astype(np.float32)
    j = np.arange(128)[:, None]
    r = np.arange(128)[None, :]
    et = np.zeros((128, 2, NQH, 128), dtype=np.float32)
    for h in range(NQH):
        dprev = (128 + r - j).astype(np.float32)
        dcur = (r - j).astype(np.float32)
        et[:, 0, h, :] = np.where(j > r, np.exp(-slopes[h] * np.maximum(dprev, 0.0)), 0.0)
        et[:, 1, h, :] = np.where(j <= r, np.exp(-slopes[h] * np.maximum(dcur, 0.0)), 0.0)
    identf = np.eye(128, dtype=np.float32)
    identb = np.eye(128, dtype=np.float32).astype(ml_dtypes.bfloat16)
    return et.reshape(128, -1), identf, identb


def prep_weights(inputs, depth):
    L = depth
    f = lambda a: np.ascontiguousarray(np.asarray(a, dtype=np.float32))
    perm = _win_perm()
    w_in = f(np.asarray(inputs["w_in"])[:L][:, :, perm])

    def cols(v):
        return np.asarray(v, dtype=np.float32).reshape(8, 128).T

    pv = np.zeros((128, L * 64 + 8), dtype=np.float32)
    for l in range(L):
        o = l * 64
        pv[:, o + 0:o + 8] = cols(inputs["g_mix"][l])
        pv[:, o + 8:o + 16] = cols(inputs["g_mlp"][l])
        pv[:, o + 16:o + 24] = cols(np.asarray(inputs["b_gates"])[l, :D])
        pv[:, o + 24:o + 32] = cols(np.asarray(inputs["b_gates"])[l, D:])
        pv[:, o + 32:o + 40] = cols(np.asarray(inputs["conv_w"])[l, 0])
        pv[:, o + 40:o + 48] = cols(np.asarray(inputs["conv_w"])[l, 1])
        pv[:, o + 48:o + 56] = cols(np.asarray(inputs["conv_w"])[l, 2])
        pv[:, o + 56:o + 64] = cols(inputs["conv_b"][l])
    pv[:, L * 64:L * 64 + 8] = cols(inputs["g_final"])
    sinkb = np.ascontiguousarray(np.broadcast_to(np.asarray(inputs["sinks"], dtype=np.float32)[:L].reshape(1, L * 16), (128, L * 16)))
    et, identf, identb = host_consts(L)
    shared = {
        "w_in": w_in,
        "w_ao": f(np.asarray(inputs["w_attn_out"])[:L]),
        "w_co": f(np.asarray(inputs["w_conv_out"])[:L]),
        "w_o": f(np.asarray(inputs["w_o"])[:L]),
        "w_up": f(np.asarray(inputs["w_up"])[:L]),
        "w_dn": f(np.asarray(inputs["w_down"])[:L]),
        "pvec": pv, "sinkb": sinkb, "etab": et, "identf": identf, "identb": identb,
    }
    return shared


_CACHE = {}


def run(inputs, depth=L_FULL, seq=SEQ, dbg=None):
    x = np.asarray(inputs["x"], dtype=np.float32)
    B, S, _ = x.shape
    assert S == seq and B % N_CORES == 0
    bpc = B // N_CORES
    tiles_per_seq = S // T
    ntiles = bpc * tiles_per_seq
    key = (ntiles, depth, tiles_per_seq, dbg)
    if key not in _CACHE:
        _CACHE[key] = build_program(ntiles, depth, tiles_per_seq, dbg)[0]
    nc = _CACHE[key]
    shared = prep_weights(inputs, depth)
    in_maps = []
    for c in range(N_CORES):
        m = dict(shared)
        m["x"] = np.ascontiguousarray(x[c * bpc:(c + 1) * bpc].reshape(bpc * S, D))
        in_maps.append(m)
    res = run_bass_kernel_spmd(nc, in_maps, core_ids=list(range(N_CORES)))
    outs = [np.asarray(r["out"], dtype=np.float32).reshape(bpc, S, D) for r in res.results]
    return np.concatenate(outs, axis=0)


def kernel(**inputs):
    return run(inputs)
```

```python
import numpy as np
import ml_dtypes
import concourse.bass as bass
import concourse.mybir as mybir
from concourse.bass_utils import run_bass_kernel_spmd

F32 = mybir.dt.float32
BF16 = mybir.dt.bfloat16
AF = mybir.ActivationFunctionType
ALU = mybir.AluOpType

D = 1024
KC = 8
T = 512
NBLK = 4
L_FULL = 2
SEQ = 2048
N_CORES = 8
NQH = 16
HD = 64
EPS = 1e-6
NWSLOT = 5
NPSB = 7
ADEPTH = 2
TRLAG = 2

OQ, OK_, OV, OCB, OCC, OCU, OGA, OGC = 0, 1024, 1280, 1536, 2560, 3584, 4608, 5632
HE = [0, 1, 2, 3, 8, 9, 10, 11]
HO = [4, 5, 6, 7, 12, 13, 14, 15]


def _win_perm():
    cols = []
    for c in range(8):
        cols += list(range(OQ + HE[c] * 64, OQ + HE[c] * 64 + 64))
        cols += list(range(OQ + HO[c] * 64, OQ + HO[c] * 64 + 64))
    cols += list(range(OK_, OK_ + 256))
    cols += list(range(OV, OV + 256))
    for i in range(8):
        cols += list(range(OCC + 128 * i, OCC + 128 * i + 128))
        cols += list(range(OCU + 128 * i, OCU + 128 * i + 128))
        cols += list(range(OCB + 128 * i, OCB + 128 * i + 128))
    cols += list(range(OGC, OGC + 1024))
    cols += list(range(OGA, OGA + 1024))
    return np.asarray(cols, dtype=np.int64)


MQ, MKV, MC, MGC, MGA = 0, 1024, 1536, 4608, 5632


def _tile_catalogue():
    t512 = [("win", 0, MQ, 512), ("win", 0, MQ + 512, 512), ("win", 0, MKV, 512)]
    for jg in range(2):
        t512 += [("win", 0, MGC + 512 * jg, 512), ("wco", 0, 512 * jg, 512)]
    for jg in range(2):
        t512 += [("win", 0, MGA + 512 * jg, 512), ("wao", 0, 512 * jg, 512)]
    t512 += [("wo", 0, 512 * og, 512) for og in range(2)]
    t512 += [("wup", 0, 512 * fg, 512) for fg in range(8)]
    t512 += [("wdn", 1024 * kq, 512 * og, 512) for og in range(2) for kq in range(4)]
    t384 = [("win", 0, MC + 384 * i, 384) for i in range(8)]
    return t512, t384


T512, T384 = _tile_catalogue()
TIDX = {k: (0, i) for i, k in enumerate(T512)}
TIDX.update({k: (1, i) for i, k in enumerate(T384)})


class Prog:
    ENG = ["pe", "act", "dve", "pool", "sp"]

    def __init__(self):
        self.q = {e: [] for e in self.ENG}
        self.cnt = {}
        self.seen = {e: {} for e in self.ENG}
        self.lastw = {}
        self.readers = {}

    def op(self, eng, fn, reads=(), writes=(), inc=True, dma=None):
        deps = {}

        def add(t):
            if t is None:
                return
            s, v = t
            if v > deps.get(s, 0):
                deps[s] = v

        for r in reads:
            add(self.lastw.get(r))
        for w in writes:
            add(self.lastw.get(w))
            rd = self.readers.get(w)
            if rd:
                for s, v in rd.items():
                    add((s, v))
        waits = []
        seen = self.seen[eng]
        for s, v in deps.items():
            if s == "pe" and eng == "pe":
                continue
            if v > seen.get(s, 0):
                waits.append((s, v))
                seen[s] = v
        if dma is not None:
            self.cnt[dma] = self.cnt.get(dma, 0) + 16
            tok = (dma, self.cnt[dma])
            incspec = (dma, 16)
        elif inc:
            self.cnt[eng] = self.cnt.get(eng, 0) + 1
            tok = (eng, self.cnt[eng])
            incspec = (eng, 1)
        else:
            tok = (eng, self.cnt.get(eng, 0) + 1)
            incspec = None
        self.q[eng].append((waits, fn, incspec))
        for r in reads:
            rd = self.readers.setdefault(r, {})
            if tok[1] > rd.get(tok[0], 0):
                rd[tok[0]] = tok[1]
        for w in writes:
            self.lastw[w] = tok
            self.readers[w] = {}
        return tok

    def final_wait(self, eng, toks):
        waits = []
        for s, v in toks:
            if v > self.seen[eng].get(s, 0):
                waits.append((s, v))
                self.seen[eng][s] = v
        self.q[eng].append((waits, None, None))

    def sem_keys(self):
        return list(self.cnt.keys())


def build_program(ntiles, depth, tiles_per_seq=4, dbg=None):
    nc = bass.Bass("TRN2", target_bir_lowering=False)
    L = depth
    ntok = ntiles * T
    dt = nc.dram_tensor
    x_d = dt("x", [ntok, D], F32, kind="ExternalInput").ap()
    out_d = dt("out", [ntok, D], F32, kind="ExternalOutput").ap()
    win_d = dt("w_in", [L, D, 6656], F32, kind="ExternalInput").ap()
    wao_d = dt("w_ao", [L, D, D], F32, kind="ExternalInput").ap()
    wco_d = dt("w_co", [L, D, D], F32, kind="ExternalInput").ap()
    wo_d = dt("w_o", [L, D, D], F32, kind="ExternalInput").ap()
    wup_d = dt("w_up", [L, D, 4 * D], F32, kind="ExternalInput").ap()
    wdn_d = dt("w_dn", [L, 4 * D, D], F32, kind="ExternalInput").ap()
    NPV = L * 64 + 8
    pvec_d = dt("pvec", [128, NPV], F32, kind="ExternalInput").ap()
    sink_d = dt("sinkb", [128, L * 16], F32, kind="ExternalInput").ap()
    etab_d = dt("etab", [128, 2 * 16 * 128], F32, kind="ExternalInput").ap()
    idf_d = dt("identf", [128, 128], F32, kind="ExternalInput").ap()
    idb_d = dt("identb", [128, 128], BF16, kind="ExternalInput").ap()
    scr512 = dt("wscr512", [L, len(T512), 128, KC * 512], BF16, kind="Internal").ap()
    scr384 = dt("wscr384", [L, len(T384), 128, KC * 384], BF16, kind="Internal").ap()

    sb = nc.alloc_sbuf_tensor
    big = sb("big", [128, 32, T], BF16)
    xT = sb("xT", [128, KC, T], F32)
    mbuf = sb("mbuf", [128, KC, T], F32)
    a2 = sb("a2", [128, KC, T], BF16)
    xin = sb("xin", [128, 4, D], F32)
    xout = sb("xout", [128, 2, D], F32)
    xsq = sb("xsq", [128, 3, T], BF16)
    rstd = sb("rstd", [128, T], F32)
    mse = sb("mse", [128, T], F32)
    epsc = sb("epsc", [128, 1], F32)
    kT = sb("kT", [128, L, 2, 5 * 128], BF16)
    Vt = sb("Vt", [128, L, 5, 4 * 65], BF16)
    ccs = sb("ccs", [128, 1, T], F32)
    yb = sb("yb", [128, 2, T + 2], F32)
    zb = sb("zb", [128, 2, T], F32)
    yh = sb("yh", [128, L, 8, 2], F32)
    sexp = sb("sexp", [128, 2, T], F32)
    PT = sb("PT", [128, 2 * (ADEPTH + 1), T], BF16)
    attn_n = sb("attn_n", [128, 2, D], BF16)
    den = sb("den", [128, 2, 4], F32)
    rden = sb("rden", [128, 2, 4], F32)
    gate = sb("gate", [128, 2, T], F32)
    tmpb = sb("tmpb", [128, 2, T], F32)
    relu = sb("relu", [128, 2, T], F32)
    wbuf = sb("wbuf", [128, NWSLOT, KC, 512], BF16)
    etab = sb("etab_s", [128, 2, 16, 128], F32)
    pvec = sb("pvec_s", [128, NPV], F32)
    hbias = sb("hbias", [128, L, 16], F32)
    sexpk = sb("sinkexp", [128, L * 16], F32)
    identf = sb("identf_s", [128, 128], F32)
    identb = sb("identb_s", [128, 128], BF16)
    ones = sb("ones", [128, 128], BF16)

    psf = nc.alloc_psum_tensor("psf", [128, NPSB, 512], F32)
    pst = nc.alloc_psum_tensor("pst", [128, 1024], BF16)

    P = Prog()
    P.sbuf_left = nc.sbuf_bytes_remaining

    def col(name, l, c):
        base = {"g_mix": 0, "g_mlp": 8, "b_ga": 16, "b_gc": 24, "cw0": 32, "cw1": 40, "cw2": 48, "cb": 56}[name]
        o = l * 64 + base + c
        return pvec[:, o:o + 1]

    def gfin(c):
        o = L * 64 + c
        return pvec[:, o:o + 1]

    P.op("sp", lambda e: e.dma_start(out=pvec[:], in_=pvec_d), writes=[("c", 0)], dma="cst")
    P.op("sp", lambda e: e.dma_start(out=sexpk[:], in_=sink_d), writes=[("c", 1)], dma="cst")
    P.op("sp", lambda e: e.dma_start(out=etab[:].rearrange("p a h q -> p (a h q)"), in_=etab_d), writes=[("c", 2)], dma="cst")
    P.op("sp", lambda e: e.dma_start(out=identf[:], in_=idf_d), writes=[("c", 3)], dma="cst")
    P.op("sp", lambda e: e.dma_start(out=identb[:], in_=idb_d), writes=[("c", 4)], dma="cst")
    CST = [("c", i) for i in range(5)]
    P.op("dve", lambda e: e.memset(ones[:], 1.0), reads=CST, writes=[("c", 5)])
    P.op("dve", lambda e: e.memset(epsc[:], EPS), writes=[("c", 6)])
    P.op("dve", lambda e: e.memset(Vt[:], 1.0), writes=[("V", l, s) for l in range(L) for s in range(5)])
    for l in range(L):
        P.op("dve", lambda e, l=l: e.tensor_scalar(hbias[:, l, :], pvec[:, l * 64 + 16:l * 64 + 32], 0.5, None, ALU.mult),
             reads=CST, writes=[("c", 7 + l)])
    P.op("act", lambda e: e.activation(sexpk[:], sexpk[:], AF.Exp), reads=CST, writes=[("c", 1)])
    ALLC = [("c", i) for i in range(7 + L)]
    P.op("pe", lambda e: e.nop(), reads=ALLC, inc=False)
    P.op("act", lambda e: e.nop(), reads=ALLC)
    P.op("dve", lambda e: e.nop(), reads=ALLC)
    P.op("pool", lambda e: e.nop(), reads=ALLC)

    def wsrc(kind, l, r0, c0, n):
        dten = {"win": win_d, "wao": wao_d, "wco": wco_d, "wo": wo_d, "wup": wup_d, "wdn": wdn_d}[kind]
        return dten[l, r0:r0 + 1024, c0:c0 + n].rearrange("(kc p) n -> p kc n", p=128)

    wtiles = []
    cur = {"t": 0}

    def wget(kind, l, r0, c0, ncol):
        i = len(wtiles)
        slot = i % NWSLOT
        res = ("w", slot)
        rd = dict(P.readers.get(res, {}))
        wtiles.append(((kind, l, r0, c0, ncol), slot, rd, cur["t"]))
        fill = i // NWSLOT + 1
        P.lastw[res] = (res, 32 * fill)
        P.readers[res] = {}
        P.cnt[res] = 32 * fill
        return slot

    def emit_weight_dmas():
        seen = P.seen["pool"]
        q = P.q["pool"]
        st_cnt = {}
        pend_st = []

        def need(s, v, waits):
            if v > seen.get(s, 0):
                waits.append((s, v))
                seen[s] = v

        def flush_store(k):
            while len(pend_st) > k:
                slot_, scr_, ncol_, fill_ = pend_st.pop(0)
                waits = []
                need(("w", slot_), 32 * fill_, waits)
                st_cnt[slot_] = st_cnt.get(slot_, 0) + 16
                P.cnt[("ws", slot_)] = st_cnt[slot_]
                q.append((waits, (lambda e, slot_=slot_, scr_=scr_, ncol_=ncol_: e.dma_start(
                    out=scr_, in_=wbuf[:, slot_, :, 0:ncol_])), (("ws", slot_), 16)))

        started_bf16 = False
        for i, ((kind, l, r0, c0, ncol), slot, rd, t_) in enumerate(wtiles):
            arr, idx = TIDX[(kind, r0, c0, ncol)]
            scr = (scr512 if arr == 0 else scr384)[l, idx].rearrange("p (kc n) -> p kc n", kc=KC)
            fill = i // NWSLOT + 1
            waits = []
            for s, v in rd.items():
                need(s, v, waits)
            if t_ == 0:
                if st_cnt.get(slot, 0):
                    need(("ws", slot), st_cnt[slot], waits)
                src_ap = wsrc(kind, l, r0, c0, ncol)
                for h in range(2):
                    fn = (lambda e, slot=slot, src_ap=src_ap, ncol=ncol, h=h: e.dma_start(
                        out=wbuf[:, slot, 4 * h:4 * h + 4, 0:ncol], in_=src_ap[:, 4 * h:4 * h + 4, :]))
                    q.append((waits if h == 0 else [], fn, (("w", slot), 16)))
                pend_st.append((slot, scr, ncol, fill))
                flush_store(2)
            else:
                if not started_bf16:
                    flush_store(0)
                    for s_, v_ in st_cnt.items():
                        need(("ws", s_), v_, waits)
                    started_bf16 = True
                fn = (lambda e, slot=slot, scr=scr, ncol=ncol: e.dma_start(out=wbuf[:, slot, :, 0:ncol], in_=scr))
                q.append((waits, fn, (("w", slot), 32)))
        flush_store(0)

    psstate = {"n": 0}

    def psget():
        b = psstate["n"] % NPSB
        psstate["n"] += 1
        return b

    rr = {}

    def ring(name, n):
        v = rr.get(name, 0)
        rr[name] = v + 1
        return v % n

    def mm_group(b, slot, c0, rhs_fn, rhs_res, ncol=128):
        for kc in range(KC):
            P.op("pe",
                 lambda e, b=b, slot=slot, c0=c0, kc=kc: e.matmul(
                     psf[0:ncol, b, :], wbuf[:, slot, kc, c0:c0 + ncol], rhs_fn(kc), start=(kc == 0), stop=(kc == KC - 1)),
                 reads=[("w", slot), rhs_res(kc)], writes=[("ps", b)], inc=(kc == KC - 1))

    def bigc(i):
        return big[:, i, :]

    H0, Q0, CV0, AT0 = 0, 8, 16, 24

    pending_out = []

    def emit_out_blocks():
        while pending_out:
            t_, blk = pending_out.pop(0)
            s = ring("xout", 2)
            for half in range(2):
                b = psget()
                for j in range(4):
                    cidx = half * 4 + j
                    P.op("pe", lambda e, b=b, j=j, cidx=cidx, blk=blk: e.transpose(
                        psf[:, b, j * 128:(j + 1) * 128], mbuf[:, cidx, blk * 128:(blk + 1) * 128], identf[:]),
                        reads=[("m", cidx)], writes=[("ps", b)], inc=(j == 3))
                if half == 0:
                    P.op("act", lambda e, b=b, s=s, half=half: e.activation(xout[:, s, half * 512:(half + 1) * 512], psf[:, b, :], AF.Copy),
                         reads=[("ps", b)], writes=[("xout", s)])
                else:
                    P.op("dve", lambda e, b=b, s=s, half=half: e.tensor_copy(xout[:, s, half * 512:(half + 1) * 512], psf[:, b, :]),
                         reads=[("ps", b)], writes=[("xout", s)])
            r0 = (t_ * NBLK + blk) * 128
            P.op("sp", lambda e, s=s, r0=r0: e.dma_start(out=out_d[r0:r0 + 128, :], in_=xout[:, s, :]),
                 reads=[("xout", s)], dma=("xo", s))

    def emit_square(kc):
        s = ring("xsq", 3)
        if kc % 2 == 0:
            P.op("act", lambda e, kc=kc, s=s: e.activation(xsq[:, s, :], xT[:, kc, :], AF.Square),
                 reads=[("xT", kc)], writes=[("xsq", s)])
        else:
            P.op("dve", lambda e, kc=kc, s=s: e.tensor_tensor(xsq[:, s, :], xT[:, kc, :], xT[:, kc, :], ALU.mult),
                 reads=[("xT", kc)], writes=[("xsq", s)])
        return s

    def emit_norm(gcol, dst_ap, dst_res):
        b = psget()
        for kc in range(KC):
            s = emit_square(kc)
            P.op("pe", lambda e, kc=kc, s=s, b=b: e.matmul(psf[:, b, :], ones[:], xsq[:, s, :], start=(kc == 0), stop=(kc == KC - 1)),
                 reads=[("xsq", s)], writes=[("ps", b)], inc=True)
        P.op("act", lambda e, b=b: e.activation(mse[:], psf[:, b, :], AF.Ln, bias=epsc[:, 0:1], scale=1.0 / D),
             reads=[("ps", b)], writes=[("mse",)])
        P.op("act", lambda e: e.activation(rstd[:], mse[:], AF.Exp, scale=-0.5),
             reads=[("mse",)], writes=[("rstd",)])
        for kc in range(KC):
            P.op("dve", lambda e, kc=kc: e.scalar_tensor_tensor(dst_ap(kc), xT[:, kc, :], gcol(kc), rstd[:], ALU.mult, ALU.mult),
                 reads=[("xT", kc), ("rstd",)], writes=[dst_res(kc)])
        emit_out_blocks()

    def emit_xload(t, blk):
        s = ring("xin", 4)
        r0 = (t * NBLK + blk) * 128
        P.op("sp", lambda e, s=s, r0=r0: e.dma_start(out=xin[:, s, :], in_=x_d[r0:r0 + 128, :]),
             writes=[("xin", s)], dma=("xin", s))
        return s

    xslots = {}

    def run_interleaved(primary, filler, nprimary, nfiller, holdback=2):
        spread = nfiller - holdback
        done_f = 0
        for i in range(nprimary):
            next(primary, None)
            want = ((i + 1) * spread) // nprimary
            while done_f < want:
                next(filler, None)
                done_f += 1
        for _ in primary:
            pass
        for _ in filler:
            pass

    for t in range(ntiles):
        cur["t"] = t
        first = (t % tiles_per_seq == 0)
        last_of_seq = (t % tiles_per_seq == tiles_per_seq - 1)
        for blk in range(NBLK):
            if (t, blk) not in xslots:
                xslots[(t, blk)] = emit_xload(t, blk)
            s = xslots[(t, blk)]
            for half in range(2):
                b = psget()
                for j in range(4):
                    cidx = half * 4 + j
                    P.op("pe", lambda e, b=b, j=j, s=s, cidx=cidx: e.transpose(
                        psf[:, b, j * 128:(j + 1) * 128], xin[:, s, cidx * 128:(cidx + 1) * 128], identf[:]),
                        reads=[("xin", s)], writes=[("ps", b)], inc=(j == 3))
                eng = "act" if half == 0 else "dve"
                if eng == "act":
                    fn = lambda e, b=b, half=half, blk=blk: e.activation(
                        xT[:, half * 4:half * 4 + 4, blk * 128:(blk + 1) * 128],
                        psf[:, b, :].rearrange("p (a q) -> p a q", a=4), AF.Copy)
                else:
                    fn = lambda e, b=b, half=half, blk=blk: e.tensor_copy(
                        xT[:, half * 4:half * 4 + 4, blk * 128:(blk + 1) * 128],
                        psf[:, b, :].rearrange("p (a q) -> p a q", a=4))
                P.op(eng, fn, reads=[("ps", b)], writes=[("xT", half * 4 + j) for j in range(4)])

        hrhs = lambda kc: bigc(H0 + kc)
        hres = lambda kc: ("big", H0 + kc)

        def gen_conv(l, first, last_of_seq):
            def emit_cb(prev):
                slot_, i_, s_ = prev
                bcb = psget()
                mm_group(bcb, slot_, 256, hrhs, hres)
                P.op("dve", lambda e, s_=s_, bcb=bcb, i_=i_: e.tensor_tensor(bigc(CV0 + i_), psf[:, bcb, :], zb[:, s_, :], ALU.mult),
                     reads=[("ps", bcb), ("z", s_)], writes=[("big", CV0 + i_)])

            prev = None
            for i in range(8):
                slot = wget("win", l, 0, MC + 384 * i, 384)
                if prev is not None:
                    emit_cb(prev)
                bcc, bcu = psget(), psget()
                mm_group(bcc, slot, 0, hrhs, hres)
                mm_group(bcu, slot, 128, hrhs, hres)
                s = ring("conv", 2)
                P.op("act", lambda e, s=s, bcc=bcc: e.activation(ccs[:, 0, :], psf[:, bcc, :], AF.Copy),
                     reads=[("ps", bcc)], writes=[("ccs", 0)])
                if first:
                    P.op("dve", lambda e, s=s: e.memset(yb[:, s, 0:2], 0.0), writes=[("ybh", s)])
                else:
                    P.op("dve", lambda e, s=s, l=l, i=i: e.tensor_copy(yb[:, s, 0:2], yh[:, l, i, :]),
                         reads=[("yh", l, i)], writes=[("ybh", s)])
                P.op("dve", lambda e, s=s, bcu=bcu: e.tensor_tensor(yb[:, s, 2:T + 2], psf[:, bcu, :], ccs[:, 0, :], ALU.mult),
                     reads=[("ps", bcu), ("ccs", 0)], writes=[("yb", s)])
                P.op("act", lambda e, s=s, l=l, i=i: e.activation(zb[:, s, :], yb[:, s, 2:T + 2], AF.Identity,
                                                                  bias=col("cb", l, i), scale=col("cw2", l, i)),
                     reads=[("yb", s)], writes=[("z", s)])
                P.op("dve", lambda e, s=s, l=l, i=i: e.scalar_tensor_tensor(zb[:, s, :], yb[:, s, 1:T + 1], col("cw1", l, i), zb[:, s, :], ALU.mult, ALU.add),
                     reads=[("yb", s), ("ybh", s), ("z", s)], writes=[("z", s)])
                P.op("dve", lambda e, s=s, l=l, i=i: e.scalar_tensor_tensor(zb[:, s, :], yb[:, s, 0:T], col("cw0", l, i), zb[:, s, :], ALU.mult, ALU.add),
                     reads=[("yb", s), ("ybh", s), ("z", s)], writes=[("z", s)])
                if not last_of_seq:
                    P.op("dve", lambda e, s=s, l=l, i=i: e.tensor_copy(yh[:, l, i, :], yb[:, s, T:T + 2]),
                         reads=[("yb", s)], writes=[("yh", l, i)])
                prev = (slot, i, s)
                yield
            emit_cb(prev)
            yield

        def gen_yconv(l):
            for jg in range(2):
                sg = wget("win", l, 0, MGC + 512 * jg, 512)
                sw = wget("wco", l, 0, 512 * jg, 512)
                for jj in range(4):
                    j = 4 * jg + jj
                    bg = psget()
                    mm_group(bg, sg, jj * 128, hrhs, hres)
                    gs = ring("gate", 2)
                    P.op("act", lambda e, bg=bg, gs=gs, l=l, j=j: e.activation(gate[:, gs, :], psf[:, bg, :], AF.Tanh,
                                                                               bias=hbias[:, l, 8 + j:9 + j], scale=0.5),
                         reads=[("ps", bg)], writes=[("gate", gs)])
                    by = psget()
                    mm_group(by, sw, jj * 128, lambda kc: bigc(CV0 + kc), lambda kc: ("big", CV0 + kc))
                    P.op("dve", lambda e, by=by, gs=gs, j=j: e.scalar_tensor_tensor(mbuf[:, j, :], gate[:, gs, :], 1.0, psf[:, by, :], ALU.add, ALU.mult),
                         reads=[("ps", by), ("gate", gs)], writes=[("m", j)])
                    yield

        def gen_attn(l, first, last_of_seq):
            units = [(blk, g) for blk in range(NBLK) for g in range(4)]

            def emit_scores(blk, g):
                p = g % 2
                cb0 = 4 * (g // 2)
                kc2 = g // 2
                kbs = [1] if (first and blk == 0) else [0, 1]
                pts = {}
                for kb in kbs:
                    slot_k = blk + kb
                    b = psget()
                    P.op("pe", lambda e, b=b, p=p, cb0=cb0, kc2=kc2, slot_k=slot_k, blk=blk, l=l: e.matmul(
                        psf[:, b, :].rearrange("p (a q) -> p a q", a=4),
                        kT[p * 64:(p + 1) * 64, l, kc2, slot_k * 128:(slot_k + 1) * 128],
                        big[p * 64:(p + 1) * 64, Q0 + cb0:Q0 + cb0 + 4, blk * 128:(blk + 1) * 128],
                        start=True, stop=True),
                        reads=[("kT", l, slot_k)] + [("big", Q0 + cb0 + a) for a in range(4)], writes=[("ps", b)], inc=True)
                    se = ring("sexp", 2)
                    P.op("act", lambda e, b=b, se=se: e.activation(sexp[:, se, :], psf[:, b, :], AF.Exp),
                         reads=[("ps", b)], writes=[("sexp", se)])
                    pt = ring("PT", 2 * (ADEPTH + 1))
                    pts[kb] = pt
                    P.op("dve", lambda e, se=se, pt=pt, kb=kb, g=g: e.tensor_tensor(
                        PT[:, pt, :].rearrange("p (a q) -> p a q", a=4),
                        sexp[:, se, :].rearrange("p (a q) -> p a q", a=4),
                        etab[:, kb, 4 * g:4 * g + 4, :], ALU.mult),
                        reads=[("sexp", se)], writes=[("PT", pt)])
                return kbs, pts

            def emit_pv(blk, g, kbs, pts, an):
                bo = psget()
                for a in range(4):
                    for ki, kb in enumerate(kbs):
                        slot_k = blk + kb
                        P.op("pe", lambda e, bo=bo, a=a, kb=kb, pt=pts[kb], slot_k=slot_k, g=g, ki=ki, nk=len(kbs), l=l: e.matmul(
                            psf[:, bo, a * 65:(a + 1) * 65], PT[:, pt, a * 128:(a + 1) * 128],
                            Vt[:, l, slot_k, g * 65:(g + 1) * 65], start=(ki == 0), stop=(ki == nk - 1)),
                            reads=[("PT", pts[kb]), ("V", l, slot_k)], writes=[("ps", bo)],
                            inc=(a == 3 and ki == len(kbs) - 1))
                ds = ring("den", 2)
                P.op("dve", lambda e, bo=bo, ds=ds, g=g, l=l: e.tensor_tensor(
                    den[:, ds, :], psf[:, bo, 0:260].rearrange("p (a d) -> p a d", a=4)[:, :, 64],
                    sexpk[:, l * 16 + 4 * g:l * 16 + 4 * g + 4], ALU.add),
                    reads=[("ps", bo)], writes=[("den", ds)])
                P.op("dve", lambda e, ds=ds: e.reciprocal(rden[:, ds, :], den[:, ds, :]),
                     reads=[("den", ds)], writes=[("rden", ds)])
                P.op("dve", lambda e, bo=bo, g=g, ds=ds, an=an: e.tensor_tensor(
                    attn_n[:, an, g * 256:(g + 1) * 256].rearrange("p (a d) -> p a d", a=4),
                    psf[:, bo, 0:260].rearrange("p (a d) -> p a d", a=4)[:, :, 0:64],
                    rden[:, ds, :].unsqueeze(2).broadcast_to([128, 4, 64]), ALU.mult),
                    reads=[("ps", bo), ("rden", ds)], writes=[("attn_n", an, g)])

            def emit_tr(blk, an):
                for j in range(8):
                    P.op("pe", lambda e, j=j, an=an: e.transpose(pst[:, j * 128:(j + 1) * 128], attn_n[:, an, j * 128:(j + 1) * 128], identb[:]),
                         reads=[("attn_n", an, g_) for g_ in range(4)], writes=[("pst",)], inc=(j == 7))
                P.op("dve", lambda e, blk=blk: e.tensor_copy(
                    big[:, AT0:AT0 + 8, blk * 128:(blk + 1) * 128], pst[:].rearrange("p (a q) -> p a q", a=8)),
                    reads=[("pst",)], writes=[("big", AT0 + j) for j in range(8)])

            ans = {}
            pend = []
            trq = []

            def do_pv(item):
                pb, pg, pk, pp = item
                emit_pv(pb, pg, pk, pp, ans[pb])
                if pg == 3:
                    trq.append([pb, TRLAG])

            def tick_tr(force=False):
                for it in list(trq):
                    it[1] -= 1
                    if it[1] <= 0 or force:
                        emit_tr(it[0], ans[it[0]])
                        trq.remove(it)

            for (blk, g) in units:
                if g == 0:
                    ans[blk] = ring("attn_n", 2)
                sc = emit_scores(blk, g)
                pend.append((blk, g, sc[0], sc[1]))
                if len(pend) > ADEPTH:
                    do_pv(pend.pop(0))
                tick_tr()
                yield
            while pend:
                do_pv(pend.pop(0))
                tick_tr()
                yield
            tick_tr(force=True)
            if not last_of_seq:
                P.op("dve", lambda e, l=l: e.tensor_copy(kT[:, l, :, 0:128], kT[:, l, :, 512:640]),
                     reads=[("kT", l, 4)], writes=[("kT", l, 0)])
                P.op("dve", lambda e, l=l: e.tensor_copy(Vt[:, l, 0, :], Vt[:, l, 4, :]),
                     reads=[("V", l, 4)], writes=[("V", l, 0)])
            yield

        def chain(*gens):
            for g_ in gens:
                for _ in g_:
                    yield

        def emit_layer(l, first, last_of_seq):
            emit_norm(lambda kc, l=l: col("g_mix", l, kc), lambda kc: bigc(H0 + kc), lambda kc: ("big", H0 + kc))
            for qi in range(2):
                slot = wget("win", l, 0, MQ + 512 * qi, 512)
                if qi == 0:
                    bs = [psget() for _ in range(4)]
                    for kc in range(KC):
                        for c in range(4):
                            P.op("pe", lambda e, b=bs[c], slot=slot, kc=kc, c=c: e.matmul(
                                psf[:, b, :], wbuf[:, slot, kc, c * 128:(c + 1) * 128], bigc(H0 + kc),
                                start=(kc == 0), stop=(kc == KC - 1)),
                                reads=[("w", slot), ("big", H0 + kc)], writes=[("ps", bs[c])], inc=(kc == KC - 1))
                    for c in range(4):
                        P.op("act", lambda e, b=bs[c], c=c: e.activation(bigc(Q0 + c), psf[:, b, :], AF.Copy, scale=0.125),
                             reads=[("ps", bs[c])], writes=[("big", Q0 + c)])
                    continue
                for c in range(4):
                    b = psget()
                    mm_group(b, slot, c * 128, hrhs, hres)
                    P.op("act", lambda e, b=b, qi=qi, c=c: e.activation(bigc(Q0 + 4 * qi + c), psf[:, b, :], AF.Copy, scale=0.125),
                         reads=[("ps", b)], writes=[("big", Q0 + 4 * qi + c)])
            slot = wget("win", l, 0, MKV, 512)
            for cc_ in range(2):
                b = psget()
                mm_group(b, slot, cc_ * 128, hrhs, hres)
                P.op("dve", lambda e, b=b, cc_=cc_, l=l: e.tensor_copy(kT[:, l, cc_, 128:640], psf[:, b, :]),
                     reads=[("ps", b)], writes=[("kT", l, 1 + s_) for s_ in range(4)])
            for blk in range(NBLK):
                b = psget()
                for kc in range(KC):
                    P.op("pe", lambda e, b=b, slot=slot, kc=kc, blk=blk: e.matmul(
                        psf[:, b, 0:256], bigc(H0 + kc)[:, blk * 128:(blk + 1) * 128], wbuf[:, slot, kc, 256:512],
                        start=(kc == 0), stop=(kc == KC - 1)),
                        reads=[("w", slot), ("big", H0 + kc)], writes=[("ps", b)], inc=(kc == KC - 1))
                P.op("dve", lambda e, b=b, blk=blk, l=l: e.tensor_copy(
                    Vt[:, l, 1 + blk, :].rearrange("p (g d) -> p g d", g=4)[:, :, 0:64],
                    psf[:, b, 0:256].rearrange("p (g d) -> p g d", g=4)),
                    reads=[("ps", b)], writes=[("V", l, 1 + blk)])
            run_interleaved(gen_attn(l, first, last_of_seq), chain(gen_conv(l, first, last_of_seq), gen_yconv(l)),
                            NBLK * 4 + ADEPTH + 1, 17, holdback=2)
            for jg in range(2):
                sg = wget("win", l, 0, MGA + 512 * jg, 512)
                sw = wget("wao", l, 0, 512 * jg, 512)
                for jj in range(4):
                    j = 4 * jg + jj
                    bg = psget()
                    mm_group(bg, sg, jj * 128, hrhs, hres)
                    gs = ring("gate", 2)
                    P.op("act", lambda e, bg=bg, gs=gs, l=l, j=j: e.activation(gate[:, gs, :], psf[:, bg, :], AF.Tanh,
                                                                               bias=hbias[:, l, j:j + 1], scale=0.5),
                         reads=[("ps", bg)], writes=[("gate", gs)])
                    by = psget()
                    mm_group(by, sw, jj * 128, lambda kc: bigc(AT0 + kc), lambda kc: ("big", AT0 + kc))
                    ts = ring("tmpb", 2)
                    P.op("dve", lambda e, by=by, gs=gs, ts=ts: e.scalar_tensor_tensor(tmpb[:, ts, :], gate[:, gs, :], 1.0, psf[:, by, :], ALU.add, ALU.mult),
                         reads=[("ps", by), ("gate", gs)], writes=[("tmpb", ts)])
                    P.op("dve", lambda e, ts=ts, j=j: e.tensor_tensor(bigc(CV0 + j), tmpb[:, ts, :], mbuf[:, j, :], ALU.add),
                         reads=[("tmpb", ts), ("m", j)], writes=[("big", CV0 + j)])
            for og in range(2):
                slot = wget("wo", l, 0, 512 * og, 512)
                for jj in range(4):
                    j = 4 * og + jj
                    b = psget()
                    mm_group(b, slot, jj * 128, lambda kc: bigc(CV0 + kc), lambda kc: ("big", CV0 + kc))
                    P.op("dve", lambda e, b=b, j=j: e.scalar_tensor_tensor(xT[:, j, :], psf[:, b, :], 0.5, xT[:, j, :], ALU.mult, ALU.add),
                         reads=[("ps", b), ("xT", j)], writes=[("xT", j)])
                    P.op("act", lambda e, j=j, l=l: e.activation(a2[:, j, :], xT[:, j, :], AF.Copy, scale=col("g_mlp", l, j)),
                         reads=[("xT", j)], writes=[("a2", j)])
            for fg in range(8):
                slot = wget("wup", l, 0, 512 * fg, 512)
                for ff in range(4):
                    f = 4 * fg + ff
                    b = psget()
                    mm_group(b, slot, ff * 128, lambda kc: a2[:, kc, :], lambda kc: ("a2", kc))
                    rs = ring("relu", 2)
                    P.op("act", lambda e, b=b, rs=rs: e.activation(relu[:, rs, :], psf[:, b, :], AF.Relu),
                         reads=[("ps", b)], writes=[("relu", rs)])
                    P.op("dve", lambda e, rs=rs, f=f: e.tensor_tensor(bigc(f), relu[:, rs, :], relu[:, rs, :], ALU.mult),
                         reads=[("relu", rs)], writes=[("big", f)])
                if fg == 0:
                    b = psget()
                    for kc in range(KC):
                        s = emit_square(kc)
                        P.op("pe", lambda e, kc=kc, s=s, b=b: e.matmul(psf[:, b, :], ones[:], xsq[:, s, :], start=(kc == 0), stop=(kc == KC - 1)),
                             reads=[("xsq", s)], writes=[("ps", b)], inc=True)
                    P.op("act", lambda e, b=b: e.activation(mse[:], psf[:, b, :], AF.Ln, bias=epsc[:, 0:1], scale=1.0 / D),
                         reads=[("ps", b)], writes=[("mse",)])
                    P.op("act", lambda e: e.activation(rstd[:], mse[:], AF.Exp, scale=-1.0),
                         reads=[("mse",)], writes=[("rstd",)])
            for og in range(2):
                bs = [psget() for _ in range(4)]
                for kq in range(4):
                    slot = wget("wdn", l, 1024 * kq, 512 * og, 512)
                    for jj in range(4):
                        for kc in range(KC):
                            f = kq * 8 + kc
                            P.op("pe", lambda e, b=bs[jj], slot=slot, jj=jj, kc=kc, f=f, kq=kq: e.matmul(
                                psf[:, b, :], wbuf[:, slot, kc, jj * 128:(jj + 1) * 128], bigc(f),
                                start=(kq == 0 and kc == 0), stop=(kq == 3 and kc == KC - 1)),
                                reads=[("w", slot), ("big", f)], writes=[("ps", bs[jj])], inc=(kc == KC - 1))
                for jj in range(4):
                    j = 4 * og + jj
                    ts = ring("tmpb", 2)
                    P.op("dve", lambda e, b=bs[jj], ts=ts: e.tensor_tensor(tmpb[:, ts, :], psf[:, b, :], rstd[:], ALU.mult),
                         reads=[("ps", bs[jj]), ("rstd",)], writes=[("tmpb", ts)])
                    P.op("dve", lambda e, ts=ts, j=j: e.tensor_tensor(xT[:, j, :], tmpb[:, ts, :], xT[:, j, :], ALU.add),
                         reads=[("tmpb", ts), ("xT", j)], writes=[("xT", j)])

        for l in range(L):
            emit_layer(l, first, last_of_seq)
        dsrc = None
        if t + 1 < ntiles:
            for blk in range(4):
                xslots[(t + 1, blk)] = emit_xload(t + 1, blk)
        b = psget()
        for kc in range(KC):
            s = emit_square(kc)
            P.op("pe", lambda e, kc=kc, s=s, b=b: e.matmul(psf[:, b, :], ones[:], xsq[:, s, :], start=(kc == 0), stop=(kc == KC - 1)),
                 reads=[("xsq", s)], writes=[("ps", b)], inc=True)
        for kc in range(KC):
            P.op("act", lambda e, kc=kc: e.activation(mbuf[:, kc, :], xT[:, kc, :], AF.Copy, scale=gfin(kc)),
                 reads=[("xT", kc)], writes=[("m", kc)])
        P.op("act", lambda e, b=b: e.activation(mse[:], psf[:, b, :], AF.Ln, bias=epsc[:, 0:1], scale=1.0 / D),
             reads=[("ps", b)], writes=[("mse",)])
        P.op("act", lambda e: e.activation(rstd[:], mse[:], AF.Exp, scale=-0.5),
             reads=[("mse",)], writes=[("rstd",)])
        for kc in range(KC):
            P.op("dve", lambda e, kc=kc: e.tensor_tensor(mbuf[:, kc, :], mbuf[:, kc, :], rstd[:], ALU.mult),
                 reads=[("m", kc), ("rstd",)], writes=[("m", kc)])
        for blk in range(NBLK):
            pending_out.append((t, blk))

    emit_out_blocks()
    emit_weight_dmas()
    P.final_wait("sp", [(k, v) for k, v in P.cnt.items() if isinstance(k, tuple) and k[0] == "xo"])

    sems = {}
    for k in P.sem_keys():
        nm = "s_" + ("_".join(str(z) for z in k) if isinstance(k, tuple) else str(k))
        sems[k] = nc.alloc_semaphore(nm)

    def runner(name):
        def body(e):
            for waits, fn, inc in P.q[name]:
                for s, v in waits:
                    e.wait_ge(sems[s], v)
                if fn is None:
                    continue
                ins = fn(e)
                if inc is not None:
                    ins.then_inc(sems[inc[0]], inc[1])
        return body

    with nc.Block() as block:
        block.tensor(runner("pe"))
        block.scalar(runner("act"))
        block.vector(runner("dve"))
        block.gpsimd(runner("pool"))
        block.sync(runner("sp"))
    return nc, P


def host_consts(depth):
    slopes = np.power(np.float32(2.0), -8.0 * np.arange(1, NQH + 1, dtype=np.float32) / NQH).astype(np.float32)
    j = np.arange(128)[:, None]
    r = np.arange(128)[None, :]
    et = np.zeros((128, 2, NQH, 128), dtype=np.float32)
    for h in range(NQH):
        dprev = (128 + r - j).astype(np.float32)
        dcur = (r - j).astype(np.float32)
        et[:, 0, h, :] = np.where(j > r, np.exp(-slopes[h] * np.maximum(dprev, 0.0)), 0.0)
        et[:, 1, h, :] = np.where(j <= r, np.exp(-slopes[h] * np.maximum(dcur, 0.0)), 0.0)
    identf = np.eye(128, dtype=np.float32)
    identb = np.eye(128, dtype=np.float32).astype(ml_dtypes.bfloat16)
    return et.reshape(128, -1), identf, identb


def prep_weights(inputs, depth):
    L = depth
    f = lambda a: np.ascontiguousarray(np.asarray(a, dtype=np.float32))
    perm = _win_perm()
    w_in = f(np.asarray(inputs["w_in"])[:L][:, :, perm])

    def cols(v):
        return np.asarray(v, dtype=np.float32).reshape(8, 128).T

    pv = np.zeros((128, L * 64 + 8), dtype=np.float32)
    for l in range(L):
        o = l * 64
        pv[:, o + 0:o + 8] = cols(inputs["g_mix"][l])
        pv[:, o + 8:o + 16] = cols(inputs["g_mlp"][l])
        pv[:, o + 16:o + 24] = cols(np.asarray(inputs["b_gates"])[l, :D])
        pv[:, o + 24:o + 32] = cols(np.asarray(inputs["b_gates"])[l, D:])
        pv[:, o + 32:o + 40] = cols(np.asarray(inputs["conv_w"])[l, 0])
        pv[:, o + 40:o + 48] = cols(np.asarray(inputs["conv_w"])[l, 1])
        pv[:, o + 48:o + 56] = cols(np.asarray(inputs["conv_w"])[l, 2])
        pv[:, o + 56:o + 64] = cols(inputs["conv_b"][l])
    pv[:, L * 64:L * 64 + 8] = cols(inputs["g_final"])
    sinkb = np.ascontiguousarray(np.broadcast_to(np.asarray(inputs["sinks"], dtype=np.float32)[:L].reshape(1, L * 16), (128, L * 16)))
    et, identf, identb = host_consts(L)
    shared = {
        "w_in": w_in,
        "w_ao": f(np.asarray(inputs["w_attn_out"])[:L]),
        "w_co": f(np.asarray(inputs["w_conv_out"])[:L]),
        "w_o": f(np.asarray(inputs["w_o"])[:L]),
        "w_up": f(np.asarray(inputs["w_up"])[:L]),
        "w_dn": f(np.asarray(inputs["w_down"])[:L]),
        "pvec": pv, "sinkb": sinkb, "etab": et, "identf": identf, "identb": identb,
    }
    return shared


_CACHE = {}


def run(inputs, depth=L_FULL, seq=SEQ, dbg=None):
    x = np.asarray(inputs["x"], dtype=np.float32)
    B, S, _ = x.shape
    assert S == seq and B % N_CORES == 0
    bpc = B // N_CORES
    tiles_per_seq = S // T
    ntiles = bpc * tiles_per_seq
    key = (ntiles, depth, tiles_per_seq, dbg)
    if key not in _CACHE:
        _CACHE[key] = build_program(ntiles, depth, tiles_per_seq, dbg)[0]
    nc = _CACHE[key]
    shared = prep_weights(inputs, depth)
    in_maps = []
    for c in range(N_CORES):
        m = dict(shared)
        m["x"] = np.ascontiguousarray(x[c * bpc:(c + 1) * bpc].reshape(bpc * S, D))
        in_maps.append(m)
    res = run_bass_kernel_spmd(nc, in_maps, core_ids=list(range(N_CORES)))
    outs = [np.asarray(r["out"], dtype=np.float32).reshape(bpc, S, D) for r in res.results]
    return np.concatenate(outs, axis=0)


def kernel(**inputs):
    return run(inputs)
```

```python
import numpy as np
import ml_dtypes
import concourse.bass as bass
import concourse.mybir as mybir
from concourse.bass_utils import run_bass_kernel_spmd

F32 = mybir.dt.float32
BF16 = mybir.dt.bfloat16
AF = mybir.ActivationFunctionType
ALU = mybir.AluOpType

D = 1024
KC = 8
T = 512
NBLK = 4
L_FULL = 2
SEQ = 2048
N_CORES = 8
NQH = 16
HD = 64
EPS = 1e-6
NWSLOT = 5
NPSB = 7
ADEPTH = 2
TRLAG = 2

OQ, OK_, OV, OCB, OCC, OCU, OGA, OGC = 0, 1024, 1280, 1536, 2560, 3584, 4608, 5632
HE = [0, 1, 2, 3, 8, 9, 10, 11]
HO = [4, 5, 6, 7, 12, 13, 14, 15]


def _win_perm():
    cols = []
    for c in range(8):
        cols += list(range(OQ + HE[c] * 64, OQ + HE[c] * 64 + 64))
        cols += list(range(OQ + HO[c] * 64, OQ + HO[c] * 64 + 64))
    cols += list(range(OK_, OK_ + 256))
    cols += list(range(OV, OV + 256))
    for i in range(8):
        cols += list(range(OCC + 128 * i, OCC + 128 * i + 128))
        cols += list(range(OCU + 128 * i, OCU + 128 * i + 128))
        cols += list(range(OCB + 128 * i, OCB + 128 * i + 128))
    cols += list(range(OGC, OGC + 1024))
    cols += list(range(OGA, OGA + 1024))
    return np.asarray(cols, dtype=np.int64)


MQ, MKV, MC, MGC, MGA = 0, 1024, 1536, 4608, 5632


def _tile_catalogue():
    t512 = [("win", 0, MQ, 512), ("win", 0, MQ + 512, 512), ("win", 0, MKV, 512)]
    for jg in range(2):
        t512 += [("win", 0, MGC + 512 * jg, 512), ("wco", 0, 512 * jg, 512)]
    for jg in range(2):
        t512 += [("win", 0, MGA + 512 * jg, 512), ("wao", 0, 512 * jg, 512)]
    t512 += [("wo", 0, 512 * og, 512) for og in range(2)]
    t512 += [("wup", 0, 512 * fg, 512) for fg in range(8)]
    t512 += [("wdn", 1024 * kq, 512 * og, 512) for og in range(2) for kq in range(4)]
    t384 = [("win", 0, MC + 384 * i, 384) for i in range(8)]
    return t512, t384


T512, T384 = _tile_catalogue()
TIDX = {k: (0, i) for i, k in enumerate(T512)}
TIDX.update({k: (1, i) for i, k in enumerate(T384)})


class Prog:
    ENG = ["pe", "act", "dve", "pool", "sp"]

    def __init__(self):
        self.q = {e: [] for e in self.ENG}
        self.cnt = {}
        self.seen = {e: {} for e in self.ENG}
        self.lastw = {}
        self.readers = {}

    def op(self, eng, fn, reads=(), writes=(), inc=True, dma=None):
        deps = {}

        def add(t):
            if t is None:
                return
            s, v = t
            if v > deps.get(s, 0):
                deps[s] = v

        for r in reads:
            add(self.lastw.get(r))
        for w in writes:
            add(self.lastw.get(w))
            rd = self.readers.get(w)
            if rd:
                for s, v in rd.items():
                    add((s, v))
        waits = []
        seen = self.seen[eng]
        for s, v in deps.items():
            if s == "pe" and eng == "pe":
                continue
            if v > seen.get(s, 0):
                waits.append((s, v))
                seen[s] = v
        if dma is not None:
            self.cnt[dma] = self.cnt.get(dma, 0) + 16
            tok = (dma, self.cnt[dma])
            incspec = (dma, 16)
        elif inc:
            self.cnt[eng] = self.cnt.get(eng, 0) + 1
            tok = (eng, self.cnt[eng])
            incspec = (eng, 1)
        else:
            tok = (eng, self.cnt.get(eng, 0) + 1)
            incspec = None
        self.q[eng].append((waits, fn, incspec))
        for r in reads:
            rd = self.readers.setdefault(r, {})
            if tok[1] > rd.get(tok[0], 0):
                rd[tok[0]] = tok[1]
        for w in writes:
            self.lastw[w] = tok
            self.readers[w] = {}
        return tok

    def final_wait(self, eng, toks):
        waits = []
        for s, v in toks:
            if v > self.seen[eng].get(s, 0):
                waits.append((s, v))
                self.seen[eng][s] = v
        self.q[eng].append((waits, None, None))

    def sem_keys(self):
        return list(self.cnt.keys())


def build_program(ntiles, depth, tiles_per_seq=4, dbg=None):
    nc = bass.Bass("TRN2", target_bir_lowering=False)
    L = depth
    ntok = ntiles * T
    dt = nc.dram_tensor
    x_d = dt("x", [ntok, D], F32, kind="ExternalInput").ap()
    out_d = dt("out", [ntok, D], F32, kind="ExternalOutput").ap()
    win_d = dt("w_in", [L, D, 6656], F32, kind="ExternalInput").ap()
    wao_d = dt("w_ao", [L, D, D], F32, kind="ExternalInput").ap()
    wco_d = dt("w_co", [L, D, D], F32, kind="ExternalInput").ap()
    wo_d = dt("w_o", [L, D, D], F32, kind="ExternalInput").ap()
    wup_d = dt("w_up", [L, D, 4 * D], F32, kind="ExternalInput").ap()
    wdn_d = dt("w_dn", [L, 4 * D, D], F32, kind="ExternalInput").ap()
    NPV = L * 64 + 8
    pvec_d = dt("pvec", [128, NPV], F32, kind="ExternalInput").ap()
    sink_d = dt("sinkb", [128, L * 16], F32, kind="ExternalInput").ap()
    etab_d = dt("etab", [128, 2 * 16 * 128], F32, kind="ExternalInput").ap()
    idf_d = dt("identf", [128, 128], F32, kind="ExternalInput").ap()
    idb_d = dt("identb", [128, 128], BF16, kind="ExternalInput").ap()
    scr512 = dt("wscr512", [L, len(T512), 128, KC * 512], BF16, kind="Internal").ap()
    scr384 = dt("wscr384", [L, len(T384), 128, KC * 384], BF16, kind="Internal").ap()

    sb = nc.alloc_sbuf_tensor
    big = sb("big", [128, 32, T], BF16)
    xT = sb("xT", [128, KC, T], F32)
    mbuf = sb("mbuf", [128, KC, T], F32)
    a2 = sb("a2", [128, KC, T], BF16)
    xin = sb("xin", [128, 4, D], F32)
    xout = sb("xout", [128, 2, D], F32)
    xsq = sb("xsq", [128, 3, T], BF16)
    rstd = sb("rstd", [128, T], F32)
    mse = sb("mse", [128, T], F32)
    epsc = sb("epsc", [128, 1], F32)
    kT = sb("kT", [128, L, 2, 5 * 128], BF16)
    Vt = sb("Vt", [128, L, 5, 4 * 65], BF16)
    ccs = sb("ccs", [128, 1, T], F32)
    yb = sb("yb", [128, 2, T + 2], F32)
    zb = sb("zb", [128, 2, T], F32)
    yh = sb("yh", [128, L, 8, 2], F32)
    sexp = sb("sexp", [128, 2, T], F32)
    PT = sb("PT", [128, 2 * (ADEPTH + 1), T], BF16)
    attn_n = sb("attn_n", [128, 2, D], BF16)
    den = sb("den", [128, 2, 4], F32)
    rden = sb("rden", [128, 2, 4], F32)
    gate = sb("gate", [128, 2, T], F32)
    tmpb = sb("tmpb", [128, 2, T], F32)
    relu = sb("relu", [128, 2, T], F32)
    wbuf = sb("wbuf", [128, NWSLOT, KC, 512], BF16)
    etab = sb("etab_s", [128, 2, 16, 128], F32)
    pvec = sb("pvec_s", [128, NPV], F32)
    hbias = sb("hbias", [128, L, 16], F32)
    sexpk = sb("sinkexp", [128, L * 16], F32)
    identf = sb("identf_s", [128, 128], F32)
    identb = sb("identb_s", [128, 128], BF16)
    ones = sb("ones", [128, 128], BF16)

    psf = nc.alloc_psum_tensor("psf", [128, NPSB, 512], F32)
    pst = nc.alloc_psum_tensor("pst", [128, 1024], BF16)

    P = Prog()
    P.sbuf_left = nc.sbuf_bytes_remaining

    def col(name, l, c):
        base = {"g_mix": 0, "g_mlp": 8, "b_ga": 16, "b_gc": 24, "cw0": 32, "cw1": 40, "cw2": 48, "cb": 56}[name]
        o = l * 64 + base + c
        return pvec[:, o:o + 1]

    def gfin(c):
        o = L * 64 + c
        return pvec[:, o:o + 1]

    P.op("sp", lambda e: e.dma_start(out=pvec[:], in_=pvec_d), writes=[("c", 0)], dma="cst")
    P.op("sp", lambda e: e.dma_start(out=sexpk[:], in_=sink_d), writes=[("c", 1)], dma="cst")
    P.op("sp", lambda e: e.dma_start(out=etab[:].rearrange("p a h q -> p (a h q)"), in_=etab_d), writes=[("c", 2)], dma="cst")
    P.op("sp", lambda e: e.dma_start(out=identf[:], in_=idf_d), writes=[("c", 3)], dma="cst")
    P.op("sp", lambda e: e.dma_start(out=identb[:], in_=idb_d), writes=[("c", 4)], dma="cst")
    CST = [("c", i) for i in range(5)]
    P.op("dve", lambda e: e.memset(ones[:], 1.0), reads=CST, writes=[("c", 5)])
    P.op("dve", lambda e: e.memset(epsc[:], EPS), writes=[("c", 6)])
    P.op("dve", lambda e: e.memset(Vt[:], 1.0), writes=[("V", l, s) for l in range(L) for s in range(5)])
    for l in range(L):
        P.op("dve", lambda e, l=l: e.tensor_scalar(hbias[:, l, :], pvec[:, l * 64 + 16:l * 64 + 32], 0.5, None, ALU.mult),
             reads=CST, writes=[("c", 7 + l)])
    P.op("act", lambda e: e.activation(sexpk[:], sexpk[:], AF.Exp), reads=CST, writes=[("c", 1)])
    ALLC = [("c", i) for i in range(7 + L)]
    P.op("pe", lambda e: e.nop(), reads=ALLC, inc=False)
    P.op("act", lambda e: e.nop(), reads=ALLC)
    P.op("dve", lambda e: e.nop(), reads=ALLC)
    P.op("pool", lambda e: e.nop(), reads=ALLC)

    def wsrc(kind, l, r0, c0, n):
        dten = {"win": win_d, "wao": wao_d, "wco": wco_d, "wo": wo_d, "wup": wup_d, "wdn": wdn_d}[kind]
        return dten[l, r0:r0 + 1024, c0:c0 + n].rearrange("(kc p) n -> p kc n", p=128)

    wtiles = []
    cur = {"t": 0}

    def wget(kind, l, r0, c0, ncol):
        i = len(wtiles)
        slot = i % NWSLOT
        res = ("w", slot)
        rd = dict(P.readers.get(res, {}))
        wtiles.append(((kind, l, r0, c0, ncol), slot, rd, cur["t"]))
        fill = i // NWSLOT + 1
        P.lastw[res] = (res, 32 * fill)
        P.readers[res] = {}
        P.cnt[res] = 32 * fill
        return slot

    def emit_weight_dmas():
        seen = P.seen["pool"]
        q = P.q["pool"]
        st_cnt = {}
        pend_st = []

        def need(s, v, waits):
            if v > seen.get(s, 0):
                waits.append((s, v))
                seen[s] = v

        def flush_store(k):
            while len(pend_st) > k:
                slot_, scr_, ncol_, fill_ = pend_st.pop(0)
                waits = []
                need(("w", slot_), 32 * fill_, waits)
                st_cnt[slot_] = st_cnt.get(slot_, 0) + 16
                P.cnt[("ws", slot_)] = st_cnt[slot_]
                q.append((waits, (lambda e, slot_=slot_, scr_=scr_, ncol_=ncol_: e.dma_start(
                    out=scr_, in_=wbuf[:, slot_, :, 0:ncol_])), (("ws", slot_), 16)))

        started_bf16 = False
        for i, ((kind, l, r0, c0, ncol), slot, rd, t_) in enumerate(wtiles):
            arr, idx = TIDX[(kind, r0, c0, ncol)]
            scr = (scr512 if arr == 0 else scr384)[l, idx].rearrange("p (kc n) -> p kc n", kc=KC)
            fill = i // NWSLOT + 1
            waits = []
            for s, v in rd.items():
                need(s, v, waits)
            if t_ == 0:
                if st_cnt.get(slot, 0):
                    need(("ws", slot), st_cnt[slot], waits)
                src_ap = wsrc(kind, l, r0, c0, ncol)
                for h in range(2):
                    fn = (lambda e, slot=slot, src_ap=src_ap, ncol=ncol, h=h: e.dma_start(
                        out=wbuf[:, slot, 4 * h:4 * h + 4, 0:ncol], in_=src_ap[:, 4 * h:4 * h + 4, :]))
                    q.append((waits if h == 0 else [], fn, (("w", slot), 16)))
                pend_st.append((slot, scr, ncol, fill))
                flush_store(2)
            else:
                if not started_bf16:
                    flush_store(0)
                    for s_, v_ in st_cnt.items():
                        need(("ws", s_), v_, waits)
                    started_bf16 = True
                fn = (lambda e, slot=slot, scr=scr, ncol=ncol: e.dma_start(out=wbuf[:, slot, :, 0:ncol], in_=scr))
                q.append((waits, fn, (("w", slot), 32)))
        flush_store(0)

    psstate = {"n": 0}

    def psget():
        b = psstate["n"] % NPSB
        psstate["n"] += 1
        return b

    rr = {}

    def ring(name, n):
        v = rr.get(name, 0)
        rr[name] = v + 1
        return v % n

    def mm_group(b, slot, c0, rhs_fn, rhs_res, ncol=128):
        for kc in range(KC):
            P.op("pe",
                 lambda e, b=b, slot=slot, c0=c0, kc=kc: e.matmul(
                     psf[0:ncol, b, :], wbuf[:, slot, kc, c0:c0 + ncol], rhs_fn(kc), start=(kc == 0), stop=(kc == KC - 1)),
                 reads=[("w", slot), rhs_res(kc)], writes=[("ps", b)], inc=(kc == KC - 1))

    def bigc(i):
        return big[:, i, :]

    H0, Q0, CV0, AT0 = 0, 8, 16, 24

    pending_out = []

    def emit_out_blocks():
        while pending_out:
            t_, blk = pending_out.pop(0)
            s = ring("xout", 2)
            for half in range(2):
                b = psget()
                for j in range(4):
                    cidx = half * 4 + j
                    P.op("pe", lambda e, b=b, j=j, cidx=cidx, blk=blk: e.transpose(
                        psf[:, b, j * 128:(j + 1) * 128], mbuf[:, cidx, blk * 128:(blk + 1) * 128], identf[:]),
                        reads=[("m", cidx)], writes=[("ps", b)], inc=(j == 3))
                if half == 0:
                    P.op("act", lambda e, b=b, s=s, half=half: e.activation(xout[:, s, half * 512:(half + 1) * 512], psf[:, b, :], AF.Copy),
                         reads=[("ps", b)], writes=[("xout", s)])
                else:
                    P.op("dve", lambda e, b=b, s=s, half=half: e.tensor_copy(xout[:, s, half * 512:(half + 1) * 512], psf[:, b, :]),
                         reads=[("ps", b)], writes=[("xout", s)])
            r0 = (t_ * NBLK + blk) * 128
            P.op("sp", lambda e, s=s, r0=r0: e.dma_start(out=out_d[r0:r0 + 128, :], in_=xout[:, s, :]),
                 reads=[("xout", s)], dma=("xo", s))

    def emit_square(kc):
        s = ring("xsq", 3)
        if kc % 2 == 0:
            P.op("act", lambda e, kc=kc, s=s: e.activation(xsq[:, s, :], xT[:, kc, :], AF.Square),
                 reads=[("xT", kc)], writes=[("xsq", s)])
        else:
            P.op("dve", lambda e, kc=kc, s=s: e.tensor_tensor(xsq[:, s, :], xT[:, kc, :], xT[:, kc, :], ALU.mult),
                 reads=[("xT", kc)], writes=[("xsq", s)])
        return s

    def emit_norm(gcol, dst_ap, dst_res):
        b = psget()
        for kc in range(KC):
            s = emit_square(kc)
            P.op("pe", lambda e, kc=kc, s=s, b=b: e.matmul(psf[:, b, :], ones[:], xsq[:, s, :], start=(kc == 0), stop=(kc == KC - 1)),
                 reads=[("xsq", s)], writes=[("ps", b)], inc=True)
        P.op("act", lambda e, b=b: e.activation(mse[:], psf[:, b, :], AF.Ln, bias=epsc[:, 0:1], scale=1.0 / D),
             reads=[("ps", b)], writes=[("mse",)])
        P.op("act", lambda e: e.activation(rstd[:], mse[:], AF.Exp, scale=-0.5),
             reads=[("mse",)], writes=[("rstd",)])
        for kc in range(KC):
            P.op("dve", lambda e, kc=kc: e.scalar_tensor_tensor(dst_ap(kc), xT[:, kc, :], gcol(kc), rstd[:], ALU.mult, ALU.mult),
                 reads=[("xT", kc), ("rstd",)], writes=[dst_res(kc)])
        emit_out_blocks()

    def emit_xload(t, blk):
        s = ring("xin", 4)
        r0 = (t * NBLK + blk) * 128
        P.op("sp", lambda e, s=s, r0=r0: e.dma_start(out=xin[:, s, :], in_=x_d[r0:r0 + 128, :]),
             writes=[("xin", s)], dma=("xin", s))
        return s

    xslots = {}

    def run_interleaved(primary, filler, nprimary, nfiller, holdback=2):
        spread = nfiller - holdback
        done_f = 0
        for i in range(nprimary):
            next(primary, None)
            want = ((i + 1) * spread) // nprimary
            while done_f < want:
                next(filler, None)
                done_f += 1
        for _ in filler:
            pass
        for _ in primary:
            pass

    for t in range(ntiles):
        cur["t"] = t
        first = (t % tiles_per_seq == 0)
        last_of_seq = (t % tiles_per_seq == tiles_per_seq - 1)
        for blk in range(NBLK):
            if (t, blk) not in xslots:
                xslots[(t, blk)] = emit_xload(t, blk)
            s = xslots[(t, blk)]
            for half in range(2):
                b = psget()
                for j in range(4):
                    cidx = half * 4 + j
                    P.op("pe", lambda e, b=b, j=j, s=s, cidx=cidx: e.transpose(
                        psf[:, b, j * 128:(j + 1) * 128], xin[:, s, cidx * 128:(cidx + 1) * 128], identf[:]),
                        reads=[("xin", s)], writes=[("ps", b)], inc=(j == 3))
                eng = "act" if half == 0 else "dve"
                if eng == "act":
                    fn = lambda e, b=b, half=half, blk=blk: e.activation(
                        xT[:, half * 4:half * 4 + 4, blk * 128:(blk + 1) * 128],
                        psf[:, b, :].rearrange("p (a q) -> p a q", a=4), AF.Copy)
                else:
                    fn = lambda e, b=b, half=half, blk=blk: e.tensor_copy(
                        xT[:, half * 4:half * 4 + 4, blk * 128:(blk + 1) * 128],
                        psf[:, b, :].rearrange("p (a q) -> p a q", a=4))
                P.op(eng, fn, reads=[("ps", b)], writes=[("xT", half * 4 + j) for j in range(4)])

        hrhs = lambda kc: bigc(H0 + kc)
        hres = lambda kc: ("big", H0 + kc)

        def gen_conv(l, first, last_of_seq):
            def emit_cb(prev):
                slot_, i_, s_ = prev
                bcb = psget()
                mm_group(bcb, slot_, 256, hrhs, hres)
                P.op("dve", lambda e, s_=s_, bcb=bcb, i_=i_: e.tensor_tensor(bigc(CV0 + i_), psf[:, bcb, :], zb[:, s_, :], ALU.mult),
                     reads=[("ps", bcb), ("z", s_)], writes=[("big", CV0 + i_)])

            prev = None
            for i in range(8):
                slot = wget("win", l, 0, MC + 384 * i, 384)
                if prev is not None:
                    emit_cb(prev)
                bcc, bcu = psget(), psget()
                mm_group(bcc, slot, 0, hrhs, hres)
                mm_group(bcu, slot, 128, hrhs, hres)
                s = ring("conv", 2)
                P.op("act", lambda e, s=s, bcc=bcc: e.activation(ccs[:, 0, :], psf[:, bcc, :], AF.Copy),
                     reads=[("ps", bcc)], writes=[("ccs", 0)])
                if first:
                    P.op("dve", lambda e, s=s: e.memset(yb[:, s, 0:2], 0.0), writes=[("ybh", s)])
                else:
                    P.op("dve", lambda e, s=s, l=l, i=i: e.tensor_copy(yb[:, s, 0:2], yh[:, l, i, :]),
                         reads=[("yh", l, i)], writes=[("ybh", s)])
                P.op("dve", lambda e, s=s, bcu=bcu: e.tensor_tensor(yb[:, s, 2:T + 2], psf[:, bcu, :], ccs[:, 0, :], ALU.mult),
                     reads=[("ps", bcu), ("ccs", 0)], writes=[("yb", s)])
                P.op("act", lambda e, s=s, l=l, i=i: e.activation(zb[:, s, :], yb[:, s, 2:T + 2], AF.Identity,
                                                                  bias=col("cb", l, i), scale=col("cw2", l, i)),
                     reads=[("yb", s)], writes=[("z", s)])
                P.op("dve", lambda e, s=s, l=l, i=i: e.scalar_tensor_tensor(zb[:, s, :], yb[:, s, 1:T + 1], col("cw1", l, i), zb[:, s, :], ALU.mult, ALU.add),
                     reads=[("yb", s), ("ybh", s), ("z", s)], writes=[("z", s)])
                P.op("dve", lambda e, s=s, l=l, i=i: e.scalar_tensor_tensor(zb[:, s, :], yb[:, s, 0:T], col("cw0", l, i), zb[:, s, :], ALU.mult, ALU.add),
                     reads=[("yb", s), ("ybh", s), ("z", s)], writes=[("z", s)])
                if not last_of_seq:
                    P.op("dve", lambda e, s=s, l=l, i=i: e.tensor_copy(yh[:, l, i, :], yb[:, s, T:T + 2]),
                         reads=[("yb", s)], writes=[("yh", l, i)])
                prev = (slot, i, s)
                yield
            emit_cb(prev)
            yield

        def gen_yconv(l):
            for jg in range(2):
                sg = wget("win", l, 0, MGC + 512 * jg, 512)
                sw = wget("wco", l, 0, 512 * jg, 512)
                for jj in range(4):
                    j = 4 * jg + jj
                    bg = psget()
                    mm_group(bg, sg, jj * 128, hrhs, hres)
                    gs = ring("gate", 2)
                    P.op("act", lambda e, bg=bg, gs=gs, l=l, j=j: e.activation(gate[:, gs, :], psf[:, bg, :], AF.Tanh,
                                                                               bias=hbias[:, l, 8 + j:9 + j], scale=0.5),
                         reads=[("ps", bg)], writes=[("gate", gs)])
                    by = psget()
                    mm_group(by, sw, jj * 128, lambda kc: bigc(CV0 + kc), lambda kc: ("big", CV0 + kc))
                    P.op("dve", lambda e, by=by, gs=gs, j=j: e.scalar_tensor_tensor(mbuf[:, j, :], gate[:, gs, :], 1.0, psf[:, by, :], ALU.add, ALU.mult),
                         reads=[("ps", by), ("gate", gs)], writes=[("m", j)])
                    yield

        def gen_attn(l, first, last_of_seq):
            units = [(blk, g) for blk in range(NBLK) for g in range(4)]

            def emit_scores(blk, g):
                p = g % 2
                cb0 = 4 * (g // 2)
                kc2 = g // 2
                kbs = [1] if (first and blk == 0) else [0, 1]
                pts = {}
                for kb in kbs:
                    slot_k = blk + kb
                    b = psget()
                    P.op("pe", lambda e, b=b, p=p, cb0=cb0, kc2=kc2, slot_k=slot_k, blk=blk, l=l: e.matmul(
                        psf[:, b, :].rearrange("p (a q) -> p a q", a=4),
                        kT[p * 64:(p + 1) * 64, l, kc2, slot_k * 128:(slot_k + 1) * 128],
                        big[p * 64:(p + 1) * 64, Q0 + cb0:Q0 + cb0 + 4, blk * 128:(blk + 1) * 128],
                        start=True, stop=True),
                        reads=[("kT", l, slot_k)] + [("big", Q0 + cb0 + a) for a in range(4)], writes=[("ps", b)], inc=True)
                    se = ring("sexp", 2)
                    P.op("act", lambda e, b=b, se=se: e.activation(sexp[:, se, :], psf[:, b, :], AF.Exp),
                         reads=[("ps", b)], writes=[("sexp", se)])
                    pt = ring("PT", 2 * (ADEPTH + 1))
                    pts[kb] = pt
                    P.op("dve", lambda e, se=se, pt=pt, kb=kb, g=g: e.tensor_tensor(
                        PT[:, pt, :].rearrange("p (a q) -> p a q", a=4),
                        sexp[:, se, :].rearrange("p (a q) -> p a q", a=4),
                        etab[:, kb, 4 * g:4 * g + 4, :], ALU.mult),
                        reads=[("sexp", se)], writes=[("PT", pt)])
                return kbs, pts

            def emit_pv(blk, g, kbs, pts, an):
                bo = psget()
                for a in range(4):
                    for ki, kb in enumerate(kbs):
                        slot_k = blk + kb
                        P.op("pe", lambda e, bo=bo, a=a, kb=kb, pt=pts[kb], slot_k=slot_k, g=g, ki=ki, nk=len(kbs), l=l: e.matmul(
                            psf[:, bo, a * 65:(a + 1) * 65], PT[:, pt, a * 128:(a + 1) * 128],
                            Vt[:, l, slot_k, g * 65:(g + 1) * 65], start=(ki == 0), stop=(ki == nk - 1)),
                            reads=[("PT", pts[kb]), ("V", l, slot_k)], writes=[("ps", bo)],
                            inc=(a == 3 and ki == len(kbs) - 1))
                ds = ring("den", 2)
                P.op("dve", lambda e, bo=bo, ds=ds, g=g, l=l: e.tensor_tensor(
                    den[:, ds, :], psf[:, bo, 0:260].rearrange("p (a d) -> p a d", a=4)[:, :, 64],
                    sexpk[:, l * 16 + 4 * g:l * 16 + 4 * g + 4], ALU.add),
                    reads=[("ps", bo)], writes=[("den", ds)])
                P.op("dve", lambda e, ds=ds: e.reciprocal(rden[:, ds, :], den[:, ds, :]),
                     reads=[("den", ds)], writes=[("rden", ds)])
                P.op("dve", lambda e, bo=bo, g=g, ds=ds, an=an: e.tensor_tensor(
                    attn_n[:, an, g * 256:(g + 1) * 256].rearrange("p (a d) -> p a d", a=4),
                    psf[:, bo, 0:260].rearrange("p (a d) -> p a d", a=4)[:, :, 0:64],
                    rden[:, ds, :].unsqueeze(2).broadcast_to([128, 4, 64]), ALU.mult),
                    reads=[("ps", bo), ("rden", ds)], writes=[("attn_n", an, g)])

            def emit_tr(blk, an):
                for j in range(8):
                    P.op("pe", lambda e, j=j, an=an: e.transpose(pst[:, j * 128:(j + 1) * 128], attn_n[:, an, j * 128:(j + 1) * 128], identb[:]),
                         reads=[("attn_n", an, g_) for g_ in range(4)], writes=[("pst",)], inc=(j == 7))
                P.op("dve", lambda e, blk=blk: e.tensor_copy(
                    big[:, AT0:AT0 + 8, blk * 128:(blk + 1) * 128], pst[:].rearrange("p (a q) -> p a q", a=8)),
                    reads=[("pst",)], writes=[("big", AT0 + j) for j in range(8)])

            ans = {}
            pend = []
            trq = []

            def do_pv(item):
                pb, pg, pk, pp = item
                emit_pv(pb, pg, pk, pp, ans[pb])
                if pg == 3:
                    trq.append([pb, TRLAG])

            def tick_tr(force=False):
                for it in list(trq):
                    it[1] -= 1
                    if it[1] <= 0 or force:
                        emit_tr(it[0], ans[it[0]])
                        trq.remove(it)

            for (blk, g) in units:
                if g == 0:
                    ans[blk] = ring("attn_n", 2)
                sc = emit_scores(blk, g)
                pend.append((blk, g, sc[0], sc[1]))
                if len(pend) > ADEPTH:
                    do_pv(pend.pop(0))
                tick_tr()
                yield
            while pend:
                do_pv(pend.pop(0))
                tick_tr()
                yield
            yield
            tick_tr(force=True)
            if not last_of_seq:
                P.op("dve", lambda e, l=l: e.tensor_copy(kT[:, l, :, 0:128], kT[:, l, :, 512:640]),
                     reads=[("kT", l, 4)], writes=[("kT", l, 0)])
                P.op("dve", lambda e, l=l: e.tensor_copy(Vt[:, l, 0, :], Vt[:, l, 4, :]),
                     reads=[("V", l, 4)], writes=[("V", l, 0)])
            yield

        def chain(*gens):
            for g_ in gens:
                for _ in g_:
                    yield

        def emit_layer(l, first, last_of_seq):
            emit_norm(lambda kc, l=l: col("g_mix", l, kc), lambda kc: bigc(H0 + kc), lambda kc: ("big", H0 + kc))
            for qi in range(2):
                slot = wget("win", l, 0, MQ + 512 * qi, 512)
                if qi == 0:
                    bs = [psget() for _ in range(4)]
                    for kc in range(KC):
                        for c in range(4):
                            P.op("pe", lambda e, b=bs[c], slot=slot, kc=kc, c=c: e.matmul(
                                psf[:, b, :], wbuf[:, slot, kc, c * 128:(c + 1) * 128], bigc(H0 + kc),
                                start=(kc == 0), stop=(kc == KC - 1)),
                                reads=[("w", slot), ("big", H0 + kc)], writes=[("ps", bs[c])], inc=(kc == KC - 1))
                    for c in range(4):
                        P.op("act", lambda e, b=bs[c], c=c: e.activation(bigc(Q0 + c), psf[:, b, :], AF.Copy, scale=0.125),
                             reads=[("ps", bs[c])], writes=[("big", Q0 + c)])
                    continue
                for c in range(4):
                    b = psget()
                    mm_group(b, slot, c * 128, hrhs, hres)
                    P.op("act", lambda e, b=b, qi=qi, c=c: e.activation(bigc(Q0 + 4 * qi + c), psf[:, b, :], AF.Copy, scale=0.125),
                         reads=[("ps", b)], writes=[("big", Q0 + 4 * qi + c)])
            slot = wget("win", l, 0, MKV, 512)
            for cc_ in range(2):
                b = psget()
                mm_group(b, slot, cc_ * 128, hrhs, hres)
                P.op("dve", lambda e, b=b, cc_=cc_, l=l: e.tensor_copy(kT[:, l, cc_, 128:640], psf[:, b, :]),
                     reads=[("ps", b)], writes=[("kT", l, 1 + s_) for s_ in range(4)])
            for blk in range(NBLK):
                b = psget()
                for kc in range(KC):
                    P.op("pe", lambda e, b=b, slot=slot, kc=kc, blk=blk: e.matmul(
                        psf[:, b, 0:256], bigc(H0 + kc)[:, blk * 128:(blk + 1) * 128], wbuf[:, slot, kc, 256:512],
                        start=(kc == 0), stop=(kc == KC - 1)),
                        reads=[("w", slot), ("big", H0 + kc)], writes=[("ps", b)], inc=(kc == KC - 1))
                P.op("dve", lambda e, b=b, blk=blk, l=l: e.tensor_copy(
                    Vt[:, l, 1 + blk, :].rearrange("p (g d) -> p g d", g=4)[:, :, 0:64],
                    psf[:, b, 0:256].rearrange("p (g d) -> p g d", g=4)),
                    reads=[("ps", b)], writes=[("V", l, 1 + blk)])
            run_interleaved(gen_attn(l, first, last_of_seq), chain(gen_conv(l, first, last_of_seq), gen_yconv(l)),
                            NBLK * 4 + ADEPTH + 1, 17, holdback=2)
            for jg in range(2):
                sg = wget("win", l, 0, MGA + 512 * jg, 512)
                sw = wget("wao", l, 0, 512 * jg, 512)
                for jj in range(4):
                    j = 4 * jg + jj
                    bg = psget()
                    mm_group(bg, sg, jj * 128, hrhs, hres)
                    gs = ring("gate", 2)
                    P.op("act", lambda e, bg=bg, gs=gs, l=l, j=j: e.activation(gate[:, gs, :], psf[:, bg, :], AF.Tanh,
                                                                               bias=hbias[:, l, j:j + 1], scale=0.5),
                         reads=[("ps", bg)], writes=[("gate", gs)])
                    by = psget()
                    mm_group(by, sw, jj * 128, lambda kc: bigc(AT0 + kc), lambda kc: ("big", AT0 + kc))
                    ts = ring("tmpb", 2)
                    P.op("dve", lambda e, by=by, gs=gs, ts=ts: e.scalar_tensor_tensor(tmpb[:, ts, :], gate[:, gs, :], 1.0, psf[:, by, :], ALU.add, ALU.mult),
                         reads=[("ps", by), ("gate", gs)], writes=[("tmpb", ts)])
                    P.op("dve", lambda e, ts=ts, j=j: e.tensor_tensor(bigc(CV0 + j), tmpb[:, ts, :], mbuf[:, j, :], ALU.add),
                         reads=[("tmpb", ts), ("m", j)], writes=[("big", CV0 + j)])
            for og in range(2):
                slot = wget("wo", l, 0, 512 * og, 512)
                for jj in range(4):
                    j = 4 * og + jj
                    b = psget()
                    mm_group(b, slot, jj * 128, lambda kc: bigc(CV0 + kc), lambda kc: ("big", CV0 + kc))
                    P.op("dve", lambda e, b=b, j=j: e.scalar_tensor_tensor(xT[:, j, :], psf[:, b, :], 0.5, xT[:, j, :], ALU.mult, ALU.add),
                         reads=[("ps", b), ("xT", j)], writes=[("xT", j)])
                    P.op("act", lambda e, j=j, l=l: e.activation(a2[:, j, :], xT[:, j, :], AF.Copy, scale=col("g_mlp", l, j)),
                         reads=[("xT", j)], writes=[("a2", j)])
            for fg in range(8):
                slot = wget("wup", l, 0, 512 * fg, 512)
                for ff in range(4):
                    f = 4 * fg + ff
                    b = psget()
                    mm_group(b, slot, ff * 128, lambda kc: a2[:, kc, :], lambda kc: ("a2", kc))
                    rs = ring("relu", 2)
                    P.op("act", lambda e, b=b, rs=rs: e.activation(relu[:, rs, :], psf[:, b, :], AF.Relu),
                         reads=[("ps", b)], writes=[("relu", rs)])
                    P.op("dve", lambda e, rs=rs, f=f: e.tensor_tensor(bigc(f), relu[:, rs, :], relu[:, rs, :], ALU.mult),
                         reads=[("relu", rs)], writes=[("big", f)])
                if fg == 0:
                    b = psget()
                    for kc in range(KC):
                        s = emit_square(kc)
                        P.op("pe", lambda e, kc=kc, s=s, b=b: e.matmul(psf[:, b, :], ones[:], xsq[:, s, :], start=(kc == 0), stop=(kc == KC - 1)),
                             reads=[("xsq", s)], writes=[("ps", b)], inc=True)
                    P.op("act", lambda e, b=b: e.activation(mse[:], psf[:, b, :], AF.Ln, bias=epsc[:, 0:1], scale=1.0 / D),
                         reads=[("ps", b)], writes=[("mse",)])
                    P.op("act", lambda e: e.activation(rstd[:], mse[:], AF.Exp, scale=-1.0),
                         reads=[("mse",)], writes=[("rstd",)])
            for og in range(2):
                bs = [psget() for _ in range(4)]
                for kq in range(4):
                    slot = wget("wdn", l, 1024 * kq, 512 * og, 512)
                    for jj in range(4):
                        for kc in range(KC):
                            f = kq * 8 + kc
                            P.op("pe", lambda e, b=bs[jj], slot=slot, jj=jj, kc=kc, f=f, kq=kq: e.matmul(
                                psf[:, b, :], wbuf[:, slot, kc, jj * 128:(jj + 1) * 128], bigc(f),
                                start=(kq == 0 and kc == 0), stop=(kq == 3 and kc == KC - 1)),
                                reads=[("w", slot), ("big", f)], writes=[("ps", bs[jj])], inc=(kc == KC - 1))
                for jj in range(4):
                    j = 4 * og + jj
                    ts = ring("tmpb", 2)
                    P.op("dve", lambda e, b=bs[jj], ts=ts: e.tensor_tensor(tmpb[:, ts, :], psf[:, b, :], rstd[:], ALU.mult),
                         reads=[("ps", bs[jj]), ("rstd",)], writes=[("tmpb", ts)])
                    P.op("dve", lambda e, ts=ts, j=j: e.tensor_tensor(xT[:, j, :], tmpb[:, ts, :], xT[:, j, :], ALU.add),
                         reads=[("tmpb", ts), ("xT", j)], writes=[("xT", j)])

        for l in range(L):
            emit_layer(l, first, last_of_seq)
        dsrc = None
        if t + 1 < ntiles:
            for blk in range(4):
                xslots[(t + 1, blk)] = emit_xload(t + 1, blk)
        b = psget()
        for kc in range(KC):
            s = emit_square(kc)
            P.op("pe", lambda e, kc=kc, s=s, b=b: e.matmul(psf[:, b, :], ones[:], xsq[:, s, :], start=(kc == 0), stop=(kc == KC - 1)),
                 reads=[("xsq", s)], writes=[("ps", b)], inc=True)
        P.op("act", lambda e, b=b: e.activation(mse[:], psf[:, b, :], AF.Ln, bias=epsc[:, 0:1], scale=1.0 / D),
             reads=[("ps", b)], writes=[("mse",)])
        P.op("act", lambda e: e.activation(rstd[:], mse[:], AF.Exp, scale=-0.5),
             reads=[("mse",)], writes=[("rstd",)])
        for kc in range(KC):
            P.op("dve", lambda e, kc=kc: e.scalar_tensor_tensor(mbuf[:, kc, :], xT[:, kc, :], gfin(kc), rstd[:], ALU.mult, ALU.mult),
                 reads=[("xT", kc), ("rstd",)], writes=[("m", kc)])
        for blk in range(NBLK):
            pending_out.append((t, blk))

    emit_out_blocks()
    emit_weight_dmas()
    P.final_wait("sp", [(k, v) for k, v in P.cnt.items() if isinstance(k, tuple) and k[0] == "xo"])

    sems = {}
    for k in P.sem_keys():
        nm = "s_" + ("_".join(str(z) for z in k) if isinstance(k, tuple) else str(k))
        sems[k] = nc.alloc_semaphore(nm)

    def runner(name):
        def body(e):
            for waits, fn, inc in P.q[name]:
                for s, v in waits:
                    e.wait_ge(sems[s], v)
                if fn is None:
                    continue
                ins = fn(e)
                if inc is not None:
                    ins.then_inc(sems[inc[0]], inc[1])
        return body

    with nc.Block() as block:
        block.tensor(runner("pe"))
        block.scalar(runner("act"))
        block.vector(runner("dve"))
        block.gpsimd(runner("pool"))
        block.sync(runner("sp"))
    return nc, P


def host_consts(depth):
    slopes = np.power(np.float32(2.0), -8.0 * np.arange(1, NQH + 1, dtype=np.float32) / NQH).astype(np.float32)
    j = np.arange(128)[:, None]
    r = np.arange(128)[None, :]
    et = np.zeros((128, 2, NQH, 128), dtype=np.float32)
    for h in range(NQH):
        dprev = (128 + r - j).astype(np.float32)
        dcur = (r - j).astype(np.float32)
        et[:, 0, h, :] = np.where(j > r, np.exp(-slopes[h] * np.maximum(dprev, 0.0)), 0.0)
        et[:, 1, h, :] = np.where(j <= r, np.exp(-slopes[h] * np.maximum(dcur, 0.0)), 0.0)
    identf = np.eye(128, dtype=np.float32)
    identb = np.eye(128, dtype=np.float32).astype(ml_dtypes.bfloat16)
    return et.reshape(128, -1), identf, identb


def prep_weights(inputs, depth):
    L = depth
    f = lambda a: np.ascontiguousarray(np.asarray(a, dtype=np.float32))
    perm = _win_perm()
    w_in = f(np.asarray(inputs["w_in"])[:L][:, :, perm])

    def cols(v):
        return np.asarray(v, dtype=np.float32).reshape(8, 128).T

    pv = np.zeros((128, L * 64 + 8), dtype=np.float32)
    for l in range(L):
        o = l * 64
        pv[:, o + 0:o + 8] = cols(inputs["g_mix"][l])
        pv[:, o + 8:o + 16] = cols(inputs["g_mlp"][l])
        pv[:, o + 16:o + 24] = cols(np.asarray(inputs["b_gates"])[l, :D])
        pv[:, o + 24:o + 32] = cols(np.asarray(inputs["b_gates"])[l, D:])
        pv[:, o + 32:o + 40] = cols(np.asarray(inputs["conv_w"])[l, 0])
        pv[:, o + 40:o + 48] = cols(np.asarray(inputs["conv_w"])[l, 1])
        pv[:, o + 48:o + 56] = cols(np.asarray(inputs["conv_w"])[l, 2])
        pv[:, o + 56:o + 64] = cols(inputs["conv_b"][l])
    pv[:, L * 64:L * 64 + 8] = cols(inputs["g_final"])
    sinkb = np.ascontiguousarray(np.broadcast_to(np.asarray(inputs["sinks"], dtype=np.float32)[:L].reshape(1, L * 16), (128, L * 16)))
    et, identf, identb = host_consts(L)
    shared = {
        "w_in": w_in,
        "w_ao": f(np.asarray(inputs["w_attn_out"])[:L]),
        "w_co": f(np.asarray(inputs["w_conv_out"])[:L]),
        "w_o": f(np.asarray(inputs["w_o"])[:L]),
        "w_up": f(np.asarray(inputs["w_up"])[:L]),
        "w_dn": f(np.asarray(inputs["w_down"])[:L]),
        "pvec": pv, "sinkb": sinkb, "etab": et, "identf": identf, "identb": identb,
    }
    return shared


_CACHE = {}


def run(inputs, depth=L_FULL, seq=SEQ, dbg=None):
    x = np.asarray(inputs["x"], dtype=np.float32)
    B, S, _ = x.shape
    assert S == seq and B % N_CORES == 0
    bpc = B // N_CORES
    tiles_per_seq = S // T
    ntiles = bpc * tiles_per_seq
    key = (ntiles, depth, tiles_per_seq, dbg)
    if key not in _CACHE:
        _CACHE[key] = build_program(ntiles, depth, tiles_per_seq, dbg)[0]
    nc = _CACHE[key]
    shared = prep_weights(inputs, depth)
    in_maps = []
    for c in range(N_CORES):
        m = dict(shared)
        m["x"] = np.ascontiguousarray(x[c * bpc:(c + 1) * bpc].reshape(bpc * S, D))
        in_maps.append(m)
    res = run_bass_kernel_spmd(nc, in_maps, core_ids=list(range(N_CORES)))
    outs = [np.asarray(r["out"], dtype=np.float32).reshape(bpc, S, D) for r in res.results]
    return np.concatenate(outs, axis=0)


def kernel(**inputs):
    return run(inputs)
```

```python
import numpy as np
import ml_dtypes
import concourse.bass as bass
import concourse.mybir as mybir
from concourse.bass_utils import run_bass_kernel_spmd

F32 = mybir.dt.float32
BF16 = mybir.dt.bfloat16
AF = mybir.ActivationFunctionType
ALU = mybir.AluOpType

D = 1024
KC = 8
T = 512
NBLK = 4
L_FULL = 2
SEQ = 2048
N_CORES = 8
NQH = 16
HD = 64
EPS = 1e-6
NWSLOT = 5
NPSB = 7
ADEPTH = 2
TRLAG = 4

OQ, OK_, OV, OCB, OCC, OCU, OGA, OGC = 0, 1024, 1280, 1536, 2560, 3584, 4608, 5632
HE = [0, 1, 2, 3, 8, 9, 10, 11]
HO = [4, 5, 6, 7, 12, 13, 14, 15]


def _win_perm():
    cols = []
    for c in range(8):
        cols += list(range(OQ + HE[c] * 64, OQ + HE[c] * 64 + 64))
        cols += list(range(OQ + HO[c] * 64, OQ + HO[c] * 64 + 64))
    cols += list(range(OK_, OK_ + 256))
    cols += list(range(OV, OV + 256))
    for i in range(8):
        cols += list(range(OCC + 128 * i, OCC + 128 * i + 128))
        cols += list(range(OCU + 128 * i, OCU + 128 * i + 128))
        cols += list(range(OCB + 128 * i, OCB + 128 * i + 128))
    cols += list(range(OGC, OGC + 1024))
    cols += list(range(OGA, OGA + 1024))
    return np.asarray(cols, dtype=np.int64)


MQ, MKV, MC, MGC, MGA = 0, 1024, 1536, 4608, 5632


def _tile_catalogue():
    t512 = [("win", 0, MQ, 512), ("win", 0, MQ + 512, 512), ("win", 0, MKV, 512)]
    for jg in range(2):
        t512 += [("win", 0, MGC + 512 * jg, 512), ("wco", 0, 512 * jg, 512)]
    for jg in range(2):
        t512 += [("win", 0, MGA + 512 * jg, 512), ("wao", 0, 512 * jg, 512)]
    t512 += [("wo", 0, 512 * og, 512) for og in range(2)]
    t512 += [("wup", 0, 512 * fg, 512) for fg in range(8)]
    t512 += [("wdn", 1024 * kq, 512 * og, 512) for og in range(2) for kq in range(4)]
    t384 = [("win", 0, MC + 384 * i, 384) for i in range(8)]
    return t512, t384


T512, T384 = _tile_catalogue()
TIDX = {k: (0, i) for i, k in enumerate(T512)}
TIDX.update({k: (1, i) for i, k in enumerate(T384)})


class Prog:
    ENG = ["pe", "act", "dve", "pool", "sp"]

    def __init__(self):
        self.q = {e: [] for e in self.ENG}
        self.cnt = {}
        self.seen = {e: {} for e in self.ENG}
        self.lastw = {}
        self.readers = {}

    def op(self, eng, fn, reads=(), writes=(), inc=True, dma=None):
        deps = {}

        def add(t):
            if t is None:
                return
            s, v = t
            if v > deps.get(s, 0):
                deps[s] = v

        for r in reads:
            add(self.lastw.get(r))
        for w in writes:
            add(self.lastw.get(w))
            rd = self.readers.get(w)
            if rd:
                for s, v in rd.items():
                    add((s, v))
        waits = []
        seen = self.seen[eng]
        for s, v in deps.items():
            if s == "pe" and eng == "pe":
                continue
            if v > seen.get(s, 0):
                waits.append((s, v))
                seen[s] = v
        if dma is not None:
            self.cnt[dma] = self.cnt.get(dma, 0) + 16
            tok = (dma, self.cnt[dma])
            incspec = (dma, 16)
        elif inc:
            self.cnt[eng] = self.cnt.get(eng, 0) + 1
            tok = (eng, self.cnt[eng])
            incspec = (eng, 1)
        else:
            tok = (eng, self.cnt.get(eng, 0) + 1)
            incspec = None
        self.q[eng].append((waits, fn, incspec))
        for r in reads:
            rd = self.readers.setdefault(r, {})
            if tok[1] > rd.get(tok[0], 0):
                rd[tok[0]] = tok[1]
        for w in writes:
            self.lastw[w] = tok
            self.readers[w] = {}
        return tok

    def final_wait(self, eng, toks):
        waits = []
        for s, v in toks:
            if v > self.seen[eng].get(s, 0):
                waits.append((s, v))
                self.seen[eng][s] = v
        self.q[eng].append((waits, None, None))

    def sem_keys(self):
        return list(self.cnt.keys())


def build_program(ntiles, depth, tiles_per_seq=4, dbg=None):
    nc = bass.Bass("TRN2", target_bir_lowering=False)
    L = depth
    ntok = ntiles * T
    dt = nc.dram_tensor
    x_d = dt("x", [ntok, D], F32, kind="ExternalInput").ap()
    out_d = dt("out", [ntok, D], F32, kind="ExternalOutput").ap()
    win_d = dt("w_in", [L, D, 6656], F32, kind="ExternalInput").ap()
    wao_d = dt("w_ao", [L, D, D], F32, kind="ExternalInput").ap()
    wco_d = dt("w_co", [L, D, D], F32, kind="ExternalInput").ap()
    wo_d = dt("w_o", [L, D, D], F32, kind="ExternalInput").ap()
    wup_d = dt("w_up", [L, D, 4 * D], F32, kind="ExternalInput").ap()
    wdn_d = dt("w_dn", [L, 4 * D, D], F32, kind="ExternalInput").ap()
    NPV = L * 64 + 8
    pvec_d = dt("pvec", [128, NPV], F32, kind="ExternalInput").ap()
    sink_d = dt("sinkb", [128, L * 16], F32, kind="ExternalInput").ap()
    etab_d = dt("etab", [128, 2 * 16 * 128], F32, kind="ExternalInput").ap()
    idf_d = dt("identf", [128, 128], F32, kind="ExternalInput").ap()
    idb_d = dt("identb", [128, 128], BF16, kind="ExternalInput").ap()
    scr512 = dt("wscr512", [L, len(T512), 128, KC * 512], BF16, kind="Internal").ap()
    scr384 = dt("wscr384", [L, len(T384), 128, KC * 384], BF16, kind="Internal").ap()

    sb = nc.alloc_sbuf_tensor
    big = sb("big", [128, 32, T], BF16)
    xT = sb("xT", [128, KC, T], F32)
    mbuf = sb("mbuf", [128, KC, T], F32)
    a2 = sb("a2", [128, KC, T], BF16)
    xin = sb("xin", [128, 4, D], F32)
    xout = sb("xout", [128, 2, D], F32)
    xsq = sb("xsq", [128, 3, T], BF16)
    rstd = sb("rstd", [128, T], F32)
    mse = sb("mse", [128, T], F32)
    epsc = sb("epsc", [128, 1], F32)
    kT = sb("kT", [128, L, 2, 5 * 128], BF16)
    Vt = sb("Vt", [128, L, 5, 4 * 65], BF16)
    ccs = sb("ccs", [128, 1, T], F32)
    yb = sb("yb", [128, 2, T + 2], F32)
    zb = sb("zb", [128, 2, T], F32)
    yh = sb("yh", [128, L, 8, 2], F32)
    sexp = sb("sexp", [128, 2, T], F32)
    PT = sb("PT", [128, 2 * (ADEPTH + 1), T], BF16)
    attn_n = sb("attn_n", [128, 2, D], BF16)
    den = sb("den", [128, 2, 4], F32)
    rden = sb("rden", [128, 2, 4], F32)
    gate = sb("gate", [128, 2, T], F32)
    tmpb = sb("tmpb", [128, 2, T], F32)
    relu = sb("relu", [128, 2, T], F32)
    wbuf = sb("wbuf", [128, NWSLOT, KC, 512], BF16)
    etab = sb("etab_s", [128, 2, 16, 128], F32)
    pvec = sb("pvec_s", [128, NPV], F32)
    hbias = sb("hbias", [128, L, 16], F32)
    sexpk = sb("sinkexp", [128, L * 16], F32)
    identf = sb("identf_s", [128, 128], F32)
    identb = sb("identb_s", [128, 128], BF16)
    ones = sb("ones", [128, 128], BF16)

    psf = nc.alloc_psum_tensor("psf", [128, NPSB, 512], F32)
    pst = nc.alloc_psum_tensor("pst", [128, 1024], BF16)

    P = Prog()
    P.sbuf_left = nc.sbuf_bytes_remaining

    def col(name, l, c):
        base = {"g_mix": 0, "g_mlp": 8, "b_ga": 16, "b_gc": 24, "cw0": 32, "cw1": 40, "cw2": 48, "cb": 56}[name]
        o = l * 64 + base + c
        return pvec[:, o:o + 1]

    def gfin(c):
        o = L * 64 + c
        return pvec[:, o:o + 1]

    P.op("sp", lambda e: e.dma_start(out=pvec[:], in_=pvec_d), writes=[("c", 0)], dma="cst")
    P.op("sp", lambda e: e.dma_start(out=sexpk[:], in_=sink_d), writes=[("c", 1)], dma="cst")
    P.op("sp", lambda e: e.dma_start(out=etab[:].rearrange("p a h q -> p (a h q)"), in_=etab_d), writes=[("c", 2)], dma="cst")
    P.op("sp", lambda e: e.dma_start(out=identf[:], in_=idf_d), writes=[("c", 3)], dma="cst")
    P.op("sp", lambda e: e.dma_start(out=identb[:], in_=idb_d), writes=[("c", 4)], dma="cst")
    CST = [("c", i) for i in range(5)]
    P.op("dve", lambda e: e.memset(ones[:], 1.0), reads=CST, writes=[("c", 5)])
    P.op("dve", lambda e: e.memset(epsc[:], EPS), writes=[("c", 6)])
    P.op("dve", lambda e: e.memset(Vt[:], 1.0), writes=[("V", l, s) for l in range(L) for s in range(5)])
    for l in range(L):
        P.op("dve", lambda e, l=l: e.tensor_scalar(hbias[:, l, :], pvec[:, l * 64 + 16:l * 64 + 32], 0.5, None, ALU.mult),
             reads=CST, writes=[("c", 7 + l)])
    P.op("act", lambda e: e.activation(sexpk[:], sexpk[:], AF.Exp), reads=CST, writes=[("c", 1)])
    ALLC = [("c", i) for i in range(7 + L)]
    P.op("pe", lambda e: e.nop(), reads=ALLC, inc=False)
    P.op("act", lambda e: e.nop(), reads=ALLC)
    P.op("dve", lambda e: e.nop(), reads=ALLC)
    P.op("pool", lambda e: e.nop(), reads=ALLC)

    def wsrc(kind, l, r0, c0, n):
        dten = {"win": win_d, "wao": wao_d, "wco": wco_d, "wo": wo_d, "wup": wup_d, "wdn": wdn_d}[kind]
        return dten[l, r0:r0 + 1024, c0:c0 + n].rearrange("(kc p) n -> p kc n", p=128)

    wtiles = []
    cur = {"t": 0}

    def wget(kind, l, r0, c0, ncol):
        i = len(wtiles)
        slot = i % NWSLOT
        res = ("w", slot)
        rd = dict(P.readers.get(res, {}))
        wtiles.append(((kind, l, r0, c0, ncol), slot, rd, cur["t"]))
        fill = i // NWSLOT + 1
        P.lastw[res] = (res, 32 * fill)
        P.readers[res] = {}
        P.cnt[res] = 32 * fill
        return slot

    def emit_weight_dmas():
        seen = P.seen["pool"]
        q = P.q["pool"]
        st_cnt = {}
        pend_st = []

        def need(s, v, waits):
            if v > seen.get(s, 0):
                waits.append((s, v))
                seen[s] = v

        def flush_store(k):
            while len(pend_st) > k:
                slot_, scr_, ncol_, fill_ = pend_st.pop(0)
                waits = []
                need(("w", slot_), 32 * fill_, waits)
                st_cnt[slot_] = st_cnt.get(slot_, 0) + 16
                P.cnt[("ws", slot_)] = st_cnt[slot_]
                q.append((waits, (lambda e, slot_=slot_, scr_=scr_, ncol_=ncol_: e.dma_start(
                    out=scr_, in_=wbuf[:, slot_, :, 0:ncol_])), (("ws", slot_), 16)))

        started_bf16 = False
        for i, ((kind, l, r0, c0, ncol), slot, rd, t_) in enumerate(wtiles):
            arr, idx = TIDX[(kind, r0, c0, ncol)]
            scr = (scr512 if arr == 0 else scr384)[l, idx].rearrange("p (kc n) -> p kc n", kc=KC)
            fill = i // NWSLOT + 1
            waits = []
            for s, v in rd.items():
                need(s, v, waits)
            if t_ == 0:
                if st_cnt.get(slot, 0):
                    need(("ws", slot), st_cnt[slot], waits)
                src_ap = wsrc(kind, l, r0, c0, ncol)
                for h in range(2):
                    fn = (lambda e, slot=slot, src_ap=src_ap, ncol=ncol, h=h: e.dma_start(
                        out=wbuf[:, slot, 4 * h:4 * h + 4, 0:ncol], in_=src_ap[:, 4 * h:4 * h + 4, :]))
                    q.append((waits if h == 0 else [], fn, (("w", slot), 16)))
                pend_st.append((slot, scr, ncol, fill))
                flush_store(2)
            else:
                if not started_bf16:
                    flush_store(0)
                    for s_, v_ in st_cnt.items():
                        need(("ws", s_), v_, waits)
                    started_bf16 = True
                fn = (lambda e, slot=slot, scr=scr, ncol=ncol: e.dma_start(out=wbuf[:, slot, :, 0:ncol], in_=scr))
                q.append((waits, fn, (("w", slot), 32)))
        flush_store(0)

    psstate = {"n": 0}

    def psget():
        b = psstate["n"] % NPSB
        psstate["n"] += 1
        return b

    rr = {}

    def ring(name, n):
        v = rr.get(name, 0)
        rr[name] = v + 1
        return v % n

    def mm_group(b, slot, c0, rhs_fn, rhs_res, ncol=128):
        for kc in range(KC):
            P.op("pe",
                 lambda e, b=b, slot=slot, c0=c0, kc=kc: e.matmul(
                     psf[0:ncol, b, :], wbuf[:, slot, kc, c0:c0 + ncol], rhs_fn(kc), start=(kc == 0), stop=(kc == KC - 1)),
                 reads=[("w", slot), rhs_res(kc)], writes=[("ps", b)], inc=(kc == KC - 1))

    def bigc(i):
        return big[:, i, :]

    H0, Q0, CV0, AT0 = 0, 8, 16, 24

    pending_out = []

    def emit_out_blocks():
        while pending_out:
            t_, blk = pending_out.pop(0)
            s = ring("xout", 2)
            for half in range(2):
                b = psget()
                for j in range(4):
                    cidx = half * 4 + j
                    P.op("pe", lambda e, b=b, j=j, cidx=cidx, blk=blk: e.transpose(
                        psf[:, b, j * 128:(j + 1) * 128], mbuf[:, cidx, blk * 128:(blk + 1) * 128], identf[:]),
                        reads=[("m", cidx)], writes=[("ps", b)], inc=(j == 3))
                if half == 0:
                    P.op("act", lambda e, b=b, s=s, half=half: e.activation(xout[:, s, half * 512:(half + 1) * 512], psf[:, b, :], AF.Copy),
                         reads=[("ps", b)], writes=[("xout", s)])
                else:
                    P.op("dve", lambda e, b=b, s=s, half=half: e.tensor_copy(xout[:, s, half * 512:(half + 1) * 512], psf[:, b, :]),
                         reads=[("ps", b)], writes=[("xout", s)])
            r0 = (t_ * NBLK + blk) * 128
            P.op("sp", lambda e, s=s, r0=r0: e.dma_start(out=out_d[r0:r0 + 128, :], in_=xout[:, s, :]),
                 reads=[("xout", s)], dma=("xo", s))

    def emit_square(kc):
        s = ring("xsq", 3)
        if kc % 2 == 0:
            P.op("act", lambda e, kc=kc, s=s: e.activation(xsq[:, s, :], xT[:, kc, :], AF.Square),
                 reads=[("xT", kc)], writes=[("xsq", s)])
        else:
            P.op("dve", lambda e, kc=kc, s=s: e.tensor_tensor(xsq[:, s, :], xT[:, kc, :], xT[:, kc, :], ALU.mult),
                 reads=[("xT", kc)], writes=[("xsq", s)])
        return s

    def emit_norm(gcol, dst_ap, dst_res):
        b = psget()
        for kc in range(KC):
            s = emit_square(kc)
            P.op("pe", lambda e, kc=kc, s=s, b=b: e.matmul(psf[:, b, :], ones[:], xsq[:, s, :], start=(kc == 0), stop=(kc == KC - 1)),
                 reads=[("xsq", s)], writes=[("ps", b)], inc=True)
        P.op("act", lambda e, b=b: e.activation(mse[:], psf[:, b, :], AF.Ln, bias=epsc[:, 0:1], scale=1.0 / D),
             reads=[("ps", b)], writes=[("mse",)])
        P.op("act", lambda e: e.activation(rstd[:], mse[:], AF.Exp, scale=-0.5),
             reads=[("mse",)], writes=[("rstd",)])
        for kc in range(KC):
            P.op("dve", lambda e, kc=kc: e.scalar_tensor_tensor(dst_ap(kc), xT[:, kc, :], gcol(kc), rstd[:], ALU.mult, ALU.mult),
                 reads=[("xT", kc), ("rstd",)], writes=[dst_res(kc)])
        emit_out_blocks()

    def emit_xload(t, blk):
        s = ring("xin", 4)
        r0 = (t * NBLK + blk) * 128
        P.op("sp", lambda e, s=s, r0=r0: e.dma_start(out=xin[:, s, :], in_=x_d[r0:r0 + 128, :]),
             writes=[("xin", s)], dma=("xin", s))
        return s

    xslots = {}

    def run_interleaved(primary, filler, nprimary, nfiller, holdback=2):
        spread = nfiller - holdback
        done_f = 0
        for i in range(nprimary):
            next(primary, None)
            want = ((i + 1) * spread) // nprimary
            while done_f < want:
                next(filler, None)
                done_f += 1
        for _ in filler:
            pass
        for _ in primary:
            pass

    for t in range(ntiles):
        cur["t"] = t
        first = (t % tiles_per_seq == 0)
        last_of_seq = (t % tiles_per_seq == tiles_per_seq - 1)
        for blk in range(NBLK):
            if (t, blk) not in xslots:
                xslots[(t, blk)] = emit_xload(t, blk)
            s = xslots[(t, blk)]
            for half in range(2):
                b = psget()
                for j in range(4):
                    cidx = half * 4 + j
                    P.op("pe", lambda e, b=b, j=j, s=s, cidx=cidx: e.transpose(
                        psf[:, b, j * 128:(j + 1) * 128], xin[:, s, cidx * 128:(cidx + 1) * 128], identf[:]),
                        reads=[("xin", s)], writes=[("ps", b)], inc=(j == 3))
                eng = "act" if half == 0 else "dve"
                if eng == "act":
                    fn = lambda e, b=b, half=half, blk=blk: e.activation(
                        xT[:, half * 4:half * 4 + 4, blk * 128:(blk + 1) * 128],
                        psf[:, b, :].rearrange("p (a q) -> p a q", a=4), AF.Copy)
                else:
                    fn = lambda e, b=b, half=half, blk=blk: e.tensor_copy(
                        xT[:, half * 4:half * 4 + 4, blk * 128:(blk + 1) * 128],
                        psf[:, b, :].rearrange("p (a q) -> p a q", a=4))
                P.op(eng, fn, reads=[("ps", b)], writes=[("xT", half * 4 + j) for j in range(4)])

        hrhs = lambda kc: bigc(H0 + kc)
        hres = lambda kc: ("big", H0 + kc)

        def gen_conv(l, first, last_of_seq):
            def emit_cb(prev):
                slot_, i_, s_ = prev
                bcb = psget()
                mm_group(bcb, slot_, 256, hrhs, hres)
                P.op("dve", lambda e, s_=s_, bcb=bcb, i_=i_: e.tensor_tensor(bigc(CV0 + i_), psf[:, bcb, :], zb[:, s_, :], ALU.mult),
                     reads=[("ps", bcb), ("z", s_)], writes=[("big", CV0 + i_)])

            prev = None
            for i in range(8):
                slot = wget("win", l, 0, MC + 384 * i, 384)
                if prev is not None:
                    emit_cb(prev)
                bcc, bcu = psget(), psget()
                mm_group(bcc, slot, 0, hrhs, hres)
                mm_group(bcu, slot, 128, hrhs, hres)
                s = ring("conv", 2)
                P.op("act", lambda e, s=s, bcc=bcc: e.activation(ccs[:, 0, :], psf[:, bcc, :], AF.Copy),
                     reads=[("ps", bcc)], writes=[("ccs", 0)])
                if first:
                    P.op("dve", lambda e, s=s: e.memset(yb[:, s, 0:2], 0.0), writes=[("ybh", s)])
                else:
                    P.op("dve", lambda e, s=s, l=l, i=i: e.tensor_copy(yb[:, s, 0:2], yh[:, l, i, :]),
                         reads=[("yh", l, i)], writes=[("ybh", s)])
                P.op("dve", lambda e, s=s, bcu=bcu: e.tensor_tensor(yb[:, s, 2:T + 2], psf[:, bcu, :], ccs[:, 0, :], ALU.mult),
                     reads=[("ps", bcu), ("ccs", 0)], writes=[("yb", s)])
                P.op("act", lambda e, s=s, l=l, i=i: e.activation(zb[:, s, :], yb[:, s, 2:T + 2], AF.Identity,
                                                                  bias=col("cb", l, i), scale=col("cw2", l, i)),
                     reads=[("yb", s)], writes=[("z", s)])
                P.op("dve", lambda e, s=s, l=l, i=i: e.scalar_tensor_tensor(zb[:, s, :], yb[:, s, 1:T + 1], col("cw1", l, i), zb[:, s, :], ALU.mult, ALU.add),
                     reads=[("yb", s), ("ybh", s), ("z", s)], writes=[("z", s)])
                P.op("dve", lambda e, s=s, l=l, i=i: e.scalar_tensor_tensor(zb[:, s, :], yb[:, s, 0:T], col("cw0", l, i), zb[:, s, :], ALU.mult, ALU.add),
                     reads=[("yb", s), ("ybh", s), ("z", s)], writes=[("z", s)])
                if not last_of_seq:
                    P.op("dve", lambda e, s=s, l=l, i=i: e.tensor_copy(yh[:, l, i, :], yb[:, s, T:T + 2]),
                         reads=[("yb", s)], writes=[("yh", l, i)])
                prev = (slot, i, s)
                yield
            emit_cb(prev)
            yield

        def gen_yconv(l):
            for jg in range(2):
                sg = wget("win", l, 0, MGC + 512 * jg, 512)
                sw = wget("wco", l, 0, 512 * jg, 512)
                for jj in range(4):
                    j = 4 * jg + jj
                    bg = psget()
                    mm_group(bg, sg, jj * 128, hrhs, hres)
                    gs = ring("gate", 2)
                    P.op("act", lambda e, bg=bg, gs=gs, l=l, j=j: e.activation(gate[:, gs, :], psf[:, bg, :], AF.Tanh,
                                                                               bias=hbias[:, l, 8 + j:9 + j], scale=0.5),
                         reads=[("ps", bg)], writes=[("gate", gs)])
                    by = psget()
                    mm_group(by, sw, jj * 128, lambda kc: bigc(CV0 + kc), lambda kc: ("big", CV0 + kc))
                    P.op("dve", lambda e, by=by, gs=gs, j=j: e.scalar_tensor_tensor(mbuf[:, j, :], gate[:, gs, :], 1.0, psf[:, by, :], ALU.add, ALU.mult),
                         reads=[("ps", by), ("gate", gs)], writes=[("m", j)])
                    yield

        def gen_attn(l, first, last_of_seq):
            units = [(blk, g) for blk in range(NBLK) for g in range(4)]

            def emit_scores(blk, g):
                p = g % 2
                cb0 = 4 * (g // 2)
                kc2 = g // 2
                kbs = [1] if (first and blk == 0) else [0, 1]
                pts = {}
                for kb in kbs:
                    slot_k = blk + kb
                    b = psget()
                    P.op("pe", lambda e, b=b, p=p, cb0=cb0, kc2=kc2, slot_k=slot_k, blk=blk, l=l: e.matmul(
                        psf[:, b, :].rearrange("p (a q) -> p a q", a=4),
                        kT[p * 64:(p + 1) * 64, l, kc2, slot_k * 128:(slot_k + 1) * 128],
                        big[p * 64:(p + 1) * 64, Q0 + cb0:Q0 + cb0 + 4, blk * 128:(blk + 1) * 128],
                        start=True, stop=True),
                        reads=[("kT", l, slot_k)] + [("big", Q0 + cb0 + a) for a in range(4)], writes=[("ps", b)], inc=True)
                    se = ring("sexp", 2)
                    P.op("act", lambda e, b=b, se=se: e.activation(sexp[:, se, :], psf[:, b, :], AF.Exp),
                         reads=[("ps", b)], writes=[("sexp", se)])
                    pt = ring("PT", 2 * (ADEPTH + 1))
                    pts[kb] = pt
                    P.op("dve", lambda e, se=se, pt=pt, kb=kb, g=g: e.tensor_tensor(
                        PT[:, pt, :].rearrange("p (a q) -> p a q", a=4),
                        sexp[:, se, :].rearrange("p (a q) -> p a q", a=4),
                        etab[:, kb, 4 * g:4 * g + 4, :], ALU.mult),
                        reads=[("sexp", se)], writes=[("PT", pt)])
                return kbs, pts

            def emit_pv(blk, g, kbs, pts, an):
                bo = psget()
                for a in range(4):
                    for ki, kb in enumerate(kbs):
                        slot_k = blk + kb
                        P.op("pe", lambda e, bo=bo, a=a, kb=kb, pt=pts[kb], slot_k=slot_k, g=g, ki=ki, nk=len(kbs), l=l: e.matmul(
                            psf[:, bo, a * 65:(a + 1) * 65], PT[:, pt, a * 128:(a + 1) * 128],
                            Vt[:, l, slot_k, g * 65:(g + 1) * 65], start=(ki == 0), stop=(ki == nk - 1)),
                            reads=[("PT", pts[kb]), ("V", l, slot_k)], writes=[("ps", bo)],
                            inc=(a == 3 and ki == len(kbs) - 1))
                ds = ring("den", 2)
                P.op("dve", lambda e, bo=bo, ds=ds, g=g, l=l: e.tensor_tensor(
                    den[:, ds, :], psf[:, bo, 0:260].rearrange("p (a d) -> p a d", a=4)[:, :, 64],
                    sexpk[:, l * 16 + 4 * g:l * 16 + 4 * g + 4], ALU.add),
                    reads=[("ps", bo)], writes=[("den", ds)])
                P.op("dve", lambda e, ds=ds: e.reciprocal(rden[:, ds, :], den[:, ds, :]),
                     reads=[("den", ds)], writes=[("rden", ds)])
                P.op("dve", lambda e, bo=bo, g=g, ds=ds, an=an: e.tensor_tensor(
                    attn_n[:, an, g * 256:(g + 1) * 256].rearrange("p (a d) -> p a d", a=4),
                    psf[:, bo, 0:260].rearrange("p (a d) -> p a d", a=4)[:, :, 0:64],
                    rden[:, ds, :].unsqueeze(2).broadcast_to([128, 4, 64]), ALU.mult),
                    reads=[("ps", bo), ("rden", ds)], writes=[("attn_n", an, g)])

            def emit_tr(blk, an):
                for j in range(8):
                    P.op("pe", lambda e, j=j, an=an: e.transpose(pst[:, j * 128:(j + 1) * 128], attn_n[:, an, j * 128:(j + 1) * 128], identb[:]),
                         reads=[("attn_n", an, g_) for g_ in range(4)], writes=[("pst",)], inc=(j == 7))
                P.op("dve", lambda e, blk=blk: e.tensor_copy(
                    big[:, AT0:AT0 + 8, blk * 128:(blk + 1) * 128], pst[:].rearrange("p (a q) -> p a q", a=8)),
                    reads=[("pst",)], writes=[("big", AT0 + j) for j in range(8)])

            ans = {}
            pend = []
            trq = []

            def do_pv(item):
                pb, pg, pk, pp = item
                emit_pv(pb, pg, pk, pp, ans[pb])
                if pg == 3:
                    trq.append([pb, TRLAG])

            def tick_tr(force=False):
                for it in list(trq):
                    it[1] -= 1
                    if it[1] <= 0 or force:
                        emit_tr(it[0], ans[it[0]])
                        trq.remove(it)

            for (blk, g) in units:
                if g == 0:
                    ans[blk] = ring("attn_n", 2)
                sc = emit_scores(blk, g)
                pend.append((blk, g, sc[0], sc[1]))
                if len(pend) > ADEPTH:
                    do_pv(pend.pop(0))
                tick_tr()
                yield
            while pend:
                do_pv(pend.pop(0))
                tick_tr()
                yield
            yield
            tick_tr(force=True)
            if not last_of_seq:
                P.op("dve", lambda e, l=l: e.tensor_copy(kT[:, l, :, 0:128], kT[:, l, :, 512:640]),
                     reads=[("kT", l, 4)], writes=[("kT", l, 0)])
                P.op("dve", lambda e, l=l: e.tensor_copy(Vt[:, l, 0, :], Vt[:, l, 4, :]),
                     reads=[("V", l, 4)], writes=[("V", l, 0)])
            yield

        def chain(*gens):
            for g_ in gens:
                for _ in g_:
                    yield

        def emit_layer(l, first, last_of_seq):
            emit_norm(lambda kc, l=l: col("g_mix", l, kc), lambda kc: bigc(H0 + kc), lambda kc: ("big", H0 + kc))
            for qi in range(2):
                slot = wget("win", l, 0, MQ + 512 * qi, 512)
                if qi == 0:
                    bs = [psget() for _ in range(4)]
                    for kc in range(KC):
                        for c in range(4):
                            P.op("pe", lambda e, b=bs[c], slot=slot, kc=kc, c=c: e.matmul(
                                psf[:, b, :], wbuf[:, slot, kc, c * 128:(c + 1) * 128], bigc(H0 + kc),
                                start=(kc == 0), stop=(kc == KC - 1)),
                                reads=[("w", slot), ("big", H0 + kc)], writes=[("ps", bs[c])], inc=(kc == KC - 1))
                    for c in range(4):
                        P.op("act", lambda e, b=bs[c], c=c: e.activation(bigc(Q0 + c), psf[:, b, :], AF.Copy, scale=0.125),
                             reads=[("ps", bs[c])], writes=[("big", Q0 + c)])
                    continue
                for c in range(4):
                    b = psget()
                    mm_group(b, slot, c * 128, hrhs, hres)
                    P.op("act", lambda e, b=b, qi=qi, c=c: e.activation(bigc(Q0 + 4 * qi + c), psf[:, b, :], AF.Copy, scale=0.125),
                         reads=[("ps", b)], writes=[("big", Q0 + 4 * qi + c)])
            slot = wget("win", l, 0, MKV, 512)
            for cc_ in range(2):
                b = psget()
                mm_group(b, slot, cc_ * 128, hrhs, hres)
                P.op("dve", lambda e, b=b, cc_=cc_, l=l: e.tensor_copy(kT[:, l, cc_, 128:640], psf[:, b, :]),
                     reads=[("ps", b)], writes=[("kT", l, 1 + s_) for s_ in range(4)])
            for blk in range(NBLK):
                b = psget()
                for kc in range(KC):
                    P.op("pe", lambda e, b=b, slot=slot, kc=kc, blk=blk: e.matmul(
                        psf[:, b, 0:256], bigc(H0 + kc)[:, blk * 128:(blk + 1) * 128], wbuf[:, slot, kc, 256:512],
                        start=(kc == 0), stop=(kc == KC - 1)),
                        reads=[("w", slot), ("big", H0 + kc)], writes=[("ps", b)], inc=(kc == KC - 1))
                P.op("dve", lambda e, b=b, blk=blk, l=l: e.tensor_copy(
                    Vt[:, l, 1 + blk, :].rearrange("p (g d) -> p g d", g=4)[:, :, 0:64],
                    psf[:, b, 0:256].rearrange("p (g d) -> p g d", g=4)),
                    reads=[("ps", b)], writes=[("V", l, 1 + blk)])
            run_interleaved(gen_attn(l, first, last_of_seq), chain(gen_conv(l, first, last_of_seq), gen_yconv(l)),
                            NBLK * 4 + ADEPTH + 1, 17, holdback=2)
            for jg in range(2):
                sg = wget("win", l, 0, MGA + 512 * jg, 512)
                sw = wget("wao", l, 0, 512 * jg, 512)
                for jj in range(4):
                    j = 4 * jg + jj
                    bg = psget()
                    mm_group(bg, sg, jj * 128, hrhs, hres)
                    gs = ring("gate", 2)
                    P.op("act", lambda e, bg=bg, gs=gs, l=l, j=j: e.activation(gate[:, gs, :], psf[:, bg, :], AF.Tanh,
                                                                               bias=hbias[:, l, j:j + 1], scale=0.5),
                         reads=[("ps", bg)], writes=[("gate", gs)])
                    by = psget()
                    mm_group(by, sw, jj * 128, lambda kc: bigc(AT0 + kc), lambda kc: ("big", AT0 + kc))
                    ts = ring("tmpb", 2)
                    P.op("dve", lambda e, by=by, gs=gs, ts=ts: e.scalar_tensor_tensor(tmpb[:, ts, :], gate[:, gs, :], 1.0, psf[:, by, :], ALU.add, ALU.mult),
                         reads=[("ps", by), ("gate", gs)], writes=[("tmpb", ts)])
                    P.op("dve", lambda e, ts=ts, j=j: e.tensor_tensor(bigc(CV0 + j), tmpb[:, ts, :], mbuf[:, j, :], ALU.add),
                         reads=[("tmpb", ts), ("m", j)], writes=[("big", CV0 + j)])
            for og in range(2):
                slot = wget("wo", l, 0, 512 * og, 512)
                for jj in range(4):
                    j = 4 * og + jj
                    b = psget()
                    mm_group(b, slot, jj * 128, lambda kc: bigc(CV0 + kc), lambda kc: ("big", CV0 + kc))
                    P.op("dve", lambda e, b=b, j=j: e.scalar_tensor_tensor(xT[:, j, :], psf[:, b, :], 0.5, xT[:, j, :], ALU.mult, ALU.add),
                         reads=[("ps", b), ("xT", j)], writes=[("xT", j)])
                    P.op("act", lambda e, j=j, l=l: e.activation(a2[:, j, :], xT[:, j, :], AF.Copy, scale=col("g_mlp", l, j)),
                         reads=[("xT", j)], writes=[("a2", j)])
            for fg in range(8):
                slot = wget("wup", l, 0, 512 * fg, 512)
                for ff in range(4):
                    f = 4 * fg + ff
                    b = psget()
                    mm_group(b, slot, ff * 128, lambda kc: a2[:, kc, :], lambda kc: ("a2", kc))
                    rs = ring("relu", 2)
                    P.op("act", lambda e, b=b, rs=rs: e.activation(relu[:, rs, :], psf[:, b, :], AF.Relu),
                         reads=[("ps", b)], writes=[("relu", rs)])
                    P.op("dve", lambda e, rs=rs, f=f: e.tensor_tensor(bigc(f), relu[:, rs, :], relu[:, rs, :], ALU.mult),
                         reads=[("relu", rs)], writes=[("big", f)])
                if fg == 0:
                    b = psget()
                    for kc in range(KC):
                        s = emit_square(kc)
                        P.op("pe", lambda e, kc=kc, s=s, b=b: e.matmul(psf[:, b, :], ones[:], xsq[:, s, :], start=(kc == 0), stop=(kc == KC - 1)),
                             reads=[("xsq", s)], writes=[("ps", b)], inc=True)
                    P.op("act", lambda e, b=b: e.activation(mse[:], psf[:, b, :], AF.Ln, bias=epsc[:, 0:1], scale=1.0 / D),
                         reads=[("ps", b)], writes=[("mse",)])
                    P.op("act", lambda e: e.activation(rstd[:], mse[:], AF.Exp, scale=-1.0),
                         reads=[("mse",)], writes=[("rstd",)])
            for og in range(2):
                bs = [psget() for _ in range(4)]
                for kq in range(4):
                    slot = wget("wdn", l, 1024 * kq, 512 * og, 512)
                    for jj in range(4):
                        for kc in range(KC):
                            f = kq * 8 + kc
                            P.op("pe", lambda e, b=bs[jj], slot=slot, jj=jj, kc=kc, f=f, kq=kq: e.matmul(
                                psf[:, b, :], wbuf[:, slot, kc, jj * 128:(jj + 1) * 128], bigc(f),
                                start=(kq == 0 and kc == 0), stop=(kq == 3 and kc == KC - 1)),
                                reads=[("w", slot), ("big", f)], writes=[("ps", bs[jj])], inc=(kc == KC - 1))
                for jj in range(4):
                    j = 4 * og + jj
                    ts = ring("tmpb", 2)
                    P.op("dve", lambda e, b=bs[jj], ts=ts: e.tensor_tensor(tmpb[:, ts, :], psf[:, b, :], rstd[:], ALU.mult),
                         reads=[("ps", bs[jj]), ("rstd",)], writes=[("tmpb", ts)])
                    P.op("dve", lambda e, ts=ts, j=j: e.tensor_tensor(xT[:, j, :], tmpb[:, ts, :], xT[:, j, :], ALU.add),
                         reads=[("tmpb", ts), ("xT", j)], writes=[("xT", j)])

        for l in range(L):
            emit_layer(l, first, last_of_seq)
        dsrc = None
        if t + 1 < ntiles:
            for blk in range(4):
                xslots[(t + 1, blk)] = emit_xload(t + 1, blk)
        b = psget()
        for kc in range(KC):
            s = emit_square(kc)
            P.op("pe", lambda e, kc=kc, s=s, b=b: e.matmul(psf[:, b, :], ones[:], xsq[:, s, :], start=(kc == 0), stop=(kc == KC - 1)),
                 reads=[("xsq", s)], writes=[("ps", b)], inc=True)
        P.op("act", lambda e, b=b: e.activation(mse[:], psf[:, b, :], AF.Ln, bias=epsc[:, 0:1], scale=1.0 / D),
             reads=[("ps", b)], writes=[("mse",)])
        P.op("act", lambda e: e.activation(rstd[:], mse[:], AF.Exp, scale=-0.5),
             reads=[("mse",)], writes=[("rstd",)])
        for kc in range(KC):
            P.op("dve", lambda e, kc=kc: e.scalar_tensor_tensor(mbuf[:, kc, :], xT[:, kc, :], gfin(kc), rstd[:], ALU.mult, ALU.mult),
                 reads=[("xT", kc), ("rstd",)], writes=[("m", kc)])
        for blk in range(NBLK):
            pending_out.append((t, blk))

    emit_out_blocks()
    emit_weight_dmas()
    P.final_wait("sp", [(k, v) for k, v in P.cnt.items() if isinstance(k, tuple) and k[0] == "xo"])

    sems = {}
    for k in P.sem_keys():
        nm = "s_" + ("_".join(str(z) for z in k) if isinstance(k, tuple) else str(k))
        sems[k] = nc.alloc_semaphore(nm)

    def runner(name):
        def body(e):
            for waits, fn, inc in P.q[name]:
                for s, v in waits:
                    e.wait_ge(sems[s], v)
                if fn is None:
                    continue
                ins = fn(e)
                if inc is not None:
                    ins.then_inc(sems[inc[0]], inc[1])
        return body

    with nc.Block() as block:
        block.tensor(runner("pe"))
        block.scalar(runner("act"))
        block.vector(runner("dve"))
        block.gpsimd(runner("pool"))
        block.sync(runner("sp"))
    return nc, P


def host_consts(depth):
    slopes = np.power(np.float32(2.0), -8.0 * np.arange(1, NQH + 1, dtype=np.float32) / NQH).astype(np.float32)
    j = np.arange(128)[:, None]
    r = np.arange(128)[None, :]
    et = np.zeros((128, 2, NQH, 128), dtype=np.float32)
    for h in range(NQH):
        dprev = (128 + r - j).astype(np.float32)
        dcur = (r - j).astype(np.float32)
        et[:, 0, h, :] = np.where(j > r, np.exp(-slopes[h] * np.maximum(dprev, 0.0)), 0.0)
        et[:, 1, h, :] = np.where(j <= r, np.exp(-slopes[h] * np.maximum(dcur, 0.0)), 0.0)
    identf = np.eye(128, dtype=np.float32)
    identb = np.eye(128, dtype=np.float32).astype(ml_dtypes.bfloat16)
    return et.reshape(128, -1), identf, identb


def prep_weights(inputs, depth):
    L = depth
    f = lambda a: np.ascontiguousarray(np.asarray(a, dtype=np.float32))
    perm = _win_perm()
    w_in = f(np.asarray(inputs["w_in"])[:L][:, :, perm])

    def cols(v):
        return np.asarray(v, dtype=np.float32).reshape(8, 128).T

    pv = np.zeros((128, L * 64 + 8), dtype=np.float32)
    for l in range(L):
        o = l * 64
        pv[:, o + 0:o + 8] = cols(inputs["g_mix"][l])
        pv[:, o + 8:o + 16] = cols(inputs["g_mlp"][l])
        pv[:, o + 16:o + 24] = cols(np.asarray(inputs["b_gates"])[l, :D])
        pv[:, o + 24:o + 32] = cols(np.asarray(inputs["b_gates"])[l, D:])
        pv[:, o + 32:o + 40] = cols(np.asarray(inputs["conv_w"])[l, 0])
        pv[:, o + 40:o + 48] = cols(np.asarray(inputs["conv_w"])[l, 1])
        pv[:, o + 48:o + 56] = cols(np.asarray(inputs["conv_w"])[l, 2])
        pv[:, o + 56:o + 64] = cols(inputs["conv_b"][l])
    pv[:, L * 64:L * 64 + 8] = cols(inputs["g_final"])
    sinkb = np.ascontiguousarray(np.broadcast_to(np.asarray(inputs["sinks"], dtype=np.float32)[:L].reshape(1, L * 16), (128, L * 16)))
    et, identf, identb = host_consts(L)
    shared = {
        "w_in": w_in,
        "w_ao": f(np.asarray(inputs["w_attn_out"])[:L]),
        "w_co": f(np.asarray(inputs["w_conv_out"])[:L]),
        "w_o": f(np.asarray(inputs["w_o"])[:L]),
        "w_up": f(np.asarray(inputs["w_up"])[:L]),
        "w_dn": f(np.asarray(inputs["w_down"])[:L]),
        "pvec": pv, "sinkb": sinkb, "etab": et, "identf": identf, "identb": identb,
    }
    return shared


_CACHE = {}


def run(inputs, depth=L_FULL, seq=SEQ, dbg=None):
    x = np.asarray(inputs["x"], dtype=np.float32)
    B, S, _ = x.shape
    assert S == seq and B % N_CORES == 0
    bpc = B // N_CORES
    tiles_per_seq = S // T
    ntiles = bpc * tiles_per_seq
    key = (ntiles, depth, tiles_per_seq, dbg)
    if key not in _CACHE:
        _CACHE[key] = build_program(ntiles, depth, tiles_per_seq, dbg)[0]
    nc = _CACHE[key]
    shared = prep_weights(inputs, depth)
    in_maps = []
    for c in range(N_CORES):
        m = dict(shared)
        m["x"] = np.ascontiguousarray(x[c * bpc:(c + 1) * bpc].reshape(bpc * S, D))
        in_maps.append(m)
    res = run_bass_kernel_spmd(nc, in_maps, core_ids=list(range(N_CORES)))
    outs = [np.asarray(r["out"], dtype=np.float32).reshape(bpc, S, D) for r in res.results]
    return np.concatenate(outs, axis=0)


def kernel(**inputs):
    return run(inputs)
```
